# Optimizing a Trainium2 kernel written in Bass

```python
import jax, jax.numpy as jnp
from jax import lax
import numpy as np

D_MODEL = 1024
BATCH = 8
SEQ = 2048
DEPTH = 4
DEC_BATCH = 8
DEC_SEQ = 64
PAST_LEN = 1024

CHUNK = 64
D_MIX = D_MODEL
D_POOL = D_MIX // 2
D_CONV = D_MIX - D_POOL
POOL_WINDOWS = (2, 4, 8, 16)
N_POOL_GROUPS = len(POOL_WINDOWS)
POOL_GROUP = D_POOL // N_POOL_GROUPS
POOL_STATE = max(POOL_WINDOWS) - 1
N_CONV_HEADS = 8
CONV_HEAD = D_CONV // N_CONV_HEADS
CONV_WIDTH = 3
D_FF = ((8 * D_MODEL // 3 + 127) // 128) * 128
D_IN = D_POOL + 3 * D_CONV
EPS = 1e-6

kernel_name = "hybrid_pool_shortconv_streaming_encoder_step"


def rmsnorm(x, g):
    xf = x.astype(jnp.float32)
    y = xf * lax.rsqrt(jnp.mean(xf * xf, axis=-1, keepdims=True) + EPS)
    return (y * g.astype(jnp.float32)).astype(x.dtype)


def head_rmsnorm(x, n_heads, g):
    b, t, c = x.shape
    xf = x.astype(jnp.float32).reshape(b, t, n_heads, c // n_heads)
    y = xf * lax.rsqrt(jnp.mean(xf * xf, axis=-1, keepdims=True) + EPS)
    return (y.reshape(b, t, c) * g.astype(jnp.float32)).astype(x.dtype)


def causal_dwconv(x, buf, w):
    t = x.shape[1]
    xp = jnp.concatenate([buf.astype(x.dtype), x], axis=1)
    y = sum(w[k] * xp[:, k:k + t] for k in range(CONV_WIDTH))
    return y, xp[:, -(CONV_WIDTH - 1):]


def multiscale_pool(v, buf, offset):
    b, t, c = v.shape
    L = POOL_STATE
    xp = jnp.concatenate([buf.astype(v.dtype), v], axis=1)
    cs = jnp.cumsum(xp.astype(jnp.float32), axis=1)
    cs0 = jnp.concatenate([jnp.zeros((b, 1, c), jnp.float32), cs], axis=1)
    pos = offset + jnp.arange(t)
    means = []
    for gi, w in enumerate(POOL_WINDOWS):
        sl = slice(gi * POOL_GROUP, (gi + 1) * POOL_GROUP)
        s = cs0[:, L + 1:L + 1 + t, sl] - cs0[:, L + 1 - w:L + 1 - w + t, sl]
        cnt = jnp.minimum(pos + 1, w).astype(jnp.float32)[None, :, None]
        means.append(s / cnt)
    mean = jnp.concatenate(means, axis=-1)
    out = (mean - v.astype(jnp.float32)).astype(v.dtype)
    return out, xp[:, -L:]


def layer(x, pool_buf, conv_buf, ffn_buf, offset, w_in, pool_mix, pool_scale, conv_w,
          g_pool_out, g_conv_out, w_out, g_pre_mix, g_post_mix, g_pre_ffn, g_post_ffn,
          w_up, ffn_conv_w, w_down):
    b, t, _ = x.shape
    h = rmsnorm(x, g_pre_mix)
    z = h @ w_in
    v = z[..., :D_POOL]
    gb = z[..., D_POOL:D_POOL + D_CONV]
    gc = z[..., D_POOL + D_CONV:D_POOL + 2 * D_CONV]
    u = z[..., D_POOL + 2 * D_CONV:]
    pooled, new_pool = multiscale_pool(v, pool_buf, offset)
    ya = jnp.einsum('btgc,gcd->btgd', pooled.reshape(b, t, N_POOL_GROUPS, POOL_GROUP),
                    pool_mix).reshape(b, t, D_POOL) * pool_scale
    conv_out, new_conv = causal_dwconv(gc * u, conv_buf, conv_w)
    yb = gb * conv_out
    mix = jnp.concatenate([head_rmsnorm(ya, N_POOL_GROUPS, g_pool_out),
                           head_rmsnorm(yb, N_CONV_HEADS, g_conv_out)], axis=-1) @ w_out
    x = x + rmsnorm(mix, g_post_mix)
    h = rmsnorm(x, g_pre_ffn)
    up, new_ffn = causal_dwconv(h @ w_up, ffn_buf, ffn_conv_w)
    f = (jax.nn.silu(up[..., :D_FF]) * up[..., D_FF:]) @ w_down
    x = x + rmsnorm(f, g_post_ffn)
    return x, new_pool, new_conv, new_ffn


def setup_inputs(seed: int = 0) -> dict:
    key = jax.random.key(seed)
    ks = jax.random.split(key, 24)
    f32 = jnp.float32

    def nrm(k, shape, scale):
        return jax.random.normal(k, shape, f32) * scale

    def gain(k, shape):
        return 1.0 + 0.02 * jax.random.normal(k, shape, f32)

    return {
        "x_prompt": nrm(ks[0], (BATCH, SEQ, D_MODEL), 1.0),
        "x_sample": nrm(ks[1], (DEC_BATCH, DEC_SEQ, D_MODEL), 1.0),
        "state_pool": nrm(ks[2], (DEPTH, DEC_BATCH, POOL_STATE, D_POOL), 1.0),
        "state_conv": nrm(ks[3], (DEPTH, DEC_BATCH, CONV_WIDTH - 1, D_CONV), 1.0),
        "state_ffn_conv": nrm(ks[4], (DEPTH, DEC_BATCH, CONV_WIDTH - 1, 2 * D_FF), 1.0),
        "w_in": nrm(ks[5], (DEPTH, D_MODEL, D_IN), D_MODEL ** -0.5),
        "pool_mix": nrm(ks[6], (DEPTH, N_POOL_GROUPS, POOL_GROUP, POOL_GROUP), POOL_GROUP ** -0.5),
        "pool_scale": gain(ks[7], (DEPTH, D_POOL)),
        "conv_w": nrm(ks[8], (DEPTH, CONV_WIDTH, D_CONV), CONV_WIDTH ** -0.5),
        "g_pool_out": gain(ks[9], (DEPTH, D_POOL)),
        "g_conv_out": gain(ks[10], (DEPTH, D_CONV)),
        "w_out": nrm(ks[11], (DEPTH, D_MIX, D_MODEL), D_MIX ** -0.5),
        "g_pre_mix": gain(ks[12], (DEPTH, D_MODEL)),
        "g_post_mix": gain(ks[13], (DEPTH, D_MODEL)),
        "g_pre_ffn": gain(ks[14], (DEPTH, D_MODEL)),
        "g_post_ffn": gain(ks[15], (DEPTH, D_MODEL)),
        "w_up": nrm(ks[16], (DEPTH, D_MODEL, 2 * D_FF), D_MODEL ** -0.5),
        "ffn_conv_w": nrm(ks[17], (DEPTH, CONV_WIDTH, 2 * D_FF), CONV_WIDTH ** -0.5),
        "w_down": nrm(ks[18], (DEPTH, D_FF, D_MODEL), D_FF ** -0.5),
        "g_final": gain(ks[19], (D_MODEL,)),
    }


def reference(x_prompt, x_sample, state_pool, state_conv, state_ffn_conv, w_in, pool_mix,
              pool_scale, conv_w, g_pool_out, g_conv_out, w_out, g_pre_mix, g_post_mix,
              g_pre_ffn, g_post_ffn, w_up, ffn_conv_w, w_down, g_final):
    bp = x_prompt.shape[0]
    dt = x_prompt.dtype
    xp, xs = x_prompt, x_sample
    pool_p, conv_p, ffn_p, pool_s, conv_s, ffn_s = [], [], [], [], [], []
    for l in range(DEPTH):
        weights = (w_in[l], pool_mix[l], pool_scale[l], conv_w[l], g_pool_out[l], g_conv_out[l],
                   w_out[l], g_pre_mix[l], g_post_mix[l], g_pre_ffn[l], g_post_ffn[l],
                   w_up[l], ffn_conv_w[l], w_down[l])
        xp, a, c, f = layer(xp,
                            jnp.zeros((bp, POOL_STATE, D_POOL), dt),
                            jnp.zeros((bp, CONV_WIDTH - 1, D_CONV), dt),
                            jnp.zeros((bp, CONV_WIDTH - 1, 2 * D_FF), dt),
                            0, *weights)
        pool_p.append(a); conv_p.append(c); ffn_p.append(f)
        xs, a, c, f = layer(xs, state_pool[l], state_conv[l], state_ffn_conv[l],
                            PAST_LEN, *weights)
        pool_s.append(a); conv_s.append(c); ffn_s.append(f)
    y_prompt = rmsnorm(xp, g_final)
    y_sample = rmsnorm(xs, g_final)
    return (y_prompt, y_sample,
            jnp.stack(pool_p), jnp.stack(conv_p), jnp.stack(ffn_p),
            jnp.stack(pool_s), jnp.stack(conv_s), jnp.stack(ffn_s))
```

```python
import numpy as np
from contextlib import ExitStack
import concourse.bass as bass
import concourse.mybir as mybir
from concourse.bass_utils import run_bass_kernel_spmd

F32 = mybir.dt.float32
BF16 = mybir.dt.bfloat16
ALU = mybir.AluOpType
AF = mybir.ActivationFunctionType

PE, ACT, DVE, POOL, SP = "pe", "act", "dve", "pool", "sp"
COMPUTE = (PE, ACT, DVE, POOL)

NCORES = 8
L = 4
D = 1024
KT = 8
DFF = 2816
NPAIR = 22
SEQ = 2048
DSEQ = 64
EPS = 1e-6
H = 16
NSLOT = 5
SLOT_E = 3072
PREFETCH = 2
NV_L = 188
NV = L * NV_L + 8
V_GPRE, V_GPOST, V_GPREF, V_GPOSTF, V_PSCALE, V_GPOOL, V_GCONV, V_CW, V_FW = 0, 8, 16, 24, 32, 36, 40, 44, 56


class Buf:
    def __init__(self, name, t, width):
        self.name = name
        self.t = t
        self.width = width
        self.recs = []


class Op:
    __slots__ = ("eng", "fn", "deps", "signal", "count", "is_dma", "dma_sem", "dma_val", "dma_prev")

    def __init__(self, eng, fn, is_dma):
        self.eng = eng
        self.fn = fn
        self.deps = []
        self.signal = False
        self.count = 0
        self.is_dma = is_dma
        self.dma_sem = None
        self.dma_val = 0
        self.dma_prev = None


class Prog:
    def __init__(self, dma_ring=8):
        self.streams = {e: [] for e in (PE, ACT, DVE, POOL, SP)}
        self.dma_ring = dma_ring
        self.dma_count = {e: 0 for e in (ACT, POOL, SP)}
        self.dma_hist = {e: [] for e in (ACT, POOL, SP)}

    def op(self, eng, fn, reads=(), writes=(), dma=False):
        o = Op(eng, fn, dma)
        deps = {}
        for (b, lo, hi) in reads:
            assert 0 <= lo < hi <= b.width, (b.name, lo, hi, b.width)
            for (l2, h2, k2, o2) in b.recs:
                if k2 == "w" and l2 < hi and lo < h2:
                    deps[id(o2)] = (o2, True)
        for (b, lo, hi) in writes:
            assert 0 <= lo < hi <= b.width, (b.name, lo, hi, b.width)
            for (l2, h2, k2, o2) in b.recs:
                if l2 < hi and lo < h2 and id(o2) not in deps:
                    deps[id(o2)] = (o2, False)
        for (b, lo, hi) in writes:
            b.recs = [r for r in b.recs if not (lo <= r[0] and r[1] <= hi)]
            b.recs.append((lo, hi, "w", o))
        for (b, lo, hi) in reads:
            if not dma:
                b.recs = [r for r in b.recs
                          if not (r[2] == "r" and r[3].eng == eng and not r[3].is_dma
                                  and lo <= r[0] and r[1] <= hi)]
            b.recs.append((lo, hi, "r", o))
        for (o2, israw) in deps.values():
            if o2 is o:
                continue
            if (not o2.is_dma) and (not dma) and o2.eng == eng:
                if eng == PE:
                    continue
            o.deps.append(o2)
            if not o2.is_dma:
                o2.signal = True
        if dma:
            k = self.dma_count[eng]
            self.dma_count[eng] += 1
            o.dma_val = 16 * (k // self.dma_ring + 1)
            o.dma_sem = k % self.dma_ring
            hist = self.dma_hist[eng]
            if k >= self.dma_ring:
                o.dma_prev = hist[k - self.dma_ring]
            hist.append(o)
        self.streams[eng].append(o)
        return o

    def emit(self, sems):
        tl = sems["tl"]
        dsem = sems["dma"]
        for e in COMPUTE:
            c = 0
            for o in self.streams[e]:
                if not o.is_dma and o.signal:
                    c += 1
                    o.count = c

        def run_stream(e, h):
            waited = {}
            for o in self.streams[e]:
                waits = {}
                deps = list(o.deps)
                if o.is_dma and o.dma_prev is not None:
                    deps.append(o.dma_prev)
                for d in deps:
                    if d.is_dma:
                        key = ("dma", d.eng, d.dma_sem)
                        val = d.dma_val
                        sem = dsem[d.eng][d.dma_sem]
                    else:
                        key = ("tl", d.eng)
                        val = d.count
                        sem = tl[d.eng]
                        assert val > 0
                    if waited.get(key, 0) >= val:
                        continue
                    if key not in waits or waits[key][1] < val:
                        waits[key] = (sem, val)
                for key, (sem, val) in waits.items():
                    h.wait_ge(sem, val)
                    waited[key] = val
                ins = o.fn(h)
                if o.is_dma:
                    ins.then_inc(dsem[o.eng][o.dma_sem], 16)
                elif o.signal:
                    ins.then_inc(tl[o.eng], 1)
            if e in self.dma_hist:
                last = {}
                for d in self.dma_hist[e]:
                    last[d.dma_sem] = d
                for d in last.values():
                    h.wait_ge(dsem[e][d.dma_sem], d.dma_val)

        return run_stream


class V:
    def __init__(self, b, off32, n, dt):
        self.b = b
        self.off = off32
        self.n = n
        self.dt = dt
        if dt == F32:
            self.base = b.t[:, off32:off32 + n]
        else:
            assert n % 2 == 0
            self.base = b.t[:, off32:off32 + n // 2].bitcast(BF16)

    def ap(self, lo, hi):
        assert 0 <= lo < hi <= self.n, (self.b.name, lo, hi, self.n)
        return self.base[:, lo:hi]

    def r(self, lo, hi):
        assert 0 <= lo < hi <= self.n, (self.b.name, lo, hi, self.n)
        if self.dt == F32:
            return (self.b, self.off + lo, self.off + hi)
        return (self.b, self.off + lo // 2, self.off + (hi + 1) // 2)

    def sub(self, lo, n):
        if self.dt == F32:
            return V(self.b, self.off + lo, n, F32)
        assert lo % 2 == 0
        return V(self.b, self.off + lo // 2, n, BF16)


def _fm(vec):
    return np.ascontiguousarray(vec.reshape(-1, 128).T)


def _unit(w, cols):
    k = w.shape[0] // 128
    u = w[:, cols].reshape(k, 128, len(cols)).transpose(1, 0, 2)
    return u.reshape(128, k * len(cols))


def _zin_cols():
    units = [np.arange(0, 256), np.arange(256, 512)]
    for j in range(4):
        units.append(np.concatenate([np.arange(1024 + 128 * j, 1024 + 128 * j + 128),
                                     np.arange(1536 + 128 * j, 1536 + 128 * j + 128),
                                     np.arange(512 + 128 * j, 512 + 128 * j + 128)]))
    return units


def unit_table():
    tab = []
    zc = _zin_cols()
    for i in (0, 1):
        tab.append(("z", i, 8 * len(zc[i])))
    tab.append(("pm", 0, 512))
    for i in (2, 3, 4, 5):
        tab.append(("z", i, 8 * len(zc[i])))
    for i, nc_ in enumerate((384, 384, 256)):
        tab.append(("o", i, 8 * nc_))
    for i in range(NPAIR):
        tab.append(("u", i, 8 * 256))
    for j in range(8):
        tab.append(("d", j, NPAIR * 128))
    offs = []
    o = 0
    for t in tab:
        offs.append(o)
        o += t[2]
    return tab, offs, o


def pack_weights(w_in, pool_mix, w_out, w_up, w_down):
    tab, offs, tot = unit_table()
    out = np.empty((L, 128, tot), np.float32)
    zc = _zin_cols()
    oc = [np.arange(0, 384), np.arange(384, 768), np.arange(768, 1024)]
    for l in range(L):
        for (kind, i, n), o in zip(tab, offs):
            if kind == "z":
                a = _unit(w_in[l], zc[i])
            elif kind == "pm":
                a = pool_mix[l].transpose(1, 0, 2).reshape(128, 512)
            elif kind == "o":
                a = _unit(w_out[l], oc[i])
            elif kind == "u":
                a = _unit(w_up[l], np.concatenate([np.arange(128 * i, 128 * i + 128),
                                                   np.arange(DFF + 128 * i, DFF + 128 * i + 128)]))
            else:
                a = _unit(w_down[l], np.arange(128 * i, 128 * i + 128))
            out[l, :, o:o + n] = a
    return out


def pack_vecs(pool_scale, conv_w, g_pool_out, g_conv_out, g_pre_mix, g_post_mix, g_pre_ffn, g_post_ffn,
              ffn_conv_w, g_final):
    v = np.zeros((128, NV), np.float32)
    for l in range(L):
        b = l * NV_L
        v[:, b + V_GPRE:b + V_GPRE + 8] = _fm(g_pre_mix[l])
        v[:, b + V_GPOST:b + V_GPOST + 8] = _fm(g_post_mix[l])
        v[:, b + V_GPREF:b + V_GPREF + 8] = _fm(g_pre_ffn[l])
        v[:, b + V_GPOSTF:b + V_GPOSTF + 8] = _fm(g_post_ffn[l])
        v[:, b + V_PSCALE:b + V_PSCALE + 4] = _fm(pool_scale[l])
        v[:, b + V_GPOOL:b + V_GPOOL + 4] = _fm(g_pool_out[l])
        v[:, b + V_GCONV:b + V_GCONV + 4] = _fm(g_conv_out[l])
        for k in range(3):
            v[:, b + V_CW + 4 * k:b + V_CW + 4 * k + 4] = _fm(conv_w[l, k])
            v[:, b + V_FW + 44 * k:b + V_FW + 44 * k + 44] = _fm(ffn_conv_w[l, k])
    v[:, L * NV_L:L * NV_L + 8] = _fm(g_final)
    return v


def build_nc():
    nc = bass.Bass("TRN2", target_bir_lowering=False)
    tab, offs, WTOT = unit_table()
    NU = len(tab)
    xp_d = nc.dram_tensor("xp", [KT, 128, SEQ], F32, kind="ExternalInput").ap()
    xs_d = nc.dram_tensor("xs", [KT, 128, DSEQ], F32, kind="ExternalInput").ap()
    stp_d = nc.dram_tensor("stp", [128, L * 4 * 15], F32, kind="ExternalInput").ap()
    stc_d = nc.dram_tensor("stc", [128, L * 4 * 2], F32, kind="ExternalInput").ap()
    stf_d = nc.dram_tensor("stf", [128, L * 44 * 2], F32, kind="ExternalInput").ap()
    vec_d = nc.dram_tensor("vecs", [128, NV], F32, kind="ExternalInput").ap()
    wts_d = nc.dram_tensor("wts", [L, 128, WTOT], F32, kind="ExternalInput").ap()
    yp_d = nc.dram_tensor("yp", [KT, 128, SEQ], F32, kind="ExternalOutput").ap()
    ys_d = nc.dram_tensor("ys", [KT, 128, DSEQ], F32, kind="ExternalOutput").ap()
    opool_d = nc.dram_tensor("opool", [128, L * 2 * 4 * 15], F32, kind="ExternalOutput").ap()
    oconv_d = nc.dram_tensor("oconv", [128, L * 2 * 4 * 2], F32, kind="ExternalOutput").ap()
    offn_d = nc.dram_tensor("offn", [128, L * 2 * 44 * 2], F32, kind="ExternalOutput").ap()

    WMAX = 1088
    WH = WMAX + 2 * H
    es = ExitStack()
    with es:
        def sbuf(name, n32):
            t = es.enter_context(nc.sbuf_tensor("sb_" + name, [128, n32], F32))
            return Buf(name, t, n32)

        b_x = sbuf("x", KT * WMAX)
        b_h = sbuf("h", KT * WMAX // 2)
        b_sq = sbuf("sq", KT * 512 // 2)
        b_rt = sbuf("rt", 3 * 512)
        b_w = sbuf("w", NSLOT * SLOT_E // 2)
        b_vec = sbuf("vec", NV)
        b_stp = sbuf("stp", L * 4 * 15)
        b_stc = sbuf("stc", L * 4 * 2)
        b_stf = sbuf("stf", L * 44 * 2)
        b_cv = sbuf("cv", L * 4 * 15)
        b_cc = sbuf("cc", L * 4 * 2)
        b_cf = sbuf("cf", L * 44 * 2)
        b_op = sbuf("op", L * 2 * 4 * 15)
        b_oc = sbuf("oc", L * 2 * 4 * 2)
        b_of = sbuf("of", L * 2 * 44 * 2)
        b_cst = sbuf("cst", 3 * 64 + 1 + 64 + 3)
        b_pm = sbuf("pm", 256)
        PH_N = 25200
        PH_N = 25008
        b_ph = sbuf("ph", PH_N)
        pst = es.enter_context(nc.psum_tensor("psum_all", [128, 8 * 512], F32))
        b_ps = Buf("ps", pst, 8 * 512)

        sems = {"tl": {e: es.enter_context(nc.semaphore("tl_" + e)) for e in COMPUTE},
                "dma": {e: [es.enter_context(nc.semaphore("d_%s_%d" % (e, i))) for i in range(8)]
                        for e in (ACT, POOL, SP)}}
        P = Prog()

        xv = [V(b_x, kt * WMAX, WMAX, F32) for kt in range(KT)]
        hv = [V(b_h, kt * WMAX // 2, WMAX, BF16) for kt in range(KT)]
        sqv = [V(b_sq, kt * 256, 512, BF16) for kt in range(KT)]
        rtv = [V(b_rt, i * 512, 512, F32) for i in range(3)]
        wslot = [V(b_w, s * SLOT_E // 2, SLOT_E, BF16) for s in range(NSLOT)]
        vec = V(b_vec, 0, NV, F32)
        pmbuf = V(b_pm, 0, 512, BF16)
        ones1024 = V(b_cst, 0, 128, BF16)
        ones128 = V(b_cst, 64, 128, BF16)
        onesblk = V(b_cst, 128, 128, BF16)
        epsv = V(b_cst, 192, 1, F32)
        invc = [V(b_cst, 193 + 16 * g, 16, F32) for g in range(4)]
        fixt = V(b_cst, 193 + 64, 3, F32)
        banks = [V(b_ps, k * 512, 512, F32) for k in range(8)]
        bankpair = [V(b_ps, k * 1024, 1024, F32) for k in range(4)]
        bp_ctr = [0, 0]
        o = 0
        vbuf = [V(b_ph, o + g * WH, WH, F32) for g in range(4)]; o += 4 * WH
        pooled = [V(b_ph, o + g * (WMAX // 2), WMAX, BF16) for g in range(4)]; o += 4 * WMAX // 2
        gcs = [V(b_ph, o + i * 512, 512, F32) for i in range(1)]; o += 512
        gbs = [V(b_ph, o + i * 512, 512, F32) for i in range(2)]; o += 1024
        mixraw = [[V(b_ph, par * 4480 + j * 512, 512, F32) for j in range(8)] for par in range(2)]
        cacc = [V(b_ph, o + i * 512, 512, F32) for i in range(1)]; o += 512
        assert o >= 8576
        cubuf = [V(b_ph, o + j * WH, WH, F32) for j in range(4)]
        sqO = [V(b_ph, o + j * 256, 512, BF16) for j in range(8)]
        o += 4 * WH
        Tk = [V(b_ph, o + k * WH, WH, F32) for k in range(3)]; o += 3 * WH
        Tfin = [V(b_ph, o + i * 512, 512, F32) for i in range(2)]; o += 1024
        yrot = [V(b_ph, o + i * 512, 512, F32) for i in range(4)]; o += 2048
        sqh = [V(b_ph, o + i * 256, 512, BF16) for i in range(4)]; o += 1024
        cat = [V(b_ph, o + kt * (WMAX // 2), WMAX, BF16) for kt in range(KT)]; o += KT * WMAX // 2
        assert o <= PH_N, o
        o = 0
        upraw = [V(b_ph, o + s * WH, WH, F32) for s in range(6)]; o += 6 * WH
        facc = [V(b_ph, o + i * 1024, 1024, F32) for i in range(4)]; o += 4096
        facc_s = [V(b_ph, o + i * 64, 64, F32) for i in range(4)]; o += 256
        fraw = [V(b_ph, j * WH + H, WMAX, F32) for j in range(8)]
        o = max(o, 8 * WMAX)
        actv = [V(b_ph, o + i * (WMAX // 2), WMAX, BF16) for i in range(NPAIR)]; o += NPAIR * WMAX // 2
        fsq = [V(b_ph, o + i * 256, 512, BF16) for i in range(3)]; o += 768
        fstage = [V(b_ph, 8960 + i * 512, 512, F32) for i in range(3)]
        assert 8960 + 3 * 512 <= 6 * WH + 4096 and 8960 >= 7 * WH + H + WMAX
        fstage += [V(b_ph, o + i * 512, 512, F32) for i in range(2)]; o += 1024
        assert o <= PH_N, o

        def vcol(l, off):
            c = l * NV_L + off
            return vec.ap(c, c + 1), vec.r(c, c + 1)

        def mm(out, lhsT, rhs, start, stop):
            (ov, olo, ohi), (lv, llo, lhi), (rv, rlo, rhi) = out, lhsT, rhs
            oa, la, ra = ov.ap(olo, ohi), lv.ap(llo, lhi), rv.ap(rlo, rhi)
            P.op(PE, lambda h: h.matmul(oa, la, ra, start=start, stop=stop),
                 reads=[lv.r(llo, lhi), rv.r(rlo, rhi)], writes=[ov.r(olo, ohi)])

        def act(out, in_, func, scale=None, bias=None):
            (ov, olo, ohi), (iv, ilo, ihi) = out, in_
            oa, ia = ov.ap(olo, ohi), iv.ap(ilo, ihi)
            reads = [iv.r(ilo, ihi)]
            kw = {}
            if scale is not None:
                if isinstance(scale, tuple):
                    kw["scale"] = scale[0]; reads.append(scale[1])
                else:
                    kw["scale"] = scale
            if bias is not None:
                kw["bias"] = bias[0]; reads.append(bias[1])
            P.op(ACT, lambda h: h.activation(out=oa, in_=ia, func=func, **kw), reads=reads, writes=[ov.r(olo, ohi)])

        def tt(eng, out, in0, in1, op):
            (ov, olo, ohi), (av, alo, ahi), (bv, blo, bhi) = out, in0, in1
            oa, aa, ba = ov.ap(olo, ohi), av.ap(alo, ahi), bv.ap(blo, bhi)
            P.op(eng, lambda h: h.tensor_tensor(oa, aa, ba, op),
                 reads=[av.r(alo, ahi), bv.r(blo, bhi)], writes=[ov.r(olo, ohi)])

        def stt(eng, out, in0, scalar, in1, op0, op1):
            (ov, olo, ohi), (av, alo, ahi), (bv, blo, bhi) = out, in0, in1
            oa, aa, ba = ov.ap(olo, ohi), av.ap(alo, ahi), bv.ap(blo, bhi)
            reads = [av.r(alo, ahi), bv.r(blo, bhi)]
            if isinstance(scalar, tuple):
                sc = scalar[0]; reads.append(scalar[1])
            else:
                sc = scalar
            P.op(eng, lambda h: h.scalar_tensor_tensor(oa, aa, sc, ba, op0, op1), reads=reads, writes=[ov.r(olo, ohi)])

        def recip(out, in_):
            (ov, olo, ohi), (iv, ilo, ihi) = out, in_
            oa, ia = ov.ap(olo, ohi), iv.ap(ilo, ihi)
            P.op(DVE, lambda h: h.reciprocal(oa, ia), reads=[iv.r(ilo, ihi)], writes=[ov.r(olo, ohi)])

        def copy(eng, out, in_):
            (ov, olo, ohi), (iv, ilo, ihi) = out, in_
            oa, ia = ov.ap(olo, ohi), iv.ap(ilo, ihi)
            P.op(eng, lambda h: h.tensor_copy(oa, ia), reads=[iv.r(ilo, ihi)], writes=[ov.r(olo, ohi)])

        def memset(eng, out, val):
            (ov, olo, ohi) = out
            oa = ov.ap(olo, ohi)
            P.op(eng, lambda h: h.memset(oa, val), writes=[ov.r(olo, ohi)])

        def dma(eng, out_ap, in_ap, reads=(), writes=()):
            P.op(eng, lambda h: h.dma_start(out=out_ap, in_=in_ap), reads=list(reads), writes=list(writes), dma=True)

        bank_ctr = [0, 0]

        def mm_bank(limit=6):
            k = bank_ctr[0] % limit
            bank_ctr[0] += 1
            return banks[k]

        def st_bank():
            k = 6 + bank_ctr[1] % 2
            bank_ctr[1] += 1
            return banks[k]

        rt_ctr = [0]

        def next_rt():
            r_ = rtv[rt_ctr[0] % 3]
            rt_ctr[0] += 1
            return r_

        memset(POOL, (ones1024, 0, 128), 1.0 / 1024)
        memset(POOL, (ones128, 0, 128), 1.0 / 128)
        memset(POOL, (onesblk, 0, 128), 0.0)
        oa = onesblk.base[0:64, 0:64]
        P.op(POOL, lambda h: h.memset(oa, 1.0 / 64), writes=[onesblk.r(0, 128)])
        ob = onesblk.base[64:128, 64:128]
        P.op(POOL, lambda h: h.memset(ob, 1.0 / 64), writes=[onesblk.r(0, 128)])
        memset(POOL, (epsv, 0, 1), EPS)
        for g in range(4):
            w = 2 << g
            memset(POOL, (invc[g], 0, 16), 1.0)
            for t in range(min(w - 1, 16)):
                memset(POOL, (invc[g], t, t + 1), float(w) / (t + 1))
        dma(SP, vec.ap(0, NV), vec_d, writes=[vec.r(0, NV)])
        dma(SP, b_stp.t[:, :], stp_d, writes=[(b_stp, 0, b_stp.width)])
        dma(SP, b_stc.t[:, :], stc_d, writes=[(b_stc, 0, b_stc.width)])
        dma(SP, b_stf.t[:, :], stf_d, writes=[(b_stf, 0, b_stf.width)])

        wseq = []
        for blk in range(2):
            for l in range(L):
                for ui in range(NU):
                    wseq.append((blk, l, ui))
        wstate = {"issued": 0, "used": 0, "live": set(), "next_ring": 0}
        ring_idx = []
        rc = 0
        for (_, l_, ui_) in wseq:
            if tab[ui_][0] == "pm":
                ring_idx.append(None)
            else:
                ring_idx.append(rc)
                rc += 1

        def issue_weights():
            while wstate["issued"] < len(wseq):
                q = wstate["issued"]
                (_, l, ui) = wseq[q]
                n = tab[ui][2]
                if ring_idx[q] is None:
                    if q > wstate["used"] + 4:
                        break
                    sv = pmbuf
                else:
                    r = ring_idx[q]
                    done_upto = min(wstate["live"]) if wstate["live"] else wstate["next_ring"]
                    if r - NSLOT >= done_upto:
                        break
                    sv = wslot[r % NSLOT]
                dma(POOL, sv.ap(0, n), wts_d[l, :, offs[ui]:offs[ui] + n], writes=[sv.r(0, n)])
                wstate["issued"] += 1

        def next_unit(kind):
            q = wstate["used"]
            wstate["used"] += 1
            (_, l, ui) = wseq[q]
            assert tab[ui][0] == kind, (tab[ui], kind)
            if ring_idx[q] is None:
                issue_weights()
                assert wstate["issued"] > q
                return pmbuf, None
            wstate["live"].add(ring_idx[q])
            wstate["next_ring"] = ring_idx[q] + 1
            issue_weights()
            assert wstate["issued"] > q
            return wslot[ring_idx[q] % NSLOT], ring_idx[q]

        def release(q):
            wstate["live"].discard(q)
            issue_weights()

        def stats_rstd(src, n, ones_v, nsrc, bank=None):
            pb = bank if bank is not None else st_bank()
            for i, (sv, lo) in enumerate(src):
                mm((pb, 0, n), (ones_v, 0, 128), (sv, lo, lo + n), i == 0, i == nsrc - 1)
            rt = next_rt()
            rstd_from(pb, rt, n)
            return rt

        def rstd_from(pb, rt, n):
            act((rt, 0, n), (pb, 0, n), AF.Ln, bias=(epsv.ap(0, 1), epsv.r(0, 1)))
            act((rt, 0, n), (rt, 0, n), AF.Exp, scale=-0.5)

        def pre_a(a, b):
            n = b - a
            for kt in range(KT):
                act((sqv[kt], 0, n), (xv[kt], a, b), AF.Square)

        def pre_b(l, goff, a, b, bank=None):
            n = b - a
            rt = stats_rstd([(sqv[kt], 0) for kt in range(KT)], n, ones1024, KT, bank=bank)
            for kt in range(KT):
                stt(DVE, (hv[kt], a, b), (xv[kt], a, b), vcol(l, goff + kt), (rt, 0, n), ALU.mult, ALU.mult)

        def pre_norm(l, goff, a, b):
            pre_a(a, b)
            pre_b(l, goff, a, b)

        def post_norm_residual(l, goff, raws, a, b, rt=None):
            n = b - a
            assert rt is not None
            for j in range(8):
                rv_, lo = raws[j]
                tt(DVE, (rv_, lo, lo + n), (rv_, lo, lo + n), (rt, 0, n), ALU.mult)
                tt(POOL if j % 2 == 0 else DVE, (xv[j], a, b), (xv[j], a, b), (rv_, lo, lo + n), ALU.add)

        blocks = [
            [dict(kind="p", t0=0, n=1024, pl=0, hs=0, first=True, last=False)],
            [dict(kind="p", t0=1024, n=1024, pl=0, hs=0, first=False, last=True),
             dict(kind="s", t0=0, n=DSEQ, pl=1024, hs=H + 1024, first=False, last=True)],
        ]

        def ap3(v, ncol_each, stride, cnt, lo, hi):
            full = v.b.t[:, v.off:v.off + stride * cnt].rearrange("p (g w) -> p g w", g=cnt)
            return full[:, :, lo:hi]

        for bi, segs in enumerate(blocks):
            W = sum(s["n"] for s in segs)
            chunks = []
            for si, s in enumerate(segs):
                c0 = 0
                while c0 < s["n"]:
                    n = min(512, s["n"] - c0)
                    chunks.append(dict(a=s["pl"] + c0, b=s["pl"] + c0 + n, n=n, seg=si,
                                       ha=s["hs"] + H + c0, hb=s["hs"] + H + c0 + n,
                                       first=(c0 == 0), last=(c0 + n == s["n"])))
                    c0 += n
            for c in chunks:
                s = segs[c["seg"]]
                t0 = s["t0"] + (c["a"] - s["pl"])
                for kt in range(KT):
                    src = (xp_d[kt, :, t0:t0 + c["n"]] if s["kind"] == "p" else xs_d[kt, :, t0:t0 + c["n"]])
                    dma(SP, xv[kt].ap(c["a"], c["b"]), src, writes=[xv[kt].r(c["a"], c["b"])])

            final_done = set()

            def final_norm(ci, stage, bank=None):
                c = chunks[ci]
                a, b, n = c["a"], c["b"], c["n"]
                s = segs[c["seg"]]
                pre_a(a, b)
                rt = stats_rstd([(sqv[kt], 0) for kt in range(KT)], n, ones1024, KT, bank=bank)
                for kt in range(KT):
                    st_ = stage[kt % len(stage)]
                    gcol = (vec.ap(L * NV_L + kt, L * NV_L + kt + 1), vec.r(L * NV_L + kt, L * NV_L + kt + 1))
                    stt(DVE, (st_, 0, n), (xv[kt], a, b), gcol, (rt, 0, n), ALU.mult, ALU.mult)
                    t0 = s["t0"] + (a - s["pl"])
                    dst = yp_d[kt, :, t0:t0 + n] if s["kind"] == "p" else ys_d[kt, :, t0:t0 + n]
                    dma(SP, dst, st_.ap(0, n), reads=[st_.r(0, n)])
                final_done.add(ci)

            for l in range(L):
                def halo_init(which, l=l):
                    for si, s in enumerate(segs):
                        hs = s["hs"]
                        if s["kind"] == "p" and s["first"]:
                            for g in range(4):
                                if which == "v":
                                    memset(POOL, (vbuf[g], hs, hs + H), 0.0)
                                else:
                                    memset(POOL, (cubuf[g], hs, hs + H), 0.0)
                        else:
                            srcb, srcc = (b_cv, b_cc) if s["kind"] == "p" else (b_stp, b_stc)
                            for g in range(4):
                                if which == "v":
                                    o15 = (l * 4 + g) * 15
                                    sa = srcb.t[:, o15:o15 + 15]
                                    da = vbuf[g].ap(hs + 1, hs + H)
                                    P.op(POOL, lambda h, da=da, sa=sa: h.tensor_copy(da, sa),
                                         reads=[(srcb, o15, o15 + 15)], writes=[vbuf[g].r(hs + 1, hs + H)])
                                else:
                                    o2 = (l * 4 + g) * 2
                                    sa2 = srcc.t[:, o2:o2 + 2]
                                    da2 = cubuf[g].ap(hs + H - 2, hs + H)
                                    P.op(POOL, lambda h, da2=da2, sa2=sa2: h.tensor_copy(da2, sa2),
                                         reads=[(srcc, o2, o2 + 2)], writes=[cubuf[g].r(hs + H - 2, hs + H)])

                def v_tiles(lv, cis, vunits):
                    for g in range(4):
                        ws = vunits[g // 2][0]
                        jj = g % 2
                        for ci in cis:
                            c = chunks[ci]
                            a, b, n, ha, hb = c["a"], c["b"], c["n"], c["ha"], c["hb"]
                            pb = mm_bank()
                            for kt in range(KT):
                                mm((pb, 0, n), (ws, kt * 256 + jj * 128, kt * 256 + jj * 128 + 128), (hv[kt], a, b),
                                   kt == 0, kt == KT - 1)
                            act((vbuf[g], ha, hb), (pb, 0, n), AF.Copy)

                nch = len(chunks)
                if l == 0:
                    halo_init("v")
                    for c in chunks:
                        pre_norm(l, V_GPRE, c["a"], c["b"])
                    vunits = [next_unit("z"), next_unit("z")]
                    v_tiles(l, range(nch), vunits)
                    release(vunits[0][1])
                    release(vunits[1][1])
                halo_init("cu")
                pm_slot, pmq = next_unit("pm")
                def pool_chain(g, c, par):
                    a, b, n, ha, hb = c["a"], c["b"], c["n"], c["ha"], c["hb"]
                    s = segs[c["seg"]]
                    hs = s["hs"]
                    cur = vbuf[g]
                    tf = Tfin[par]
                    for k in range(g + 1):
                        sh = 1 << k
                        if k == g:
                            tt(POOL, (tf, 0, n), (cur, ha, hb), (cur, ha - sh, hb - sh), ALU.add)
                        else:
                            dst = Tk[k]
                            lo = (hs + 2 * sh) if c["first"] else ha
                            tt(POOL, (dst, lo, hb), (cur, lo, hb), (cur, lo - sh, hb - sh), ALU.add)
                            cur = dst
                    if s["kind"] == "p" and s["first"] and c["first"]:
                        tt(POOL, (tf, 0, 16), (tf, 0, 16), (invc[g], 0, 16), ALU.mult)
                    if c is chunks[-1]:
                        for si, s in enumerate(segs):
                            he = s["hs"] + H + s["n"]
                            if s["last"]:
                                slot = 0 if s["kind"] == "p" else 1
                                o15 = ((l * 2 + slot) * 4 + g) * 15
                                dstb = b_op
                            else:
                                o15 = (l * 4 + g) * 15
                                dstb = b_cv
                            da = dstb.t[:, o15:o15 + 15]
                            sa = vbuf[g].ap(he - 15, he)
                            P.op(POOL, lambda h, da=da, sa=sa: h.tensor_copy(da, sa),
                                 reads=[vbuf[g].r(he - 15, he)], writes=[(dstb, o15, o15 + 15)])

                steps = [(j, ci) for j in range(4) for ci in range(len(chunks))]
                pool_chain(steps[0][0], chunks[steps[0][1]], 0)
                deferred = []

                def flush_deferred():
                    for fn in deferred:
                        fn()
                    del deferred[:]

                step = 0
                q_norm = []
                q_ppm = []

                def run_ppm():
                    for fn in q_ppm:
                        fn()
                    del q_ppm[:]

                for j in range(4):
                    ws, wq = next_unit("z")
                    for ci, c in enumerate(chunks):
                        a, b, n, ha, hb = c["a"], c["b"], c["n"], c["ha"], c["hb"]
                        pgc, pu, pgb = mm_bank(), mm_bank(), mm_bank()
                        for ti, pb in enumerate((pgc, pu, pgb)):
                            for kt in range(KT):
                                mm((pb, 0, n), (ws, kt * 384 + ti * 128, kt * 384 + ti * 128 + 128), (hv[kt], a, b),
                                   kt == 0, kt == KT - 1)
                        if step + 1 < len(steps):
                            pool_chain(steps[step + 1][0], chunks[steps[step + 1][1]], (step + 1) % 2)
                        stt(DVE, (pooled[j], a, b), (Tfin[step % 2], 0, n), 1.0 / (2 << j), (vbuf[j], ha, hb),
                            ALU.mult, ALU.subtract)
                        gc_ = gcs[0]
                        gb_ = gbs[step % 2]
                        ca = cacc[0]
                        yc = yrot[(2 * step) % 4]
                        sqc = sqh[(2 * step) % 4]
                        act((gc_, 0, n), (pgc, 0, n), AF.Copy)
                        act((gb_, 0, n), (pgb, 0, n), AF.Copy)
                        tt(DVE, (cubuf[j], ha, hb), (pu, 0, n), (gc_, 0, n), ALU.mult)
                        act((ca, 0, n), (cubuf[j], ha, hb), AF.Copy, scale=vcol(l, V_CW + 4 * 2 + j))
                        prev = list(q_norm)
                        del q_norm[:]
                        run_ppm()
                        for (fn_b, fn_c) in prev:
                            fn_b()
                        stt(DVE, (ca, 0, n), (cubuf[j], ha - 1, hb - 1), vcol(l, V_CW + 4 * 1 + j), (ca, 0, n),
                            ALU.mult, ALU.add)
                        stt(DVE, (ca, 0, n), (cubuf[j], ha - 2, hb - 2), vcol(l, V_CW + 4 * 0 + j), (ca, 0, n),
                            ALU.mult, ALU.add)
                        tt(DVE, (yc, 0, n), (gb_, 0, n), (ca, 0, n), ALU.mult)
                        for (fn_b, fn_c) in prev:
                            fn_c()
                        act((sqc, 0, n), (yc, 0, n), AF.Square)

                        boxc = {}

                        def stage_b_c(n=n, sqc=sqc, box=boxc):
                            box["rt"] = stats_rstd([(sqc, 0)], n, onesblk, 1)

                        def stage_c_c(j=j, a=a, b=b, n=n, yc=yc, box=boxc):
                            stt(DVE, (cat[4 + j], a, b), (yc, 0, n), vcol(l, V_GCONV + j), (box["rt"], 0, n),
                                ALU.mult, ALU.mult)

                        q_norm.append((stage_b_c, stage_c_c))

                        def ppm_stage(j=j, a=a, b=b, n=n, step=step):
                            ppm = mm_bank()
                            mm((ppm, 0, n), (pm_slot, j * 128, j * 128 + 128), (pooled[j], a, b), True, True)
                            yp_ = yrot[(2 * step + 3) % 4]
                            sqp = sqh[(2 * step + 3) % 4]
                            act((yp_, 0, n), (ppm, 0, n), AF.Copy, scale=vcol(l, V_PSCALE + j))
                            act((sqp, 0, n), (ppm, 0, n), AF.Square, scale=vcol(l, V_PSCALE + j))
                            boxp = {}

                            def stage_b_p(n=n, sqp=sqp, box=boxp):
                                box["rt"] = stats_rstd([(sqp, 0)], n, ones128, 1)

                            def stage_c_p(j=j, a=a, b=b, n=n, yp_=yp_, box=boxp):
                                stt(DVE, (cat[j], a, b), (yp_, 0, n), vcol(l, V_GPOOL + j), (box["rt"], 0, n),
                                    ALU.mult, ALU.mult)

                            q_norm.append((stage_b_p, stage_c_p))

                        q_ppm.append(ppm_stage)
                        step += 1
                    release(wq)
                    for si, s in enumerate(segs):
                        he = s["hs"] + H + s["n"]
                        if s["last"]:
                            slot = 0 if s["kind"] == "p" else 1
                            o2 = ((l * 2 + slot) * 4 + j) * 2
                            dstb = b_oc
                        else:
                            o2 = (l * 4 + j) * 2
                            dstb = b_cc
                        da = dstb.t[:, o2:o2 + 2]
                        sa = cubuf[j].ap(he - 2, he)
                        P.op(POOL, lambda h, da=da, sa=sa: h.tensor_copy(da, sa),
                             reads=[cubuf[j].r(he - 2, he)], writes=[(dstb, o2, o2 + 2)])
                prev = list(q_norm)
                del q_norm[:]
                run_ppm()
                for (fn_b, fn_c) in prev:
                    fn_b()
                    fn_c()
                last_def = list(q_norm)
                del q_norm[:]
                release(pmq)
                ou = [next_unit("o") for _ in range(3)]
                oslots = [u_[0] for u_ in ou]
                ocols = (384, 384, 256)

                def o_mm(ci, js=range(8)):
                    c = chunks[ci]
                    a, b, n = c["a"], c["b"], c["n"]
                    mr = mixraw[min(ci, 1)]
                    for j in js:
                        ui, jj = (j // 3, j % 3) if j < 6 else (2, j - 6)
                        ws, ncol = oslots[ui], ocols[ui]
                        pb = mm_bank()
                        for kt in range(KT):
                            mm((pb, 0, n), (ws, kt * ncol + jj * 128, kt * ncol + jj * 128 + 128), (cat[kt], a, b),
                               kt == 0, kt == KT - 1)
                        act((mr[j], 0, n), (pb, 0, n), AF.Copy, scale=vcol(l, V_GPOST + j))
                        act((sqO[j], 0, n), (pb, 0, n), AF.Square)

                def o_post(ci):
                    c = chunks[ci]
                    a, b, n = c["a"], c["b"], c["n"]
                    mr = mixraw[min(ci, 1)]
                    rt = stats_rstd([(sqO[j], 0) for j in range(8)], n, ones1024, 8)
                    post_norm_residual(l, V_GPOST, [(mr[j], 0) for j in range(8)], a, b, rt=rt)

                nch = len(chunks)
                if nch > 2:
                    for (fn_b, fn_c) in last_def:
                        fn_b()
                        fn_c()
                    o_mm(0)
                else:
                    o_mm(0, range(0, 4))
                    for (fn_b, fn_c) in last_def:
                        fn_b()
                        fn_c()
                    o_mm(0, range(4, 8))
                o_post(0)
                for k in range(1, nch):
                    if k == 1:
                        o_mm(k, range(0, 5))
                        pre_a(chunks[0]["a"], chunks[0]["b"])
                        o_mm(k, range(5, 8))
                    else:
                        o_mm(k)
                    if k == nch - 1:
                        for u_ in ou:
                            release(u_[1])
                    if k == 1:
                        pre_b(l, V_GPREF, chunks[0]["a"], chunks[0]["b"])
                    o_post(k)
                NLEAD = 3

                def u_halo(i):
                    for t_ in range(2):
                        jt = i + NPAIR * t_
                        us = upraw[(2 * i + t_) % 6]
                        for si, s in enumerate(segs):
                            hs = s["hs"]
                            if s["kind"] == "p" and s["first"]:
                                memset(POOL, (us, hs + H - 2, hs + H), 0.0)
                            else:
                                srcb = b_cf if s["kind"] == "p" else b_stf
                                o2 = (l * 44 + jt) * 2
                                sa = srcb.t[:, o2:o2 + 2]
                                da = us.ap(hs + H - 2, hs + H)
                                P.op(POOL, lambda h, da=da, sa=sa: h.tensor_copy(da, sa),
                                     reads=[(srcb, o2, o2 + 2)], writes=[us.r(hs + H - 2, hs + H)])

                def u_mm(i, ws, ci, small_bank=None):
                    c = chunks[ci]
                    a, b, n = c["a"], c["b"], c["n"]
                    pbs = []
                    for t_ in range(2):
                        if small_bank is not None:
                            pb, c0_ = small_bank, 64 * t_
                        else:
                            pb, c0_ = mm_bank(), 0
                        for kt in range(KT):
                            mm((pb, c0_, c0_ + n), (ws, kt * 256 + t_ * 128, kt * 256 + t_ * 128 + 128), (hv[kt], a, b),
                               kt == 0, kt == KT - 1)
                        pbs.append((pb, c0_))
                    return pbs

                def u_ew(i, ci, pbs, defer_b=None):
                    c = chunks[ci]
                    a, b, n, ha, hb = c["a"], c["b"], c["n"], c["ha"], c["hb"]
                    accs = []
                    for t_ in range(2):
                        jt = i + NPAIR * t_
                        us = upraw[(2 * i + t_) % 6]
                        pb, c0_ = pbs[t_]
                        if n <= 64:
                            fa = facc_s[fctr[1] % 4]
                            fctr[1] += 1
                        else:
                            fa = facc[fctr[0] % 4]
                            fctr[0] += 1
                        act((us, ha, hb), (pb, c0_, c0_ + n), AF.Copy)
                        act((fa, 0, n), (pb, c0_, c0_ + n), AF.Copy, scale=vcol(l, V_FW + 44 * 2 + jt))
                        stt(DVE, (fa, 0, n), (us, ha - 1, hb - 1), vcol(l, V_FW + 44 * 1 + jt), (fa, 0, n),
                            ALU.mult, ALU.add)
                        stt(DVE, (fa, 0, n), (us, ha - 2, hb - 2), vcol(l, V_FW + 44 * 0 + jt), (fa, 0, n),
                            ALU.mult, ALU.add)
                        accs.append(fa)

                    def part_b(i=i, a=a, b=b, n=n, accs=accs):
                        act((accs[0], 0, n), (accs[0], 0, n), AF.Silu)
                        tt(DVE, (actv[i], a, b), (accs[0], 0, n), (accs[1], 0, n), ALU.mult)

                    if defer_b is None:
                        part_b()
                    else:
                        defer_b.append(part_b)

                def u_step(i, ws, ci):
                    u_ew(i, ci, u_mm(i, ws, ci))

                def u_step2(i, ws):
                    c0, c1 = chunks[0], chunks[1]
                    a, b = c0["a"], c1["b"]
                    ha, hb = c0["ha"], c1["hb"]
                    assert c0["b"] == c1["a"] and c0["hb"] == c1["ha"] and b - a == 1024
                    accs = []
                    for t_ in range(2):
                        jt = i + NPAIR * t_
                        us = upraw[(2 * i + t_) % 6]
                        k2 = bp_ctr[0] % 3
                        bp_ctr[0] += 1
                        for ci, c in enumerate((c0, c1)):
                            pb = banks[2 * k2 + ci]
                            for kt in range(KT):
                                mm((pb, 0, 512), (ws, kt * 256 + t_ * 128, kt * 256 + t_ * 128 + 128),
                                   (hv[kt], c["a"], c["b"]), kt == 0, kt == KT - 1)
                        pp = bankpair[k2]
                        fa = facc[fctr[0] % 4]
                        fctr[0] += 1
                        act((us, ha, hb), (pp, 0, 1024), AF.Copy)
                        act((fa, 0, 1024), (pp, 0, 1024), AF.Copy, scale=vcol(l, V_FW + 44 * 2 + jt))
                        stt(DVE, (fa, 0, 1024), (us, ha - 1, hb - 1), vcol(l, V_FW + 44 * 1 + jt), (fa, 0, 1024),
                            ALU.mult, ALU.add)
                        stt(DVE, (fa, 0, 1024), (us, ha - 2, hb - 2), vcol(l, V_FW + 44 * 0 + jt), (fa, 0, 1024),
                            ALU.mult, ALU.add)
                        accs.append(fa)

                    def part_b(i=i, a=a, b=b, accs=accs):
                        act((accs[0], 0, 1024), (accs[0], 0, 1024), AF.Silu)
                        tt(DVE, (actv[i], a, b), (accs[0], 0, 1024), (accs[1], 0, 1024), ALU.mult)

                    return part_b

                def u_tails(i):
                    for t_ in range(2):
                        jt = i + NPAIR * t_
                        us = upraw[(2 * i + t_) % 6]
                        for si, s in enumerate(segs):
                            he = s["hs"] + H + s["n"]
                            if s["last"]:
                                slot = 0 if s["kind"] == "p" else 1
                                o2 = ((l * 2 + slot) * 44 + jt) * 2
                                dstb = b_of
                            else:
                                o2 = (l * 44 + jt) * 2
                                dstb = b_cf
                            da = dstb.t[:, o2:o2 + 2]
                            sa = us.ap(he - 2, he)
                            P.op(POOL, lambda h, da=da, sa=sa: h.tensor_copy(da, sa),
                                 reads=[us.r(he - 2, he)], writes=[(dstb, o2, o2 + 2)])

                fctr = [0, 0]
                lead = []
                pend = []
                if nch > 2:
                    for i in range(2):
                        ws, wq = next_unit("u")
                        lead.append((ws, wq))
                        u_halo(i)
                        u_step(i, ws, 0)
                    pre_a(chunks[1]["a"], chunks[1]["b"])
                    pre_b(l, V_GPREF, chunks[1]["a"], chunks[1]["b"])
                    for i in range(2):
                        u_step(i, lead[i][0], 1)
                for i in range(len(lead), NLEAD):
                    ws, wq = next_unit("u")
                    lead.append((ws, wq))
                    u_halo(i)
                    for ci in range(nch - 1):
                        nleft = (NLEAD - 1 - i) * (nch - 1) + (nch - 2 - ci)
                        if nleft >= 2:
                            u_step(i, ws, ci)
                        else:
                            pend.append((i, ci, u_mm(i, ws, ci)))
                    if i == 1 or (nch > 2 and i == 2):
                        pre_a(chunks[nch - 1]["a"], chunks[nch - 1]["b"])
                pre_b(l, V_GPREF, chunks[nch - 1]["a"], chunks[nch - 1]["b"])
                for (i, ci, pbs) in pend:
                    u_ew(i, ci, pbs)
                for i in range(NLEAD):
                    ws, wq = lead[i]
                    u_step(i, ws, nch - 1)
                    release(wq)
                    u_tails(i)
                u_halo(NLEAD)
                small_b = []
                for i in range(NLEAD, NPAIR):
                    ws, wq = next_unit("u")
                    if i + 1 < NPAIR:
                        u_halo(i + 1)
                    pb2 = u_step2(i, ws)
                    prev_small = list(small_b)
                    del small_b[:]
                    for ci in range(2, nch):
                        sb_ = banks[6 + (bp_ctr[1] % 2)]
                        bp_ctr[1] += 1
                        u_ew(i, ci, u_mm(i, ws, ci, small_bank=sb_), defer_b=small_b)
                    pb2()
                    for fn in prev_small:
                        fn()
                    release(wq)
                    u_tails(i)
                for fn in small_b:
                    fn()
                del small_b[:]
                def d_step(j, ws, ci):
                    c = chunks[ci]
                    a, b, n = c["a"], c["b"], c["n"]
                    pb = mm_bank(5)
                    for kt in range(NPAIR):
                        mm((pb, 0, n), (ws, kt * 128, kt * 128 + 128), (actv[kt], a, b), kt == 0, kt == NPAIR - 1)
                    act((fraw[j], a, b), (pb, 0, n), AF.Copy, scale=vcol(l, V_GPOSTF + j))
                    fq = fsq[fctr[2] % 3]
                    fctr[2] += 1
                    act((fq, 0, n), (pb, 0, n), AF.Square)
                    flush_deferred()
                    deferred.append(lambda ci=ci, n=n, fq=fq, j=j: mm((banks[5 + ci], 0, n), (ones1024, 0, 128),
                                                                     (fq, 0, n), j == 0, j == 7))

                def post_fin(ci):
                    c = chunks[ci]
                    a, b, n = c["a"], c["b"], c["n"]
                    rt = next_rt()
                    rstd_from(banks[5 + ci], rt, n)
                    post_norm_residual(l, V_GPOSTF, [(fraw[j], a) for j in range(8)], a, b, rt=rt)

                fctr.append(0)
                KL = 3
                for j in range(8 - KL):
                    ws, wq = next_unit("d")
                    for ci in range(nch):
                        d_step(j, ws, ci)
                    release(wq)
                if KL:
                    leadd = []
                    for j in range(8 - KL, 8):
                        ws, wq = next_unit("d")
                        leadd.append((j, ws, wq))
                        d_step(j, ws, 0)
                    bsteps = [(j, ws, ci) for (j, ws, wq) in leadd for ci in range(1, nch)]
                    d_step(*bsteps[0])
                    post_fin(0)
                    for st in bsteps[1:]:
                        d_step(*st)
                    for (j, ws, wq) in leadd:
                        release(wq)
                    flush_deferred()
                    if l == L - 1:
                        final_norm(0, fstage, bank=banks[5])
                        for ci in range(1, nch):
                            post_fin(ci)
                        continue
                    pre_a(chunks[0]["a"], chunks[0]["b"])
                    pre_b(l + 1, V_GPRE, chunks[0]["a"], chunks[0]["b"], bank=banks[5])
                    for ci in range(1, nch):
                        post_fin(ci)
                    halo_init("v", l + 1)
                    vunits = [next_unit("z"), next_unit("z")]
                    v_tiles(l + 1, [0], vunits)
                    for ci in range(1, nch):
                        pre_a(chunks[ci]["a"], chunks[ci]["b"])
                        pre_b(l + 1, V_GPRE, chunks[ci]["a"], chunks[ci]["b"])
                    v_tiles(l + 1, range(1, nch), vunits)
                    release(vunits[0][1])
                    release(vunits[1][1])
                else:
                    flush_deferred()
                    for ci in range(nch):
                        post_fin(ci)
            for ci in range(len(chunks)):
                if ci not in final_done:
                    final_norm(ci, mixraw[ci % 2])
        dma(SP, opool_d, b_op.t[:, :], reads=[(b_op, 0, b_op.width)])
        dma(SP, oconv_d, b_oc.t[:, :], reads=[(b_oc, 0, b_oc.width)])
        dma(SP, offn_d, b_of.t[:, :], reads=[(b_of, 0, b_of.width)])
        assert wstate["used"] == len(wseq), (wstate, len(wseq))

        run = P.emit(sems)
        with nc.Block() as block:
            @block.sync
            def _(h):
                run(SP, h)

            @block.scalar
            def _(h):
                run(ACT, h)

            @block.vector
            def _(h):
                run(DVE, h)

            @block.gpsimd
            def _(h):
                run(POOL, h)

            @block.tensor
            def _(h):
                run(PE, h)
    return nc


def kernel(x_prompt, x_sample, state_pool, state_conv, state_ffn_conv, w_in, pool_mix, pool_scale, conv_w,
           g_pool_out, g_conv_out, w_out, g_pre_mix, g_post_mix, g_pre_ffn, g_post_ffn, w_up, ffn_conv_w,
           w_down, g_final):
    f = lambda a: np.asarray(a, dtype=np.float32)
    x_prompt, x_sample, state_pool, state_conv, state_ffn_conv = map(f, (x_prompt, x_sample, state_pool,
                                                                         state_conv, state_ffn_conv))
    wts = pack_weights(f(w_in), f(pool_mix), f(w_out), f(w_up), f(w_down))
    vecs = pack_vecs(f(pool_scale), f(conv_w), f(g_pool_out), f(g_conv_out), f(g_pre_mix), f(g_post_mix),
                     f(g_pre_ffn), f(g_post_ffn), f(ffn_conv_w), f(g_final))
    in_maps = []
    for b in range(NCORES):
        xp = np.ascontiguousarray(x_prompt[b].T).reshape(KT, 128, SEQ)
        xs = np.ascontiguousarray(x_sample[b].T).reshape(KT, 128, DSEQ)
        stp = np.ascontiguousarray(state_pool[:, b].reshape(L, 15, 4, 128).transpose(3, 0, 2, 1)).reshape(128, -1)
        stc = np.ascontiguousarray(state_conv[:, b].reshape(L, 2, 4, 128).transpose(3, 0, 2, 1)).reshape(128, -1)
        stf = np.ascontiguousarray(state_ffn_conv[:, b].reshape(L, 2, 44, 128).transpose(3, 0, 2, 1)).reshape(128, -1)
        in_maps.append({"xp": xp, "xs": xs, "stp": stp, "stc": stc, "stf": stf, "vecs": vecs, "wts": wts})
    nc = build_nc()
    res = run_bass_kernel_spmd(nc, in_maps, core_ids=list(range(NCORES)))
    B = NCORES
    y_p = np.empty((B, SEQ, D), np.float32)
    y_s = np.empty((B, DSEQ, D), np.float32)
    npool = [np.empty((L, B, 15, 512), np.float32) for _ in range(2)]
    nconv = [np.empty((L, B, 2, 512), np.float32) for _ in range(2)]
    nffn = [np.empty((L, B, 2, 2 * DFF), np.float32) for _ in range(2)]
    for b in range(B):
        r = res.results[b]
        y_p[b] = np.asarray(r["yp"]).reshape(D, SEQ).T
        y_s[b] = np.asarray(r["ys"]).reshape(D, DSEQ).T
        op = np.asarray(r["opool"]).reshape(128, L, 2, 4, 15)
        oc = np.asarray(r["oconv"]).reshape(128, L, 2, 4, 2)
        of = np.asarray(r["offn"]).reshape(128, L, 2, 44, 2)
        for s in range(2):
            npool[s][:, b] = op[:, :, s].transpose(1, 3, 2, 0).reshape(L, 15, 512)
            nconv[s][:, b] = oc[:, :, s].transpose(1, 3, 2, 0).reshape(L, 2, 512)
            nffn[s][:, b] = of[:, :, s].transpose(1, 3, 2, 0).reshape(L, 2, 2 * DFF)
    return (y_p, y_s, npool[0], nconv[0], nffn[0], npool[1], nconv[1], nffn[1])
```

```python
import numpy as np
from contextlib import ExitStack
import concourse.bass as bass
import concourse.mybir as mybir
from concourse.bass_utils import run_bass_kernel_spmd

F32 = mybir.dt.float32
BF16 = mybir.dt.bfloat16
ALU = mybir.AluOpType
AF = mybir.ActivationFunctionType

PE, ACT, DVE, POOL, SP = "pe", "act", "dve", "pool", "sp"
COMPUTE = (PE, ACT, DVE, POOL)

NCORES = 8
L = 4
D = 1024
KT = 8
DFF = 2816
NPAIR = 22
SEQ = 2048
DSEQ = 64
EPS = 1e-6
H = 16
NSLOT = 5
SLOT_E = 3072
PREFETCH = 2
NV_L = 188
NV = L * NV_L + 8
V_GPRE, V_GPOST, V_GPREF, V_GPOSTF, V_PSCALE, V_GPOOL, V_GCONV, V_CW, V_FW = 0, 8, 16, 24, 32, 36, 40, 44, 56


class Buf:
    def __init__(self, name, t, width):
        self.name = name
        self.t = t
        self.width = width
        self.recs = []


class Op:
    __slots__ = ("eng", "fn", "deps", "signal", "count", "is_dma", "dma_sem", "dma_val", "dma_prev")

    def __init__(self, eng, fn, is_dma):
        self.eng = eng
        self.fn = fn
        self.deps = []
        self.signal = False
        self.count = 0
        self.is_dma = is_dma
        self.dma_sem = None
        self.dma_val = 0
        self.dma_prev = None


class Prog:
    def __init__(self, dma_ring=8):
        self.streams = {e: [] for e in (PE, ACT, DVE, POOL, SP)}
        self.dma_ring = dma_ring
        self.dma_count = {e: 0 for e in (ACT, POOL, SP)}
        self.dma_hist = {e: [] for e in (ACT, POOL, SP)}

    def op(self, eng, fn, reads=(), writes=(), dma=False):
        o = Op(eng, fn, dma)
        deps = {}
        for (b, lo, hi) in reads:
            assert 0 <= lo < hi <= b.width, (b.name, lo, hi, b.width)
            for (l2, h2, k2, o2) in b.recs:
                if k2 == "w" and l2 < hi and lo < h2:
                    deps[id(o2)] = (o2, True)
        for (b, lo, hi) in writes:
            assert 0 <= lo < hi <= b.width, (b.name, lo, hi, b.width)
            for (l2, h2, k2, o2) in b.recs:
                if l2 < hi and lo < h2 and id(o2) not in deps:
                    deps[id(o2)] = (o2, False)
        for (b, lo, hi) in writes:
            b.recs = [r for r in b.recs if not (lo <= r[0] and r[1] <= hi)]
            b.recs.append((lo, hi, "w", o))
        for (b, lo, hi) in reads:
            if not dma:
                b.recs = [r for r in b.recs
                          if not (r[2] == "r" and r[3].eng == eng and not r[3].is_dma
                                  and lo <= r[0] and r[1] <= hi)]
            b.recs.append((lo, hi, "r", o))
        for (o2, israw) in deps.values():
            if o2 is o:
                continue
            if (not o2.is_dma) and (not dma) and o2.eng == eng:
                if eng == PE:
                    continue
            o.deps.append(o2)
            if not o2.is_dma:
                o2.signal = True
        if dma:
            k = self.dma_count[eng]
            self.dma_count[eng] += 1
            o.dma_val = 16 * (k // self.dma_ring + 1)
            o.dma_sem = k % self.dma_ring
            hist = self.dma_hist[eng]
            if k >= self.dma_ring:
                o.dma_prev = hist[k - self.dma_ring]
            hist.append(o)
        self.streams[eng].append(o)
        return o

    def emit(self, sems):
        tl = sems["tl"]
        dsem = sems["dma"]
        for e in COMPUTE:
            c = 0
            for o in self.streams[e]:
                if not o.is_dma and o.signal:
                    c += 1
                    o.count = c

        def run_stream(e, h):
            waited = {}
            for o in self.streams[e]:
                waits = {}
                deps = list(o.deps)
                if o.is_dma and o.dma_prev is not None:
                    deps.append(o.dma_prev)
                for d in deps:
                    if d.is_dma:
                        key = ("dma", d.eng, d.dma_sem)
                        val = d.dma_val
                        sem = dsem[d.eng][d.dma_sem]
                    else:
                        key = ("tl", d.eng)
                        val = d.count
                        sem = tl[d.eng]
                        assert val > 0
                    if waited.get(key, 0) >= val:
                        continue
                    if key not in waits or waits[key][1] < val:
                        waits[key] = (sem, val)
                for key, (sem, val) in waits.items():
                    h.wait_ge(sem, val)
                    waited[key] = val
                ins = o.fn(h)
                if o.is_dma:
                    ins.then_inc(dsem[o.eng][o.dma_sem], 16)
                elif o.signal:
                    ins.then_inc(tl[o.eng], 1)
            if e in self.dma_hist:
                last = {}
                for d in self.dma_hist[e]:
                    last[d.dma_sem] = d
                for d in last.values():
                    h.wait_ge(dsem[e][d.dma_sem], d.dma_val)

        return run_stream


class V:
    def __init__(self, b, off32, n, dt):
        self.b = b
        self.off = off32
        self.n = n
        self.dt = dt
        if dt == F32:
            self.base = b.t[:, off32:off32 + n]
        else:
            assert n % 2 == 0
            self.base = b.t[:, off32:off32 + n // 2].bitcast(BF16)

    def ap(self, lo, hi):
        assert 0 <= lo < hi <= self.n, (self.b.name, lo, hi, self.n)
        return self.base[:, lo:hi]

    def r(self, lo, hi):
        assert 0 <= lo < hi <= self.n, (self.b.name, lo, hi, self.n)
        if self.dt == F32:
            return (self.b, self.off + lo, self.off + hi)
        return (self.b, self.off + lo // 2, self.off + (hi + 1) // 2)

    def sub(self, lo, n):
        if self.dt == F32:
            return V(self.b, self.off + lo, n, F32)
        assert lo % 2 == 0
        return V(self.b, self.off + lo // 2, n, BF16)


def _fm(vec):
    return np.ascontiguousarray(vec.reshape(-1, 128).T)


def _unit(w, cols):
    k = w.shape[0] // 128
    u = w[:, cols].reshape(k, 128, len(cols)).transpose(1, 0, 2)
    return u.reshape(128, k * len(cols))


def _zin_cols():
    units = [np.arange(0, 256), np.arange(256, 512)]
    for j in range(4):
        units.append(np.concatenate([np.arange(1024 + 128 * j, 1024 + 128 * j + 128),
                                     np.arange(1536 + 128 * j, 1536 + 128 * j + 128),
                                     np.arange(512 + 128 * j, 512 + 128 * j + 128)]))
    return units


def unit_table():
    tab = []
    zc = _zin_cols()
    for i in (0, 1):
        tab.append(("z", i, 8 * len(zc[i])))
    tab.append(("pm", 0, 512))
    for i in (2, 3, 4, 5):
        tab.append(("z", i, 8 * len(zc[i])))
    for i, nc_ in enumerate((384, 384, 256)):
        tab.append(("o", i, 8 * nc_))
    for i in range(NPAIR):
        tab.append(("u", i, 8 * 256))
    for j in range(8):
        tab.append(("d", j, NPAIR * 128))
    offs = []
    o = 0
    for t in tab:
        offs.append(o)
        o += t[2]
    return tab, offs, o


def pack_weights(w_in, pool_mix, w_out, w_up, w_down):
    tab, offs, tot = unit_table()
    out = np.empty((L, 128, tot), np.float32)
    zc = _zin_cols()
    oc = [np.arange(0, 384), np.arange(384, 768), np.arange(768, 1024)]
    for l in range(L):
        for (kind, i, n), o in zip(tab, offs):
            if kind == "z":
                a = _unit(w_in[l], zc[i])
            elif kind == "pm":
                a = pool_mix[l].transpose(1, 0, 2).reshape(128, 512)
            elif kind == "o":
                a = _unit(w_out[l], oc[i])
            elif kind == "u":
                a = _unit(w_up[l], np.concatenate([np.arange(128 * i, 128 * i + 128),
                                                   np.arange(DFF + 128 * i, DFF + 128 * i + 128)]))
            else:
                a = _unit(w_down[l], np.arange(128 * i, 128 * i + 128))
            out[l, :, o:o + n] = a
    return out


def pack_vecs(pool_scale, conv_w, g_pool_out, g_conv_out, g_pre_mix, g_post_mix, g_pre_ffn, g_post_ffn,
              ffn_conv_w, g_final):
    v = np.zeros((128, NV), np.float32)
    for l in range(L):
        b = l * NV_L
        v[:, b + V_GPRE:b + V_GPRE + 8] = _fm(g_pre_mix[l])
        v[:, b + V_GPOST:b + V_GPOST + 8] = _fm(g_post_mix[l])
        v[:, b + V_GPREF:b + V_GPREF + 8] = _fm(g_pre_ffn[l])
        v[:, b + V_GPOSTF:b + V_GPOSTF + 8] = _fm(g_post_ffn[l])
        v[:, b + V_PSCALE:b + V_PSCALE + 4] = _fm(pool_scale[l])
        v[:, b + V_GPOOL:b + V_GPOOL + 4] = _fm(g_pool_out[l])
        v[:, b + V_GCONV:b + V_GCONV + 4] = _fm(g_conv_out[l])
        for k in range(3):
            v[:, b + V_CW + 4 * k:b + V_CW + 4 * k + 4] = _fm(conv_w[l, k])
            v[:, b + V_FW + 44 * k:b + V_FW + 44 * k + 44] = _fm(ffn_conv_w[l, k])
    v[:, L * NV_L:L * NV_L + 8] = _fm(g_final)
    return v


def build_nc():
    nc = bass.Bass("TRN2", target_bir_lowering=False)
    tab, offs, WTOT = unit_table()
    NU = len(tab)
    xp_d = nc.dram_tensor("xp", [KT, 128, SEQ], F32, kind="ExternalInput").ap()
    xs_d = nc.dram_tensor("xs", [KT, 128, DSEQ], F32, kind="ExternalInput").ap()
    stp_d = nc.dram_tensor("stp", [128, L * 4 * 15], F32, kind="ExternalInput").ap()
    stc_d = nc.dram_tensor("stc", [128, L * 4 * 2], F32, kind="ExternalInput").ap()
    stf_d = nc.dram_tensor("stf", [128, L * 44 * 2], F32, kind="ExternalInput").ap()
    vec_d = nc.dram_tensor("vecs", [128, NV], F32, kind="ExternalInput").ap()
    wts_d = nc.dram_tensor("wts", [L, 128, WTOT], F32, kind="ExternalInput").ap()
    yp_d = nc.dram_tensor("yp", [KT, 128, SEQ], F32, kind="ExternalOutput").ap()
    ys_d = nc.dram_tensor("ys", [KT, 128, DSEQ], F32, kind="ExternalOutput").ap()
    opool_d = nc.dram_tensor("opool", [128, L * 2 * 4 * 15], F32, kind="ExternalOutput").ap()
    oconv_d = nc.dram_tensor("oconv", [128, L * 2 * 4 * 2], F32, kind="ExternalOutput").ap()
    offn_d = nc.dram_tensor("offn", [128, L * 2 * 44 * 2], F32, kind="ExternalOutput").ap()

    WMAX = 1088
    WH = WMAX + 2 * H
    es = ExitStack()
    with es:
        def sbuf(name, n32):
            t = es.enter_context(nc.sbuf_tensor("sb_" + name, [128, n32], F32))
            return Buf(name, t, n32)

        b_x = sbuf("x", KT * WMAX)
        b_h = sbuf("h", KT * WMAX // 2)
        b_sq = sbuf("sq", KT * 512 // 2)
        b_rt = sbuf("rt", 3 * 512)
        b_w = sbuf("w", NSLOT * SLOT_E // 2)
        b_vec = sbuf("vec", NV)
        b_stp = sbuf("stp", L * 4 * 15)
        b_stc = sbuf("stc", L * 4 * 2)
        b_stf = sbuf("stf", L * 44 * 2)
        b_cv = sbuf("cv", L * 4 * 15)
        b_cc = sbuf("cc", L * 4 * 2)
        b_cf = sbuf("cf", L * 44 * 2)
        b_op = sbuf("op", L * 2 * 4 * 15)
        b_oc = sbuf("oc", L * 2 * 4 * 2)
        b_of = sbuf("of", L * 2 * 44 * 2)
        b_cst = sbuf("cst", 3 * 64 + 1 + 64 + 3)
        b_pm = sbuf("pm", 256)
        PH_N = 25200
        PH_N = 25008
        b_ph = sbuf("ph", PH_N)
        pst = es.enter_context(nc.psum_tensor("psum_all", [128, 8 * 512], F32))
        b_ps = Buf("ps", pst, 8 * 512)

        sems = {"tl": {e: es.enter_context(nc.semaphore("tl_" + e)) for e in COMPUTE},
                "dma": {e: [es.enter_context(nc.semaphore("d_%s_%d" % (e, i))) for i in range(8)]
                        for e in (ACT, POOL, SP)}}
        P = Prog()

        xv = [V(b_x, kt * WMAX, WMAX, F32) for kt in range(KT)]
        hv = [V(b_h, kt * WMAX // 2, WMAX, BF16) for kt in range(KT)]
        sqv = [V(b_sq, kt * 256, 512, BF16) for kt in range(KT)]
        rtv = [V(b_rt, i * 512, 512, F32) for i in range(3)]
        wslot = [V(b_w, s * SLOT_E // 2, SLOT_E, BF16) for s in range(NSLOT)]
        vec = V(b_vec, 0, NV, F32)
        pmbuf = V(b_pm, 0, 512, BF16)
        ones1024 = V(b_cst, 0, 128, BF16)
        ones128 = V(b_cst, 64, 128, BF16)
        onesblk = V(b_cst, 128, 128, BF16)
        epsv = V(b_cst, 192, 1, F32)
        invc = [V(b_cst, 193 + 16 * g, 16, F32) for g in range(4)]
        fixt = V(b_cst, 193 + 64, 3, F32)
        banks = [V(b_ps, k * 512, 512, F32) for k in range(8)]
        bankpair = [V(b_ps, k * 1024, 1024, F32) for k in range(4)]
        bp_ctr = [0, 0]
        o = 0
        vbuf = [V(b_ph, o + g * WH, WH, F32) for g in range(4)]; o += 4 * WH
        pooled = [V(b_ph, o + g * (WMAX // 2), WMAX, BF16) for g in range(4)]; o += 4 * WMAX // 2
        gcs = [V(b_ph, o + i * 512, 512, F32) for i in range(1)]; o += 512
        gbs = [V(b_ph, o + i * 512, 512, F32) for i in range(2)]; o += 1024
        mixraw = [[V(b_ph, par * 4480 + j * 512, 512, F32) for j in range(8)] for par in range(2)]
        cacc = [V(b_ph, o + i * 512, 512, F32) for i in range(1)]; o += 512
        assert o >= 8576
        cubuf = [V(b_ph, o + j * WH, WH, F32) for j in range(4)]
        sqO = [V(b_ph, o + j * 256, 512, BF16) for j in range(8)]
        o += 4 * WH
        Tk = [V(b_ph, o + k * WH, WH, F32) for k in range(3)]; o += 3 * WH
        Tfin = [V(b_ph, o + i * 512, 512, F32) for i in range(2)]; o += 1024
        yrot = [V(b_ph, o + i * 512, 512, F32) for i in range(4)]; o += 2048
        sqh = [V(b_ph, o + i * 256, 512, BF16) for i in range(4)]; o += 1024
        cat = [V(b_ph, o + kt * (WMAX // 2), WMAX, BF16) for kt in range(KT)]; o += KT * WMAX // 2
        assert o <= PH_N, o
        o = 0
        upraw = [V(b_ph, o + s * WH, WH, F32) for s in range(6)]; o += 6 * WH
        facc = [V(b_ph, o + i * 1024, 1024, F32) for i in range(4)]; o += 4096
        facc_s = [V(b_ph, o + i * 64, 64, F32) for i in range(4)]; o += 256
        fraw = [V(b_ph, j * WH + H, WMAX, F32) for j in range(8)]
        o = max(o, 8 * WMAX)
        actv = [V(b_ph, o + i * (WMAX // 2), WMAX, BF16) for i in range(NPAIR)]; o += NPAIR * WMAX // 2
        fsq = [V(b_ph, o + i * 256, 512, BF16) for i in range(3)]; o += 768
        fstage = [V(b_ph, 8960 + i * 512, 512, F32) for i in range(3)]
        assert 8960 + 3 * 512 <= 6 * WH + 4096 and 8960 >= 7 * WH + H + WMAX
        fstage += [V(b_ph, o + i * 512, 512, F32) for i in range(2)]; o += 1024
        assert o <= PH_N, o

        def vcol(l, off):
            c = l * NV_L + off
            return vec.ap(c, c + 1), vec.r(c, c + 1)

        def mm(out, lhsT, rhs, start, stop):
            (ov, olo, ohi), (lv, llo, lhi), (rv, rlo, rhi) = out, lhsT, rhs
            oa, la, ra = ov.ap(olo, ohi), lv.ap(llo, lhi), rv.ap(rlo, rhi)
            P.op(PE, lambda h: h.matmul(oa, la, ra, start=start, stop=stop),
                 reads=[lv.r(llo, lhi), rv.r(rlo, rhi)], writes=[ov.r(olo, ohi)])

        def act(out, in_, func, scale=None, bias=None):
            (ov, olo, ohi), (iv, ilo, ihi) = out, in_
            oa, ia = ov.ap(olo, ohi), iv.ap(ilo, ihi)
            reads = [iv.r(ilo, ihi)]
            kw = {}
            if scale is not None:
                if isinstance(scale, tuple):
                    kw["scale"] = scale[0]; reads.append(scale[1])
                else:
                    kw["scale"] = scale
            if bias is not None:
                kw["bias"] = bias[0]; reads.append(bias[1])
            P.op(ACT, lambda h: h.activation(out=oa, in_=ia, func=func, **kw), reads=reads, writes=[ov.r(olo, ohi)])

        def tt(eng, out, in0, in1, op):
            (ov, olo, ohi), (av, alo, ahi), (bv, blo, bhi) = out, in0, in1
            oa, aa, ba = ov.ap(olo, ohi), av.ap(alo, ahi), bv.ap(blo, bhi)
            P.op(eng, lambda h: h.tensor_tensor(oa, aa, ba, op),
                 reads=[av.r(alo, ahi), bv.r(blo, bhi)], writes=[ov.r(olo, ohi)])

        def stt(eng, out, in0, scalar, in1, op0, op1):
            (ov, olo, ohi), (av, alo, ahi), (bv, blo, bhi) = out, in0, in1
            oa, aa, ba = ov.ap(olo, ohi), av.ap(alo, ahi), bv.ap(blo, bhi)
            reads = [av.r(alo, ahi), bv.r(blo, bhi)]
            if isinstance(scalar, tuple):
                sc = scalar[0]; reads.append(scalar[1])
            else:
                sc = scalar
            P.op(eng, lambda h: h.scalar_tensor_tensor(oa, aa, sc, ba, op0, op1), reads=reads, writes=[ov.r(olo, ohi)])

        def recip(out, in_):
            (ov, olo, ohi), (iv, ilo, ihi) = out, in_
            oa, ia = ov.ap(olo, ohi), iv.ap(ilo, ihi)
            P.op(DVE, lambda h: h.reciprocal(oa, ia), reads=[iv.r(ilo, ihi)], writes=[ov.r(olo, ohi)])

        def copy(eng, out, in_):
            (ov, olo, ohi), (iv, ilo, ihi) = out, in_
            oa, ia = ov.ap(olo, ohi), iv.ap(ilo, ihi)
            P.op(eng, lambda h: h.tensor_copy(oa, ia), reads=[iv.r(ilo, ihi)], writes=[ov.r(olo, ohi)])

        def memset(eng, out, val):
            (ov, olo, ohi) = out
            oa = ov.ap(olo, ohi)
            P.op(eng, lambda h: h.memset(oa, val), writes=[ov.r(olo, ohi)])

        def dma(eng, out_ap, in_ap, reads=(), writes=()):
            P.op(eng, lambda h: h.dma_start(out=out_ap, in_=in_ap), reads=list(reads), writes=list(writes), dma=True)

        bank_ctr = [0, 0]

        def mm_bank(limit=6):
            k = bank_ctr[0] % limit
            bank_ctr[0] += 1
            return banks[k]

        def st_bank():
            k = 6 + bank_ctr[1] % 2
            bank_ctr[1] += 1
            return banks[k]

        rt_ctr = [0]

        def next_rt():
            r_ = rtv[rt_ctr[0] % 3]
            rt_ctr[0] += 1
            return r_

        memset(POOL, (ones1024, 0, 128), 1.0 / 1024)
        memset(POOL, (ones128, 0, 128), 1.0 / 128)
        memset(POOL, (onesblk, 0, 128), 0.0)
        oa = onesblk.base[0:64, 0:64]
        P.op(POOL, lambda h: h.memset(oa, 1.0 / 64), writes=[onesblk.r(0, 128)])
        ob = onesblk.base[64:128, 64:128]
        P.op(POOL, lambda h: h.memset(ob, 1.0 / 64), writes=[onesblk.r(0, 128)])
        memset(POOL, (epsv, 0, 1), EPS)
        for g in range(4):
            w = 2 << g
            memset(POOL, (invc[g], 0, 16), 1.0)
            for t in range(min(w - 1, 16)):
                memset(POOL, (invc[g], t, t + 1), float(w) / (t + 1))
        dma(SP, vec.ap(0, NV), vec_d, writes=[vec.r(0, NV)])
        dma(SP, b_stp.t[:, :], stp_d, writes=[(b_stp, 0, b_stp.width)])
        dma(SP, b_stc.t[:, :], stc_d, writes=[(b_stc, 0, b_stc.width)])
        dma(SP, b_stf.t[:, :], stf_d, writes=[(b_stf, 0, b_stf.width)])

        wseq = []
        for blk in range(2):
            for l in range(L):
                for ui in range(NU):
                    wseq.append((blk, l, ui))
        wstate = {"issued": 0, "used": 0, "live": set(), "next_ring": 0}
        ring_idx = []
        rc = 0
        for (_, l_, ui_) in wseq:
            if tab[ui_][0] == "pm":
                ring_idx.append(None)
            else:
                ring_idx.append(rc)
                rc += 1

        def issue_weights():
            while wstate["issued"] < len(wseq):
                q = wstate["issued"]
                (_, l, ui) = wseq[q]
                n = tab[ui][2]
                if ring_idx[q] is None:
                    if q > wstate["used"] + 4:
                        break
                    sv = pmbuf
                else:
                    r = ring_idx[q]
                    done_upto = min(wstate["live"]) if wstate["live"] else wstate["next_ring"]
                    if r - NSLOT >= done_upto:
                        break
                    sv = wslot[r % NSLOT]
                dma(POOL, sv.ap(0, n), wts_d[l, :, offs[ui]:offs[ui] + n], writes=[sv.r(0, n)])
                wstate["issued"] += 1

        def next_unit(kind):
            q = wstate["used"]
            wstate["used"] += 1
            (_, l, ui) = wseq[q]
            assert tab[ui][0] == kind, (tab[ui], kind)
            if ring_idx[q] is None:
                issue_weights()
                assert wstate["issued"] > q
                return pmbuf, None
            wstate["live"].add(ring_idx[q])
            wstate["next_ring"] = ring_idx[q] + 1
            issue_weights()
            assert wstate["issued"] > q
            return wslot[ring_idx[q] % NSLOT], ring_idx[q]

        def release(q):
            wstate["live"].discard(q)
            issue_weights()

        def stats_rstd(src, n, ones_v, nsrc, bank=None):
            pb = bank if bank is not None else st_bank()
            for i, (sv, lo) in enumerate(src):
                mm((pb, 0, n), (ones_v, 0, 128), (sv, lo, lo + n), i == 0, i == nsrc - 1)
            rt = next_rt()
            rstd_from(pb, rt, n)
            return rt

        def rstd_from(pb, rt, n):
            act((rt, 0, n), (pb, 0, n), AF.Ln, bias=(epsv.ap(0, 1), epsv.r(0, 1)))
            act((rt, 0, n), (rt, 0, n), AF.Exp, scale=-0.5)

        def pre_a(a, b):
            n = b - a
            for kt in range(KT):
                act((sqv[kt], 0, n), (xv[kt], a, b), AF.Square)

        def pre_b(l, goff, a, b, bank=None):
            n = b - a
            rt = stats_rstd([(sqv[kt], 0) for kt in range(KT)], n, ones1024, KT, bank=bank)
            for kt in range(KT):
                stt(DVE, (hv[kt], a, b), (xv[kt], a, b), vcol(l, goff + kt), (rt, 0, n), ALU.mult, ALU.mult)

        def pre_norm(l, goff, a, b):
            pre_a(a, b)
            pre_b(l, goff, a, b)

        def post_norm_residual(l, goff, raws, a, b, rt=None):
            n = b - a
            assert rt is not None
            for j in range(8):
                rv_, lo = raws[j]
                tt(DVE, (rv_, lo, lo + n), (rv_, lo, lo + n), (rt, 0, n), ALU.mult)
                tt(POOL if j % 2 == 0 else DVE, (xv[j], a, b), (xv[j], a, b), (rv_, lo, lo + n), ALU.add)

        blocks = [
            [dict(kind="p", t0=0, n=1024, pl=0, hs=0, first=True, last=False)],
            [dict(kind="p", t0=1024, n=1024, pl=0, hs=0, first=False, last=True),
             dict(kind="s", t0=0, n=DSEQ, pl=1024, hs=H + 1024, first=False, last=True)],
        ]

        def ap3(v, ncol_each, stride, cnt, lo, hi):
            full = v.b.t[:, v.off:v.off + stride * cnt].rearrange("p (g w) -> p g w", g=cnt)
            return full[:, :, lo:hi]

        for bi, segs in enumerate(blocks):
            W = sum(s["n"] for s in segs)
            chunks = []
            for si, s in enumerate(segs):
                c0 = 0
                while c0 < s["n"]:
                    n = min(512, s["n"] - c0)
                    chunks.append(dict(a=s["pl"] + c0, b=s["pl"] + c0 + n, n=n, seg=si,
                                       ha=s["hs"] + H + c0, hb=s["hs"] + H + c0 + n,
                                       first=(c0 == 0), last=(c0 + n == s["n"])))
                    c0 += n
            for c in chunks:
                s = segs[c["seg"]]
                t0 = s["t0"] + (c["a"] - s["pl"])
                for kt in range(KT):
                    src = (xp_d[kt, :, t0:t0 + c["n"]] if s["kind"] == "p" else xs_d[kt, :, t0:t0 + c["n"]])
                    dma(SP, xv[kt].ap(c["a"], c["b"]), src, writes=[xv[kt].r(c["a"], c["b"])])

            final_done = set()

            def final_norm(ci, stage, bank=None):
                c = chunks[ci]
                a, b, n = c["a"], c["b"], c["n"]
                s = segs[c["seg"]]
                pre_a(a, b)
                rt = stats_rstd([(sqv[kt], 0) for kt in range(KT)], n, ones1024, KT, bank=bank)
                for kt in range(KT):
                    st_ = stage[kt % len(stage)]
                    gcol = (vec.ap(L * NV_L + kt, L * NV_L + kt + 1), vec.r(L * NV_L + kt, L * NV_L + kt + 1))
                    stt(DVE, (st_, 0, n), (xv[kt], a, b), gcol, (rt, 0, n), ALU.mult, ALU.mult)
                    t0 = s["t0"] + (a - s["pl"])
                    dst = yp_d[kt, :, t0:t0 + n] if s["kind"] == "p" else ys_d[kt, :, t0:t0 + n]
                    dma(SP, dst, st_.ap(0, n), reads=[st_.r(0, n)])
                final_done.add(ci)

            for l in range(L):
                def halo_init(which, l=l):
                    for si, s in enumerate(segs):
                        hs = s["hs"]
                        if s["kind"] == "p" and s["first"]:
                            for g in range(4):
                                if which == "v":
                                    memset(POOL, (vbuf[g], hs, hs + H), 0.0)
                                else:
                                    memset(POOL, (cubuf[g], hs, hs + H), 0.0)
                        else:
                            srcb, srcc = (b_cv, b_cc) if s["kind"] == "p" else (b_stp, b_stc)
                            for g in range(4):
                                if which == "v":
                                    o15 = (l * 4 + g) * 15
                                    sa = srcb.t[:, o15:o15 + 15]
                                    da = vbuf[g].ap(hs + 1, hs + H)
                                    P.op(POOL, lambda h, da=da, sa=sa: h.tensor_copy(da, sa),
                                         reads=[(srcb, o15, o15 + 15)], writes=[vbuf[g].r(hs + 1, hs + H)])
                                else:
                                    o2 = (l * 4 + g) * 2
                                    sa2 = srcc.t[:, o2:o2 + 2]
                                    da2 = cubuf[g].ap(hs + H - 2, hs + H)
                                    P.op(POOL, lambda h, da2=da2, sa2=sa2: h.tensor_copy(da2, sa2),
                                         reads=[(srcc, o2, o2 + 2)], writes=[cubuf[g].r(hs + H - 2, hs + H)])

                def v_tiles(lv, cis, vunits):
                    for g in range(4):
                        ws = vunits[g // 2][0]
                        jj = g % 2
                        for ci in cis:
                            c = chunks[ci]
                            a, b, n, ha, hb = c["a"], c["b"], c["n"], c["ha"], c["hb"]
                            pb = mm_bank()
                            for kt in range(KT):
                                mm((pb, 0, n), (ws, kt * 256 + jj * 128, kt * 256 + jj * 128 + 128), (hv[kt], a, b),
                                   kt == 0, kt == KT - 1)
                            act((vbuf[g], ha, hb), (pb, 0, n), AF.Copy)

                nch = len(chunks)
                if l == 0:
                    halo_init("v")
                    for c in chunks:
                        pre_norm(l, V_GPRE, c["a"], c["b"])
                    vunits = [next_unit("z"), next_unit("z")]
                    v_tiles(l, range(nch), vunits)
                    release(vunits[0][1])
                    release(vunits[1][1])
                halo_init("cu")
                pm_slot, pmq = next_unit("pm")
                def pool_chain(g, c, par):
                    a, b, n, ha, hb = c["a"], c["b"], c["n"], c["ha"], c["hb"]
                    s = segs[c["seg"]]
                    hs = s["hs"]
                    cur = vbuf[g]
                    tf = Tfin[par]
                    for k in range(g + 1):
                        sh = 1 << k
                        if k == g:
                            tt(POOL, (tf, 0, n), (cur, ha, hb), (cur, ha - sh, hb - sh), ALU.add)
                        else:
                            dst = Tk[k]
                            lo = (hs + 2 * sh) if c["first"] else ha
                            tt(POOL, (dst, lo, hb), (cur, lo, hb), (cur, lo - sh, hb - sh), ALU.add)
                            cur = dst
                    if s["kind"] == "p" and s["first"] and c["first"]:
                        tt(POOL, (tf, 0, 16), (tf, 0, 16), (invc[g], 0, 16), ALU.mult)
                    if c is chunks[-1]:
                        for si, s in enumerate(segs):
                            he = s["hs"] + H + s["n"]
                            if s["last"]:
                                slot = 0 if s["kind"] == "p" else 1
                                o15 = ((l * 2 + slot) * 4 + g) * 15
                                dstb = b_op
                            else:
                                o15 = (l * 4 + g) * 15
                                dstb = b_cv
                            da = dstb.t[:, o15:o15 + 15]
                            sa = vbuf[g].ap(he - 15, he)
                            P.op(POOL, lambda h, da=da, sa=sa: h.tensor_copy(da, sa),
                                 reads=[vbuf[g].r(he - 15, he)], writes=[(dstb, o15, o15 + 15)])

                steps = [(j, ci) for j in range(4) for ci in range(len(chunks))]
                pool_chain(steps[0][0], chunks[steps[0][1]], 0)
                deferred = []

                def flush_deferred():
                    for fn in deferred:
                        fn()
                    del deferred[:]

                step = 0
                q_norm = []
                q_ppm = []

                def run_ppm():
                    for fn in q_ppm:
                        fn()
                    del q_ppm[:]

                for j in range(4):
                    ws, wq = next_unit("z")
                    for ci, c in enumerate(chunks):
                        a, b, n, ha, hb = c["a"], c["b"], c["n"], c["ha"], c["hb"]
                        pgc, pu, pgb = mm_bank(), mm_bank(), mm_bank()
                        for ti, pb in enumerate((pgc, pu, pgb)):
                            for kt in range(KT):
                                mm((pb, 0, n), (ws, kt * 384 + ti * 128, kt * 384 + ti * 128 + 128), (hv[kt], a, b),
                                   kt == 0, kt == KT - 1)
                        if step + 1 < len(steps):
                            pool_chain(steps[step + 1][0], chunks[steps[step + 1][1]], (step + 1) % 2)
                        stt(DVE, (pooled[j], a, b), (Tfin[step % 2], 0, n), 1.0 / (2 << j), (vbuf[j], ha, hb),
                            ALU.mult, ALU.subtract)
                        gc_ = gcs[0]
                        gb_ = gbs[step % 2]
                        ca = cacc[0]
                        yc = yrot[(2 * step) % 4]
                        sqc = sqh[(2 * step) % 4]
                        act((gc_, 0, n), (pgc, 0, n), AF.Copy)
                        act((gb_, 0, n), (pgb, 0, n), AF.Copy)
                        tt(DVE, (cubuf[j], ha, hb), (pu, 0, n), (gc_, 0, n), ALU.mult)
                        act((ca, 0, n), (cubuf[j], ha, hb), AF.Copy, scale=vcol(l, V_CW + 4 * 2 + j))
                        prev = list(q_norm)
                        del q_norm[:]
                        run_ppm()
                        for (fn_b, fn_c) in prev:
                            fn_b()
                        stt(DVE, (ca, 0, n), (cubuf[j], ha - 1, hb - 1), vcol(l, V_CW + 4 * 1 + j), (ca, 0, n),
                            ALU.mult, ALU.add)
                        stt(DVE, (ca, 0, n), (cubuf[j], ha - 2, hb - 2), vcol(l, V_CW + 4 * 0 + j), (ca, 0, n),
                            ALU.mult, ALU.add)
                        tt(DVE, (yc, 0, n), (gb_, 0, n), (ca, 0, n), ALU.mult)
                        for (fn_b, fn_c) in prev:
                            fn_c()
                        act((sqc, 0, n), (yc, 0, n), AF.Square)

                        boxc = {}

                        def stage_b_c(n=n, sqc=sqc, box=boxc):
                            box["rt"] = stats_rstd([(sqc, 0)], n, onesblk, 1)

                        def stage_c_c(j=j, a=a, b=b, n=n, yc=yc, box=boxc):
                            stt(DVE, (cat[4 + j], a, b), (yc, 0, n), vcol(l, V_GCONV + j), (box["rt"], 0, n),
                                ALU.mult, ALU.mult)

                        q_norm.append((stage_b_c, stage_c_c))

                        def ppm_stage(j=j, a=a, b=b, n=n, step=step):
                            ppm = mm_bank()
                            mm((ppm, 0, n), (pm_slot, j * 128, j * 128 + 128), (pooled[j], a, b), True, True)
                            yp_ = yrot[(2 * step + 3) % 4]
                            sqp = sqh[(2 * step + 3) % 4]
                            act((yp_, 0, n), (ppm, 0, n), AF.Copy, scale=vcol(l, V_PSCALE + j))
                            act((sqp, 0, n), (ppm, 0, n), AF.Square, scale=vcol(l, V_PSCALE + j))
                            boxp = {}

                            def stage_b_p(n=n, sqp=sqp, box=boxp):
                                box["rt"] = stats_rstd([(sqp, 0)], n, ones128, 1)

                            def stage_c_p(j=j, a=a, b=b, n=n, yp_=yp_, box=boxp):
                                stt(DVE, (cat[j], a, b), (yp_, 0, n), vcol(l, V_GPOOL + j), (box["rt"], 0, n),
                                    ALU.mult, ALU.mult)

                            q_norm.append((stage_b_p, stage_c_p))

                        q_ppm.append(ppm_stage)
                        step += 1
                    release(wq)
                    for si, s in enumerate(segs):
                        he = s["hs"] + H + s["n"]
                        if s["last"]:
                            slot = 0 if s["kind"] == "p" else 1
                            o2 = ((l * 2 + slot) * 4 + j) * 2
                            dstb = b_oc
                        else:
                            o2 = (l * 4 + j) * 2
                            dstb = b_cc
                        da = dstb.t[:, o2:o2 + 2]
                        sa = cubuf[j].ap(he - 2, he)
                        P.op(POOL, lambda h, da=da, sa=sa: h.tensor_copy(da, sa),
                             reads=[cubuf[j].r(he - 2, he)], writes=[(dstb, o2, o2 + 2)])
                prev = list(q_norm)
                del q_norm[:]
                run_ppm()
                for (fn_b, fn_c) in prev:
                    fn_b()
                    fn_c()
                last_def = list(q_norm)
                del q_norm[:]
                release(pmq)
                ou = [next_unit("o") for _ in range(3)]
                oslots = [u_[0] for u_ in ou]
                ocols = (384, 384, 256)

                def o_mm(ci, js=range(8)):
                    c = chunks[ci]
                    a, b, n = c["a"], c["b"], c["n"]
                    mr = mixraw[min(ci, 1)]
                    for j in js:
                        ui, jj = (j // 3, j % 3) if j < 6 else (2, j - 6)
                        ws, ncol = oslots[ui], ocols[ui]
                        pb = mm_bank()
                        for kt in range(KT):
                            mm((pb, 0, n), (ws, kt * ncol + jj * 128, kt * ncol + jj * 128 + 128), (cat[kt], a, b),
                               kt == 0, kt == KT - 1)
                        act((mr[j], 0, n), (pb, 0, n), AF.Copy, scale=vcol(l, V_GPOST + j))
                        act((sqO[j], 0, n), (pb, 0, n), AF.Square)

                def o_post(ci):
                    c = chunks[ci]
                    a, b, n = c["a"], c["b"], c["n"]
                    mr = mixraw[min(ci, 1)]
                    rt = stats_rstd([(sqO[j], 0) for j in range(8)], n, ones1024, 8)
                    post_norm_residual(l, V_GPOST, [(mr[j], 0) for j in range(8)], a, b, rt=rt)

                nch = len(chunks)
                if nch > 2:
                    for (fn_b, fn_c) in last_def:
                        fn_b()
                        fn_c()
                    o_mm(0)
                else:
                    o_mm(0, range(0, 4))
                    for (fn_b, fn_c) in last_def:
                        fn_b()
                        fn_c()
                    o_mm(0, range(4, 8))
                o_post(0)
                for k in range(1, nch):
                    o_mm(k, range(0, 5))
                    pre_a(chunks[k - 1]["a"], chunks[k - 1]["b"])
                    o_mm(k, range(5, 8))
                    if k == nch - 1:
                        for u_ in ou:
                            release(u_[1])
                    pre_b(l, V_GPREF, chunks[k - 1]["a"], chunks[k - 1]["b"])
                    o_post(k)
                NLEAD = 3

                def u_halo(i):
                    for t_ in range(2):
                        jt = i + NPAIR * t_
                        us = upraw[(2 * i + t_) % 6]
                        for si, s in enumerate(segs):
                            hs = s["hs"]
                            if s["kind"] == "p" and s["first"]:
                                memset(POOL, (us, hs + H - 2, hs + H), 0.0)
                            else:
                                srcb = b_cf if s["kind"] == "p" else b_stf
                                o2 = (l * 44 + jt) * 2
                                sa = srcb.t[:, o2:o2 + 2]
                                da = us.ap(hs + H - 2, hs + H)
                                P.op(POOL, lambda h, da=da, sa=sa: h.tensor_copy(da, sa),
                                     reads=[(srcb, o2, o2 + 2)], writes=[us.r(hs + H - 2, hs + H)])

                def u_mm(i, ws, ci, small_bank=None):
                    c = chunks[ci]
                    a, b, n = c["a"], c["b"], c["n"]
                    pbs = []
                    for t_ in range(2):
                        if small_bank is not None:
                            pb, c0_ = small_bank, 64 * t_
                        else:
                            pb, c0_ = mm_bank(), 0
                        for kt in range(KT):
                            mm((pb, c0_, c0_ + n), (ws, kt * 256 + t_ * 128, kt * 256 + t_ * 128 + 128), (hv[kt], a, b),
                               kt == 0, kt == KT - 1)
                        pbs.append((pb, c0_))
                    return pbs

                def u_ew(i, ci, pbs, defer_b=None):
                    c = chunks[ci]
                    a, b, n, ha, hb = c["a"], c["b"], c["n"], c["ha"], c["hb"]
                    accs = []
                    for t_ in range(2):
                        jt = i + NPAIR * t_
                        us = upraw[(2 * i + t_) % 6]
                        pb, c0_ = pbs[t_]
                        if n <= 64:
                            fa = facc_s[fctr[1] % 4]
                            fctr[1] += 1
                        else:
                            fa = facc[fctr[0] % 4]
                            fctr[0] += 1
                        act((us, ha, hb), (pb, c0_, c0_ + n), AF.Copy)
                        act((fa, 0, n), (pb, c0_, c0_ + n), AF.Copy, scale=vcol(l, V_FW + 44 * 2 + jt))
                        stt(DVE, (fa, 0, n), (us, ha - 1, hb - 1), vcol(l, V_FW + 44 * 1 + jt), (fa, 0, n),
                            ALU.mult, ALU.add)
                        stt(DVE, (fa, 0, n), (us, ha - 2, hb - 2), vcol(l, V_FW + 44 * 0 + jt), (fa, 0, n),
                            ALU.mult, ALU.add)
                        accs.append(fa)

                    def part_b(i=i, a=a, b=b, n=n, accs=accs):
                        act((accs[0], 0, n), (accs[0], 0, n), AF.Silu)
                        tt(DVE, (actv[i], a, b), (accs[0], 0, n), (accs[1], 0, n), ALU.mult)

                    if defer_b is None:
                        part_b()
                    else:
                        defer_b.append(part_b)

                def u_step(i, ws, ci):
                    u_ew(i, ci, u_mm(i, ws, ci))

                def u_step2(i, ws):
                    c0, c1 = chunks[0], chunks[1]
                    a, b = c0["a"], c1["b"]
                    ha, hb = c0["ha"], c1["hb"]
                    assert c0["b"] == c1["a"] and c0["hb"] == c1["ha"] and b - a == 1024
                    accs = []
                    for t_ in range(2):
                        jt = i + NPAIR * t_
                        us = upraw[(2 * i + t_) % 6]
                        k2 = bp_ctr[0] % 3
                        bp_ctr[0] += 1
                        for ci, c in enumerate((c0, c1)):
                            pb = banks[2 * k2 + ci]
                            for kt in range(KT):
                                mm((pb, 0, 512), (ws, kt * 256 + t_ * 128, kt * 256 + t_ * 128 + 128),
                                   (hv[kt], c["a"], c["b"]), kt == 0, kt == KT - 1)
                        pp = bankpair[k2]
                        fa = facc[fctr[0] % 4]
                        fctr[0] += 1
                        act((us, ha, hb), (pp, 0, 1024), AF.Copy)
                        act((fa, 0, 1024), (pp, 0, 1024), AF.Copy, scale=vcol(l, V_FW + 44 * 2 + jt))
                        stt(DVE, (fa, 0, 1024), (us, ha - 1, hb - 1), vcol(l, V_FW + 44 * 1 + jt), (fa, 0, 1024),
                            ALU.mult, ALU.add)
                        stt(DVE, (fa, 0, 1024), (us, ha - 2, hb - 2), vcol(l, V_FW + 44 * 0 + jt), (fa, 0, 1024),
                            ALU.mult, ALU.add)
                        accs.append(fa)

                    def part_b(i=i, a=a, b=b, accs=accs):
                        act((accs[0], 0, 1024), (accs[0], 0, 1024), AF.Silu)
                        tt(DVE, (actv[i], a, b), (accs[0], 0, 1024), (accs[1], 0, 1024), ALU.mult)

                    return part_b

                def u_tails(i):
                    for t_ in range(2):
                        jt = i + NPAIR * t_
                        us = upraw[(2 * i + t_) % 6]
                        for si, s in enumerate(segs):
                            he = s["hs"] + H + s["n"]
                            if s["last"]:
                                slot = 0 if s["kind"] == "p" else 1
                                o2 = ((l * 2 + slot) * 44 + jt) * 2
                                dstb = b_of
                            else:
                                o2 = (l * 44 + jt) * 2
                                dstb = b_cf
                            da = dstb.t[:, o2:o2 + 2]
                            sa = us.ap(he - 2, he)
                            P.op(POOL, lambda h, da=da, sa=sa: h.tensor_copy(da, sa),
                                 reads=[us.r(he - 2, he)], writes=[(dstb, o2, o2 + 2)])

                fctr = [0, 0]
                lead = []
                pend = []
                for i in range(NLEAD):
                    ws, wq = next_unit("u")
                    lead.append((ws, wq))
                    u_halo(i)
                    for ci in range(nch - 1):
                        nleft = (NLEAD - 1 - i) * (nch - 1) + (nch - 2 - ci)
                        if nleft >= 2:
                            u_step(i, ws, ci)
                        else:
                            pend.append((i, ci, u_mm(i, ws, ci)))
                    if i == 1:
                        pre_a(chunks[nch - 1]["a"], chunks[nch - 1]["b"])
                pre_b(l, V_GPREF, chunks[nch - 1]["a"], chunks[nch - 1]["b"])
                for (i, ci, pbs) in pend:
                    u_ew(i, ci, pbs)
                for i in range(NLEAD):
                    ws, wq = lead[i]
                    u_step(i, ws, nch - 1)
                    release(wq)
                    u_tails(i)
                u_halo(NLEAD)
                small_b = []
                for i in range(NLEAD, NPAIR):
                    ws, wq = next_unit("u")
                    if i + 1 < NPAIR:
                        u_halo(i + 1)
                    pb2 = u_step2(i, ws)
                    prev_small = list(small_b)
                    del small_b[:]
                    for ci in range(2, nch):
                        sb_ = banks[6 + (bp_ctr[1] % 2)]
                        bp_ctr[1] += 1
                        u_ew(i, ci, u_mm(i, ws, ci, small_bank=sb_), defer_b=small_b)
                    pb2()
                    for fn in prev_small:
                        fn()
                    release(wq)
                    u_tails(i)
                for fn in small_b:
                    fn()
                del small_b[:]
                def d_step(j, ws, ci):
                    c = chunks[ci]
                    a, b, n = c["a"], c["b"], c["n"]
                    pb = mm_bank(5)
                    for kt in range(NPAIR):
                        mm((pb, 0, n), (ws, kt * 128, kt * 128 + 128), (actv[kt], a, b), kt == 0, kt == NPAIR - 1)
                    act((fraw[j], a, b), (pb, 0, n), AF.Copy, scale=vcol(l, V_GPOSTF + j))
                    fq = fsq[fctr[2] % 3]
                    fctr[2] += 1
                    act((fq, 0, n), (pb, 0, n), AF.Square)
                    while len(deferred) > (nch - 2):
                        deferred.pop(0)()
                    deferred.append(lambda ci=ci, n=n, fq=fq, j=j: mm((banks[5 + ci], 0, n), (ones1024, 0, 128),
                                                                     (fq, 0, n), j == 0, j == 7))

                def post_fin(ci):
                    c = chunks[ci]
                    a, b, n = c["a"], c["b"], c["n"]
                    rt = next_rt()
                    rstd_from(banks[5 + ci], rt, n)
                    post_norm_residual(l, V_GPOSTF, [(fraw[j], a) for j in range(8)], a, b, rt=rt)

                fctr.append(0)
                KL = 3
                for j in range(8 - KL):
                    ws, wq = next_unit("d")
                    for ci in range(nch):
                        d_step(j, ws, ci)
                    release(wq)
                if KL:
                    leadd = []
                    for j in range(8 - KL, 8):
                        ws, wq = next_unit("d")
                        leadd.append((j, ws, wq))
                        d_step(j, ws, 0)
                    bsteps = [(j, ws, ci) for (j, ws, wq) in leadd for ci in range(1, nch)]
                    d_step(*bsteps[0])
                    flush_deferred()
                    post_fin(0)
                    for st in bsteps[1:]:
                        d_step(*st)
                    for (j, ws, wq) in leadd:
                        release(wq)
                    flush_deferred()
                    if l == L - 1:
                        final_norm(0, fstage, bank=banks[5])
                        for ci in range(1, nch):
                            post_fin(ci)
                        continue
                    pre_a(chunks[0]["a"], chunks[0]["b"])
                    pre_b(l + 1, V_GPRE, chunks[0]["a"], chunks[0]["b"], bank=banks[5])
                    for ci in range(1, nch):
                        post_fin(ci)
                    halo_init("v", l + 1)
                    vunits = [next_unit("z"), next_unit("z")]
                    v_tiles(l + 1, [0], vunits)
                    for ci in range(1, nch):
                        pre_a(chunks[ci]["a"], chunks[ci]["b"])
                        pre_b(l + 1, V_GPRE, chunks[ci]["a"], chunks[ci]["b"])
                    v_tiles(l + 1, range(1, nch), vunits)
                    release(vunits[0][1])
                    release(vunits[1][1])
                else:
                    flush_deferred()
                    for ci in range(nch):
                        post_fin(ci)
            for ci in range(len(chunks)):
                if ci not in final_done:
                    final_norm(ci, mixraw[ci % 2])
        dma(SP, opool_d, b_op.t[:, :], reads=[(b_op, 0, b_op.width)])
        dma(SP, oconv_d, b_oc.t[:, :], reads=[(b_oc, 0, b_oc.width)])
        dma(SP, offn_d, b_of.t[:, :], reads=[(b_of, 0, b_of.width)])
        assert wstate["used"] == len(wseq), (wstate, len(wseq))

        run = P.emit(sems)
        with nc.Block() as block:
            @block.sync
            def _(h):
                run(SP, h)

            @block.scalar
            def _(h):
                run(ACT, h)

            @block.vector
            def _(h):
                run(DVE, h)

            @block.gpsimd
            def _(h):
                run(POOL, h)

            @block.tensor
            def _(h):
                run(PE, h)
    return nc


def kernel(x_prompt, x_sample, state_pool, state_conv, state_ffn_conv, w_in, pool_mix, pool_scale, conv_w,
           g_pool_out, g_conv_out, w_out, g_pre_mix, g_post_mix, g_pre_ffn, g_post_ffn, w_up, ffn_conv_w,
           w_down, g_final):
    f = lambda a: np.asarray(a, dtype=np.float32)
    x_prompt, x_sample, state_pool, state_conv, state_ffn_conv = map(f, (x_prompt, x_sample, state_pool,
                                                                         state_conv, state_ffn_conv))
    wts = pack_weights(f(w_in), f(pool_mix), f(w_out), f(w_up), f(w_down))
    vecs = pack_vecs(f(pool_scale), f(conv_w), f(g_pool_out), f(g_conv_out), f(g_pre_mix), f(g_post_mix),
                     f(g_pre_ffn), f(g_post_ffn), f(ffn_conv_w), f(g_final))
    in_maps = []
    for b in range(NCORES):
        xp = np.ascontiguousarray(x_prompt[b].T).reshape(KT, 128, SEQ)
        xs = np.ascontiguousarray(x_sample[b].T).reshape(KT, 128, DSEQ)
        stp = np.ascontiguousarray(state_pool[:, b].reshape(L, 15, 4, 128).transpose(3, 0, 2, 1)).reshape(128, -1)
        stc = np.ascontiguousarray(state_conv[:, b].reshape(L, 2, 4, 128).transpose(3, 0, 2, 1)).reshape(128, -1)
        stf = np.ascontiguousarray(state_ffn_conv[:, b].reshape(L, 2, 44, 128).transpose(3, 0, 2, 1)).reshape(128, -1)
        in_maps.append({"xp": xp, "xs": xs, "stp": stp, "stc": stc, "stf": stf, "vecs": vecs, "wts": wts})
    nc = build_nc()
    res = run_bass_kernel_spmd(nc, in_maps, core_ids=list(range(NCORES)))
    B = NCORES
    y_p = np.empty((B, SEQ, D), np.float32)
    y_s = np.empty((B, DSEQ, D), np.float32)
    npool = [np.empty((L, B, 15, 512), np.float32) for _ in range(2)]
    nconv = [np.empty((L, B, 2, 512), np.float32) for _ in range(2)]
    nffn = [np.empty((L, B, 2, 2 * DFF), np.float32) for _ in range(2)]
    for b in range(B):
        r = res.results[b]
        y_p[b] = np.asarray(r["yp"]).reshape(D, SEQ).T
        y_s[b] = np.asarray(r["ys"]).reshape(D, DSEQ).T
        op = np.asarray(r["opool"]).reshape(128, L, 2, 4, 15)
        oc = np.asarray(r["oconv"]).reshape(128, L, 2, 4, 2)
        of = np.asarray(r["offn"]).reshape(128, L, 2, 44, 2)
        for s in range(2):
            npool[s][:, b] = op[:, :, s].transpose(1, 3, 2, 0).reshape(L, 15, 512)
            nconv[s][:, b] = oc[:, :, s].transpose(1, 3, 2, 0).reshape(L, 2, 512)
            nffn[s][:, b] = of[:, :, s].transpose(1, 3, 2, 0).reshape(L, 2, 2 * DFF)
    return (y_p, y_s, npool[0], nconv[0], nffn[0], npool[1], nconv[1], nffn[1])
```

```python
import numpy as np
from contextlib import ExitStack
import concourse.bass as bass
import concourse.mybir as mybir
from concourse.bass_utils import run_bass_kernel_spmd

F32 = mybir.dt.float32
BF16 = mybir.dt.bfloat16
ALU = mybir.AluOpType
AF = mybir.ActivationFunctionType

PE, ACT, DVE, POOL, SP = "pe", "act", "dve", "pool", "sp"
COMPUTE = (PE, ACT, DVE, POOL)

NCORES = 8
L = 4
D = 1024
KT = 8
DFF = 2816
NPAIR = 22
SEQ = 2048
DSEQ = 64
EPS = 1e-6
H = 16
NSLOT = 5
SLOT_E = 3072
PREFETCH = 2
NV_L = 188
NV = L * NV_L + 8
V_GPRE, V_GPOST, V_GPREF, V_GPOSTF, V_PSCALE, V_GPOOL, V_GCONV, V_CW, V_FW = 0, 8, 16, 24, 32, 36, 40, 44, 56


class Buf:
    def __init__(self, name, t, width):
        self.name = name
        self.t = t
        self.width = width
        self.recs = []


class Op:
    __slots__ = ("eng", "fn", "deps", "signal", "count", "is_dma", "dma_sem", "dma_val", "dma_prev")

    def __init__(self, eng, fn, is_dma):
        self.eng = eng
        self.fn = fn
        self.deps = []
        self.signal = False
        self.count = 0
        self.is_dma = is_dma
        self.dma_sem = None
        self.dma_val = 0
        self.dma_prev = None


class Prog:
    def __init__(self, dma_ring=8):
        self.streams = {e: [] for e in (PE, ACT, DVE, POOL, SP)}
        self.dma_ring = dma_ring
        self.dma_count = {e: 0 for e in (ACT, POOL, SP)}
        self.dma_hist = {e: [] for e in (ACT, POOL, SP)}

    def op(self, eng, fn, reads=(), writes=(), dma=False):
        o = Op(eng, fn, dma)
        deps = {}
        for (b, lo, hi) in reads:
            assert 0 <= lo < hi <= b.width, (b.name, lo, hi, b.width)
            for (l2, h2, k2, o2) in b.recs:
                if k2 == "w" and l2 < hi and lo < h2:
                    deps[id(o2)] = (o2, True)
        for (b, lo, hi) in writes:
            assert 0 <= lo < hi <= b.width, (b.name, lo, hi, b.width)
            for (l2, h2, k2, o2) in b.recs:
                if l2 < hi and lo < h2 and id(o2) not in deps:
                    deps[id(o2)] = (o2, False)
        for (b, lo, hi) in writes:
            b.recs = [r for r in b.recs if not (lo <= r[0] and r[1] <= hi)]
            b.recs.append((lo, hi, "w", o))
        for (b, lo, hi) in reads:
            if not dma:
                b.recs = [r for r in b.recs
                          if not (r[2] == "r" and r[3].eng == eng and not r[3].is_dma
                                  and lo <= r[0] and r[1] <= hi)]
            b.recs.append((lo, hi, "r", o))
        for (o2, israw) in deps.values():
            if o2 is o:
                continue
            if (not o2.is_dma) and (not dma) and o2.eng == eng:
                if eng == PE:
                    continue
            o.deps.append(o2)
            if not o2.is_dma:
                o2.signal = True
        if dma:
            k = self.dma_count[eng]
            self.dma_count[eng] += 1
            o.dma_val = 16 * (k // self.dma_ring + 1)
            o.dma_sem = k % self.dma_ring
            hist = self.dma_hist[eng]
            if k >= self.dma_ring:
                o.dma_prev = hist[k - self.dma_ring]
            hist.append(o)
        self.streams[eng].append(o)
        return o

    def emit(self, sems):
        tl = sems["tl"]
        dsem = sems["dma"]
        for e in COMPUTE:
            c = 0
            for o in self.streams[e]:
                if not o.is_dma and o.signal:
                    c += 1
                    o.count = c

        def run_stream(e, h):
            waited = {}
            for o in self.streams[e]:
                waits = {}
                deps = list(o.deps)
                if o.is_dma and o.dma_prev is not None:
                    deps.append(o.dma_prev)
                for d in deps:
                    if d.is_dma:
                        key = ("dma", d.eng, d.dma_sem)
                        val = d.dma_val
                        sem = dsem[d.eng][d.dma_sem]
                    else:
                        key = ("tl", d.eng)
                        val = d.count
                        sem = tl[d.eng]
                        assert val > 0
                    if waited.get(key, 0) >= val:
                        continue
                    if key not in waits or waits[key][1] < val:
                        waits[key] = (sem, val)
                for key, (sem, val) in waits.items():
                    h.wait_ge(sem, val)
                    waited[key] = val
                ins = o.fn(h)
                if o.is_dma:
                    ins.then_inc(dsem[o.eng][o.dma_sem], 16)
                elif o.signal:
                    ins.then_inc(tl[o.eng], 1)
            if e in self.dma_hist:
                last = {}
                for d in self.dma_hist[e]:
                    last[d.dma_sem] = d
                for d in last.values():
                    h.wait_ge(dsem[e][d.dma_sem], d.dma_val)

        return run_stream


class V:
    def __init__(self, b, off32, n, dt):
        self.b = b
        self.off = off32
        self.n = n
        self.dt = dt
        if dt == F32:
            self.base = b.t[:, off32:off32 + n]
        else:
            assert n % 2 == 0
            self.base = b.t[:, off32:off32 + n // 2].bitcast(BF16)

    def ap(self, lo, hi):
        assert 0 <= lo < hi <= self.n, (self.b.name, lo, hi, self.n)
        return self.base[:, lo:hi]

    def r(self, lo, hi):
        assert 0 <= lo < hi <= self.n, (self.b.name, lo, hi, self.n)
        if self.dt == F32:
            return (self.b, self.off + lo, self.off + hi)
        return (self.b, self.off + lo // 2, self.off + (hi + 1) // 2)

    def sub(self, lo, n):
        if self.dt == F32:
            return V(self.b, self.off + lo, n, F32)
        assert lo % 2 == 0
        return V(self.b, self.off + lo // 2, n, BF16)


def _fm(vec):
    return np.ascontiguousarray(vec.reshape(-1, 128).T)


def _unit(w, cols):
    k = w.shape[0] // 128
    u = w[:, cols].reshape(k, 128, len(cols)).transpose(1, 0, 2)
    return u.reshape(128, k * len(cols))


def _zin_cols():
    units = [np.arange(0, 256), np.arange(256, 512)]
    for j in range(4):
        units.append(np.concatenate([np.arange(1024 + 128 * j, 1024 + 128 * j + 128),
                                     np.arange(1536 + 128 * j, 1536 + 128 * j + 128),
                                     np.arange(512 + 128 * j, 512 + 128 * j + 128)]))
    return units


def unit_table():
    tab = []
    zc = _zin_cols()
    for i in (0, 1):
        tab.append(("z", i, 8 * len(zc[i])))
    tab.append(("pm", 0, 512))
    for i in (2, 3, 4, 5):
        tab.append(("z", i, 8 * len(zc[i])))
    for i, nc_ in enumerate((384, 384, 256)):
        tab.append(("o", i, 8 * nc_))
    for i in range(NPAIR):
        tab.append(("u", i, 8 * 256))
    for j in range(8):
        tab.append(("d", j, NPAIR * 128))
    offs = []
    o = 0
    for t in tab:
        offs.append(o)
        o += t[2]
    return tab, offs, o


def pack_weights(w_in, pool_mix, w_out, w_up, w_down):
    tab, offs, tot = unit_table()
    out = np.empty((L, 128, tot), np.float32)
    zc = _zin_cols()
    oc = [np.arange(0, 384), np.arange(384, 768), np.arange(768, 1024)]
    for l in range(L):
        for (kind, i, n), o in zip(tab, offs):
            if kind == "z":
                a = _unit(w_in[l], zc[i])
            elif kind == "pm":
                a = pool_mix[l].transpose(1, 0, 2).reshape(128, 512)
            elif kind == "o":
                a = _unit(w_out[l], oc[i])
            elif kind == "u":
                a = _unit(w_up[l], np.concatenate([np.arange(128 * i, 128 * i + 128),
                                                   np.arange(DFF + 128 * i, DFF + 128 * i + 128)]))
            else:
                a = _unit(w_down[l], np.arange(128 * i, 128 * i + 128))
            out[l, :, o:o + n] = a
    return out


def pack_vecs(pool_scale, conv_w, g_pool_out, g_conv_out, g_pre_mix, g_post_mix, g_pre_ffn, g_post_ffn,
              ffn_conv_w, g_final):
    v = np.zeros((128, NV), np.float32)
    for l in range(L):
        b = l * NV_L
        v[:, b + V_GPRE:b + V_GPRE + 8] = _fm(g_pre_mix[l])
        v[:, b + V_GPOST:b + V_GPOST + 8] = _fm(g_post_mix[l])
        v[:, b + V_GPREF:b + V_GPREF + 8] = _fm(g_pre_ffn[l])
        v[:, b + V_GPOSTF:b + V_GPOSTF + 8] = _fm(g_post_ffn[l])
        v[:, b + V_PSCALE:b + V_PSCALE + 4] = _fm(pool_scale[l])
        v[:, b + V_GPOOL:b + V_GPOOL + 4] = _fm(g_pool_out[l])
        v[:, b + V_GCONV:b + V_GCONV + 4] = _fm(g_conv_out[l])
        for k in range(3):
            v[:, b + V_CW + 4 * k:b + V_CW + 4 * k + 4] = _fm(conv_w[l, k])
            v[:, b + V_FW + 44 * k:b + V_FW + 44 * k + 44] = _fm(ffn_conv_w[l, k])
    v[:, L * NV_L:L * NV_L + 8] = _fm(g_final)
    return v


def build_nc():
    nc = bass.Bass("TRN2", target_bir_lowering=False)
    tab, offs, WTOT = unit_table()
    NU = len(tab)
    xp_d = nc.dram_tensor("xp", [KT, 128, SEQ], F32, kind="ExternalInput").ap()
    xs_d = nc.dram_tensor("xs", [KT, 128, DSEQ], F32, kind="ExternalInput").ap()
    stp_d = nc.dram_tensor("stp", [128, L * 4 * 15], F32, kind="ExternalInput").ap()
    stc_d = nc.dram_tensor("stc", [128, L * 4 * 2], F32, kind="ExternalInput").ap()
    stf_d = nc.dram_tensor("stf", [128, L * 44 * 2], F32, kind="ExternalInput").ap()
    vec_d = nc.dram_tensor("vecs", [128, NV], F32, kind="ExternalInput").ap()
    wts_d = nc.dram_tensor("wts", [L, 128, WTOT], F32, kind="ExternalInput").ap()
    yp_d = nc.dram_tensor("yp", [KT, 128, SEQ], F32, kind="ExternalOutput").ap()
    ys_d = nc.dram_tensor("ys", [KT, 128, DSEQ], F32, kind="ExternalOutput").ap()
    opool_d = nc.dram_tensor("opool", [128, L * 2 * 4 * 15], F32, kind="ExternalOutput").ap()
    oconv_d = nc.dram_tensor("oconv", [128, L * 2 * 4 * 2], F32, kind="ExternalOutput").ap()
    offn_d = nc.dram_tensor("offn", [128, L * 2 * 44 * 2], F32, kind="ExternalOutput").ap()

    WMAX = 1088
    WH = WMAX + 2 * H
    es = ExitStack()
    with es:
        def sbuf(name, n32):
            t = es.enter_context(nc.sbuf_tensor("sb_" + name, [128, n32], F32))
            return Buf(name, t, n32)

        b_x = sbuf("x", KT * WMAX)
        b_h = sbuf("h", KT * WMAX // 2)
        b_sq = sbuf("sq", KT * 512 // 2)
        b_rt = sbuf("rt", 3 * 512)
        b_w = sbuf("w", NSLOT * SLOT_E // 2)
        b_vec = sbuf("vec", NV)
        b_stp = sbuf("stp", L * 4 * 15)
        b_stc = sbuf("stc", L * 4 * 2)
        b_stf = sbuf("stf", L * 44 * 2)
        b_cv = sbuf("cv", L * 4 * 15)
        b_cc = sbuf("cc", L * 4 * 2)
        b_cf = sbuf("cf", L * 44 * 2)
        b_op = sbuf("op", L * 2 * 4 * 15)
        b_oc = sbuf("oc", L * 2 * 4 * 2)
        b_of = sbuf("of", L * 2 * 44 * 2)
        b_cst = sbuf("cst", 3 * 64 + 1 + 64 + 3)
        b_pm = sbuf("pm", 256)
        PH_N = 25200
        PH_N = 25008
        b_ph = sbuf("ph", PH_N)
        pst = es.enter_context(nc.psum_tensor("psum_all", [128, 8 * 512], F32))
        b_ps = Buf("ps", pst, 8 * 512)

        sems = {"tl": {e: es.enter_context(nc.semaphore("tl_" + e)) for e in COMPUTE},
                "dma": {e: [es.enter_context(nc.semaphore("d_%s_%d" % (e, i))) for i in range(8)]
                        for e in (ACT, POOL, SP)}}
        P = Prog()

        xv = [V(b_x, kt * WMAX, WMAX, F32) for kt in range(KT)]
        hv = [V(b_h, kt * WMAX // 2, WMAX, BF16) for kt in range(KT)]
        sqv = [V(b_sq, kt * 256, 512, BF16) for kt in range(KT)]
        rtv = [V(b_rt, i * 512, 512, F32) for i in range(3)]
        wslot = [V(b_w, s * SLOT_E // 2, SLOT_E, BF16) for s in range(NSLOT)]
        vec = V(b_vec, 0, NV, F32)
        pmbuf = V(b_pm, 0, 512, BF16)
        ones1024 = V(b_cst, 0, 128, BF16)
        ones128 = V(b_cst, 64, 128, BF16)
        onesblk = V(b_cst, 128, 128, BF16)
        epsv = V(b_cst, 192, 1, F32)
        invc = [V(b_cst, 193 + 16 * g, 16, F32) for g in range(4)]
        fixt = V(b_cst, 193 + 64, 3, F32)
        banks = [V(b_ps, k * 512, 512, F32) for k in range(8)]
        bankpair = [V(b_ps, k * 1024, 1024, F32) for k in range(4)]
        bp_ctr = [0, 0]
        o = 0
        vbuf = [V(b_ph, o + g * WH, WH, F32) for g in range(4)]; o += 4 * WH
        pooled = [V(b_ph, o + g * (WMAX // 2), WMAX, BF16) for g in range(4)]; o += 4 * WMAX // 2
        gcs = [V(b_ph, o + i * 512, 512, F32) for i in range(1)]; o += 512
        gbs = [V(b_ph, o + i * 512, 512, F32) for i in range(2)]; o += 1024
        mixraw = [[V(b_ph, par * 4480 + j * 512, 512, F32) for j in range(8)] for par in range(2)]
        cacc = [V(b_ph, o + i * 512, 512, F32) for i in range(1)]; o += 512
        assert o >= 8576
        cubuf = [V(b_ph, o + j * WH, WH, F32) for j in range(4)]
        sqO = [V(b_ph, o + j * 256, 512, BF16) for j in range(8)]
        o += 4 * WH
        Tk = [V(b_ph, o + k * WH, WH, F32) for k in range(3)]; o += 3 * WH
        Tfin = [V(b_ph, o + i * 512, 512, F32) for i in range(2)]; o += 1024
        yrot = [V(b_ph, o + i * 512, 512, F32) for i in range(4)]; o += 2048
        sqh = [V(b_ph, o + i * 256, 512, BF16) for i in range(4)]; o += 1024
        cat = [V(b_ph, o + kt * (WMAX // 2), WMAX, BF16) for kt in range(KT)]; o += KT * WMAX // 2
        assert o <= PH_N, o
        o = 0
        upraw = [V(b_ph, o + s * WH, WH, F32) for s in range(6)]; o += 6 * WH
        facc = [V(b_ph, o + i * 1024, 1024, F32) for i in range(4)]; o += 4096
        facc_s = [V(b_ph, o + i * 64, 64, F32) for i in range(4)]; o += 256
        fraw = [V(b_ph, j * WH + H, WMAX, F32) for j in range(8)]
        o = max(o, 8 * WMAX)
        actv = [V(b_ph, o + i * (WMAX // 2), WMAX, BF16) for i in range(NPAIR)]; o += NPAIR * WMAX // 2
        fsq = [V(b_ph, o + i * 256, 512, BF16) for i in range(3)]; o += 768
        fstage = [V(b_ph, 8960 + i * 512, 512, F32) for i in range(3)]
        assert 8960 + 3 * 512 <= 6 * WH + 4096 and 8960 >= 7 * WH + H + WMAX
        fstage += [V(b_ph, o + i * 512, 512, F32) for i in range(2)]; o += 1024
        assert o <= PH_N, o

        def vcol(l, off):
            c = l * NV_L + off
            return vec.ap(c, c + 1), vec.r(c, c + 1)

        def mm(out, lhsT, rhs, start, stop):
            (ov, olo, ohi), (lv, llo, lhi), (rv, rlo, rhi) = out, lhsT, rhs
            oa, la, ra = ov.ap(olo, ohi), lv.ap(llo, lhi), rv.ap(rlo, rhi)
            P.op(PE, lambda h: h.matmul(oa, la, ra, start=start, stop=stop),
                 reads=[lv.r(llo, lhi), rv.r(rlo, rhi)], writes=[ov.r(olo, ohi)])

        def act(out, in_, func, scale=None, bias=None):
            (ov, olo, ohi), (iv, ilo, ihi) = out, in_
            oa, ia = ov.ap(olo, ohi), iv.ap(ilo, ihi)
            reads = [iv.r(ilo, ihi)]
            kw = {}
            if scale is not None:
                if isinstance(scale, tuple):
                    kw["scale"] = scale[0]; reads.append(scale[1])
                else:
                    kw["scale"] = scale
            if bias is not None:
                kw["bias"] = bias[0]; reads.append(bias[1])
            P.op(ACT, lambda h: h.activation(out=oa, in_=ia, func=func, **kw), reads=reads, writes=[ov.r(olo, ohi)])

        def tt(eng, out, in0, in1, op):
            (ov, olo, ohi), (av, alo, ahi), (bv, blo, bhi) = out, in0, in1
            oa, aa, ba = ov.ap(olo, ohi), av.ap(alo, ahi), bv.ap(blo, bhi)
            P.op(eng, lambda h: h.tensor_tensor(oa, aa, ba, op),
                 reads=[av.r(alo, ahi), bv.r(blo, bhi)], writes=[ov.r(olo, ohi)])

        def stt(eng, out, in0, scalar, in1, op0, op1):
            (ov, olo, ohi), (av, alo, ahi), (bv, blo, bhi) = out, in0, in1
            oa, aa, ba = ov.ap(olo, ohi), av.ap(alo, ahi), bv.ap(blo, bhi)
            reads = [av.r(alo, ahi), bv.r(blo, bhi)]
            if isinstance(scalar, tuple):
                sc = scalar[0]; reads.append(scalar[1])
            else:
                sc = scalar
            P.op(eng, lambda h: h.scalar_tensor_tensor(oa, aa, sc, ba, op0, op1), reads=reads, writes=[ov.r(olo, ohi)])

        def recip(out, in_):
            (ov, olo, ohi), (iv, ilo, ihi) = out, in_
            oa, ia = ov.ap(olo, ohi), iv.ap(ilo, ihi)
            P.op(DVE, lambda h: h.reciprocal(oa, ia), reads=[iv.r(ilo, ihi)], writes=[ov.r(olo, ohi)])

        def copy(eng, out, in_):
            (ov, olo, ohi), (iv, ilo, ihi) = out, in_
            oa, ia = ov.ap(olo, ohi), iv.ap(ilo, ihi)
            P.op(eng, lambda h: h.tensor_copy(oa, ia), reads=[iv.r(ilo, ihi)], writes=[ov.r(olo, ohi)])

        def memset(eng, out, val):
            (ov, olo, ohi) = out
            oa = ov.ap(olo, ohi)
            P.op(eng, lambda h: h.memset(oa, val), writes=[ov.r(olo, ohi)])

        def dma(eng, out_ap, in_ap, reads=(), writes=()):
            P.op(eng, lambda h: h.dma_start(out=out_ap, in_=in_ap), reads=list(reads), writes=list(writes), dma=True)

        bank_ctr = [0, 0]

        def mm_bank(limit=6):
            k = bank_ctr[0] % limit
            bank_ctr[0] += 1
            return banks[k]

        def st_bank():
            k = 6 + bank_ctr[1] % 2
            bank_ctr[1] += 1
            return banks[k]

        rt_ctr = [0]

        def next_rt():
            r_ = rtv[rt_ctr[0] % 3]
            rt_ctr[0] += 1
            return r_

        memset(POOL, (ones1024, 0, 128), 1.0 / 1024)
        memset(POOL, (ones128, 0, 128), 1.0 / 128)
        memset(POOL, (onesblk, 0, 128), 0.0)
        oa = onesblk.base[0:64, 0:64]
        P.op(POOL, lambda h: h.memset(oa, 1.0 / 64), writes=[onesblk.r(0, 128)])
        ob = onesblk.base[64:128, 64:128]
        P.op(POOL, lambda h: h.memset(ob, 1.0 / 64), writes=[onesblk.r(0, 128)])
        memset(POOL, (epsv, 0, 1), EPS)
        for g in range(4):
            w = 2 << g
            memset(POOL, (invc[g], 0, 16), 1.0)
            for t in range(min(w - 1, 16)):
                memset(POOL, (invc[g], t, t + 1), float(w) / (t + 1))
        dma(SP, vec.ap(0, NV), vec_d, writes=[vec.r(0, NV)])
        dma(SP, b_stp.t[:, :], stp_d, writes=[(b_stp, 0, b_stp.width)])
        dma(SP, b_stc.t[:, :], stc_d, writes=[(b_stc, 0, b_stc.width)])
        dma(SP, b_stf.t[:, :], stf_d, writes=[(b_stf, 0, b_stf.width)])

        wseq = []
        for blk in range(2):
            for l in range(L):
                for ui in range(NU):
                    wseq.append((blk, l, ui))
        wstate = {"issued": 0, "used": 0, "live": set(), "next_ring": 0}
        ring_idx = []
        rc = 0
        for (_, l_, ui_) in wseq:
            if tab[ui_][0] == "pm":
                ring_idx.append(None)
            else:
                ring_idx.append(rc)
                rc += 1

        def issue_weights():
            while wstate["issued"] < len(wseq):
                q = wstate["issued"]
                (_, l, ui) = wseq[q]
                n = tab[ui][2]
                if ring_idx[q] is None:
                    if q > wstate["used"] + 4:
                        break
                    sv = pmbuf
                else:
                    r = ring_idx[q]
                    done_upto = min(wstate["live"]) if wstate["live"] else wstate["next_ring"]
                    if r - NSLOT >= done_upto:
                        break
                    sv = wslot[r % NSLOT]
                dma(POOL, sv.ap(0, n), wts_d[l, :, offs[ui]:offs[ui] + n], writes=[sv.r(0, n)])
                wstate["issued"] += 1

        def next_unit(kind):
            q = wstate["used"]
            wstate["used"] += 1
            (_, l, ui) = wseq[q]
            assert tab[ui][0] == kind, (tab[ui], kind)
            if ring_idx[q] is None:
                issue_weights()
                assert wstate["issued"] > q
                return pmbuf, None
            wstate["live"].add(ring_idx[q])
            wstate["next_ring"] = ring_idx[q] + 1
            issue_weights()
            assert wstate["issued"] > q
            return wslot[ring_idx[q] % NSLOT], ring_idx[q]

        def release(q):
            wstate["live"].discard(q)
            issue_weights()

        def stats_rstd(src, n, ones_v, nsrc, bank=None):
            pb = bank if bank is not None else st_bank()
            for i, (sv, lo) in enumerate(src):
                mm((pb, 0, n), (ones_v, 0, 128), (sv, lo, lo + n), i == 0, i == nsrc - 1)
            rt = next_rt()
            rstd_from(pb, rt, n)
            return rt

        def rstd_from(pb, rt, n):
            act((rt, 0, n), (pb, 0, n), AF.Ln, bias=(epsv.ap(0, 1), epsv.r(0, 1)))
            act((rt, 0, n), (rt, 0, n), AF.Exp, scale=-0.5)

        def pre_a(a, b):
            n = b - a
            for kt in range(KT):
                act((sqv[kt], 0, n), (xv[kt], a, b), AF.Square)

        def pre_b(l, goff, a, b, bank=None):
            n = b - a
            rt = stats_rstd([(sqv[kt], 0) for kt in range(KT)], n, ones1024, KT, bank=bank)
            for kt in range(KT):
                stt(DVE, (hv[kt], a, b), (xv[kt], a, b), vcol(l, goff + kt), (rt, 0, n), ALU.mult, ALU.mult)

        def pre_norm(l, goff, a, b):
            pre_a(a, b)
            pre_b(l, goff, a, b)

        def post_norm_residual(l, goff, raws, a, b, rt=None, all_dve=False):
            n = b - a
            assert rt is not None
            for j in range(8):
                rv_, lo = raws[j]
                tt(DVE, (rv_, lo, lo + n), (rv_, lo, lo + n), (rt, 0, n), ALU.mult)
                tt(POOL if (j % 2 == 0 and not all_dve) else DVE, (xv[j], a, b), (xv[j], a, b), (rv_, lo, lo + n), ALU.add)

        blocks = [
            [dict(kind="p", t0=0, n=1024, pl=0, hs=0, first=True, last=False)],
            [dict(kind="p", t0=1024, n=1024, pl=0, hs=0, first=False, last=True),
             dict(kind="s", t0=0, n=DSEQ, pl=1024, hs=H + 1024, first=False, last=True)],
        ]

        def ap3(v, ncol_each, stride, cnt, lo, hi):
            full = v.b.t[:, v.off:v.off + stride * cnt].rearrange("p (g w) -> p g w", g=cnt)
            return full[:, :, lo:hi]

        for bi, segs in enumerate(blocks):
            W = sum(s["n"] for s in segs)
            chunks = []
            for si, s in enumerate(segs):
                c0 = 0
                while c0 < s["n"]:
                    n = min(512, s["n"] - c0)
                    chunks.append(dict(a=s["pl"] + c0, b=s["pl"] + c0 + n, n=n, seg=si,
                                       ha=s["hs"] + H + c0, hb=s["hs"] + H + c0 + n,
                                       first=(c0 == 0), last=(c0 + n == s["n"])))
                    c0 += n
            for c in chunks:
                s = segs[c["seg"]]
                t0 = s["t0"] + (c["a"] - s["pl"])
                for kt in range(KT):
                    src = (xp_d[kt, :, t0:t0 + c["n"]] if s["kind"] == "p" else xs_d[kt, :, t0:t0 + c["n"]])
                    dma(SP, xv[kt].ap(c["a"], c["b"]), src, writes=[xv[kt].r(c["a"], c["b"])])

            final_done = set()

            def final_norm(ci, stage, bank=None):
                c = chunks[ci]
                a, b, n = c["a"], c["b"], c["n"]
                s = segs[c["seg"]]
                pre_a(a, b)
                rt = stats_rstd([(sqv[kt], 0) for kt in range(KT)], n, ones1024, KT, bank=bank)
                for kt in range(KT):
                    st_ = stage[kt % len(stage)]
                    gcol = (vec.ap(L * NV_L + kt, L * NV_L + kt + 1), vec.r(L * NV_L + kt, L * NV_L + kt + 1))
                    stt(DVE, (st_, 0, n), (xv[kt], a, b), gcol, (rt, 0, n), ALU.mult, ALU.mult)
                    t0 = s["t0"] + (a - s["pl"])
                    dst = yp_d[kt, :, t0:t0 + n] if s["kind"] == "p" else ys_d[kt, :, t0:t0 + n]
                    dma(SP, dst, st_.ap(0, n), reads=[st_.r(0, n)])
                final_done.add(ci)

            for l in range(L):
                def halo_init(which, l=l):
                    for si, s in enumerate(segs):
                        hs = s["hs"]
                        if s["kind"] == "p" and s["first"]:
                            for g in range(4):
                                if which == "v":
                                    memset(POOL, (vbuf[g], hs, hs + H), 0.0)
                                else:
                                    memset(POOL, (cubuf[g], hs, hs + H), 0.0)
                        else:
                            srcb, srcc = (b_cv, b_cc) if s["kind"] == "p" else (b_stp, b_stc)
                            for g in range(4):
                                if which == "v":
                                    o15 = (l * 4 + g) * 15
                                    sa = srcb.t[:, o15:o15 + 15]
                                    da = vbuf[g].ap(hs + 1, hs + H)
                                    P.op(POOL, lambda h, da=da, sa=sa: h.tensor_copy(da, sa),
                                         reads=[(srcb, o15, o15 + 15)], writes=[vbuf[g].r(hs + 1, hs + H)])
                                else:
                                    o2 = (l * 4 + g) * 2
                                    sa2 = srcc.t[:, o2:o2 + 2]
                                    da2 = cubuf[g].ap(hs + H - 2, hs + H)
                                    P.op(POOL, lambda h, da2=da2, sa2=sa2: h.tensor_copy(da2, sa2),
                                         reads=[(srcc, o2, o2 + 2)], writes=[cubuf[g].r(hs + H - 2, hs + H)])

                def v_tiles(lv, cis, vunits):
                    for g in range(4):
                        ws = vunits[g // 2][0]
                        jj = g % 2
                        for ci in cis:
                            c = chunks[ci]
                            a, b, n, ha, hb = c["a"], c["b"], c["n"], c["ha"], c["hb"]
                            pb = mm_bank()
                            for kt in range(KT):
                                mm((pb, 0, n), (ws, kt * 256 + jj * 128, kt * 256 + jj * 128 + 128), (hv[kt], a, b),
                                   kt == 0, kt == KT - 1)
                            act((vbuf[g], ha, hb), (pb, 0, n), AF.Copy)

                nch = len(chunks)
                if l == 0:
                    halo_init("v")
                    for c in chunks:
                        pre_norm(l, V_GPRE, c["a"], c["b"])
                    vunits = [next_unit("z"), next_unit("z")]
                    v_tiles(l, range(nch), vunits)
                    release(vunits[0][1])
                    release(vunits[1][1])
                halo_init("cu")
                pm_slot, pmq = next_unit("pm")
                def pool_chain(g, c, par):
                    a, b, n, ha, hb = c["a"], c["b"], c["n"], c["ha"], c["hb"]
                    s = segs[c["seg"]]
                    hs = s["hs"]
                    cur = vbuf[g]
                    tf = Tfin[par]
                    for k in range(g + 1):
                        sh = 1 << k
                        if k == g:
                            tt(POOL, (tf, 0, n), (cur, ha, hb), (cur, ha - sh, hb - sh), ALU.add)
                        else:
                            dst = Tk[k]
                            lo = (hs + 2 * sh) if c["first"] else ha
                            tt(POOL, (dst, lo, hb), (cur, lo, hb), (cur, lo - sh, hb - sh), ALU.add)
                            cur = dst
                    if s["kind"] == "p" and s["first"] and c["first"]:
                        tt(POOL, (tf, 0, 16), (tf, 0, 16), (invc[g], 0, 16), ALU.mult)
                    if c is chunks[-1]:
                        for si, s in enumerate(segs):
                            he = s["hs"] + H + s["n"]
                            if s["last"]:
                                slot = 0 if s["kind"] == "p" else 1
                                o15 = ((l * 2 + slot) * 4 + g) * 15
                                dstb = b_op
                            else:
                                o15 = (l * 4 + g) * 15
                                dstb = b_cv
                            da = dstb.t[:, o15:o15 + 15]
                            sa = vbuf[g].ap(he - 15, he)
                            P.op(POOL, lambda h, da=da, sa=sa: h.tensor_copy(da, sa),
                                 reads=[vbuf[g].r(he - 15, he)], writes=[(dstb, o15, o15 + 15)])

                steps = [(j, ci) for j in range(4) for ci in range(len(chunks))]
                pool_chain(steps[0][0], chunks[steps[0][1]], 0)
                deferred = []

                def flush_deferred():
                    for fn in deferred:
                        fn()
                    del deferred[:]

                step = 0
                q_norm = []
                q_ppm = []

                def run_ppm():
                    for fn in q_ppm:
                        fn()
                    del q_ppm[:]

                for j in range(4):
                    ws, wq = next_unit("z")
                    for ci, c in enumerate(chunks):
                        a, b, n, ha, hb = c["a"], c["b"], c["n"], c["ha"], c["hb"]
                        pgc, pu, pgb = mm_bank(), mm_bank(), mm_bank()
                        for ti, pb in enumerate((pgc, pu, pgb)):
                            for kt in range(KT):
                                mm((pb, 0, n), (ws, kt * 384 + ti * 128, kt * 384 + ti * 128 + 128), (hv[kt], a, b),
                                   kt == 0, kt == KT - 1)
                        if step + 1 < len(steps):
                            pool_chain(steps[step + 1][0], chunks[steps[step + 1][1]], (step + 1) % 2)
                        stt(DVE, (pooled[j], a, b), (Tfin[step % 2], 0, n), 1.0 / (2 << j), (vbuf[j], ha, hb),
                            ALU.mult, ALU.subtract)
                        gc_ = gcs[0]
                        gb_ = gbs[step % 2]
                        ca = cacc[0]
                        yc = yrot[(2 * step) % 4]
                        sqc = sqh[(2 * step) % 4]
                        act((gc_, 0, n), (pgc, 0, n), AF.Copy)
                        act((gb_, 0, n), (pgb, 0, n), AF.Copy)
                        tt(DVE, (cubuf[j], ha, hb), (pu, 0, n), (gc_, 0, n), ALU.mult)
                        act((ca, 0, n), (cubuf[j], ha, hb), AF.Copy, scale=vcol(l, V_CW + 4 * 2 + j))
                        prev = list(q_norm)
                        del q_norm[:]
                        run_ppm()
                        for (fn_b, fn_c) in prev:
                            fn_b()
                        stt(DVE, (ca, 0, n), (cubuf[j], ha - 1, hb - 1), vcol(l, V_CW + 4 * 1 + j), (ca, 0, n),
                            ALU.mult, ALU.add)
                        stt(DVE, (ca, 0, n), (cubuf[j], ha - 2, hb - 2), vcol(l, V_CW + 4 * 0 + j), (ca, 0, n),
                            ALU.mult, ALU.add)
                        tt(DVE, (yc, 0, n), (gb_, 0, n), (ca, 0, n), ALU.mult)
                        for (fn_b, fn_c) in prev:
                            fn_c()
                        act((sqc, 0, n), (yc, 0, n), AF.Square)

                        boxc = {}

                        def stage_b_c(n=n, sqc=sqc, box=boxc):
                            box["rt"] = stats_rstd([(sqc, 0)], n, onesblk, 1)

                        def stage_c_c(j=j, a=a, b=b, n=n, yc=yc, box=boxc):
                            stt(DVE, (cat[4 + j], a, b), (yc, 0, n), vcol(l, V_GCONV + j), (box["rt"], 0, n),
                                ALU.mult, ALU.mult)

                        q_norm.append((stage_b_c, stage_c_c))

                        def ppm_stage(j=j, a=a, b=b, n=n, step=step):
                            ppm = mm_bank()
                            mm((ppm, 0, n), (pm_slot, j * 128, j * 128 + 128), (pooled[j], a, b), True, True)
                            yp_ = yrot[(2 * step + 3) % 4]
                            sqp = sqh[(2 * step + 3) % 4]
                            act((yp_, 0, n), (ppm, 0, n), AF.Copy, scale=vcol(l, V_PSCALE + j))
                            act((sqp, 0, n), (ppm, 0, n), AF.Square, scale=vcol(l, V_PSCALE + j))
                            boxp = {}

                            def stage_b_p(n=n, sqp=sqp, box=boxp):
                                box["rt"] = stats_rstd([(sqp, 0)], n, ones128, 1)

                            def stage_c_p(j=j, a=a, b=b, n=n, yp_=yp_, box=boxp):
                                stt(DVE, (cat[j], a, b), (yp_, 0, n), vcol(l, V_GPOOL + j), (box["rt"], 0, n),
                                    ALU.mult, ALU.mult)

                            q_norm.append((stage_b_p, stage_c_p))

                        q_ppm.append(ppm_stage)
                        step += 1
                    release(wq)
                    for si, s in enumerate(segs):
                        he = s["hs"] + H + s["n"]
                        if s["last"]:
                            slot = 0 if s["kind"] == "p" else 1
                            o2 = ((l * 2 + slot) * 4 + j) * 2
                            dstb = b_oc
                        else:
                            o2 = (l * 4 + j) * 2
                            dstb = b_cc
                        da = dstb.t[:, o2:o2 + 2]
                        sa = cubuf[j].ap(he - 2, he)
                        P.op(POOL, lambda h, da=da, sa=sa: h.tensor_copy(da, sa),
                             reads=[cubuf[j].r(he - 2, he)], writes=[(dstb, o2, o2 + 2)])
                prev = list(q_norm)
                del q_norm[:]
                run_ppm()
                for (fn_b, fn_c) in prev:
                    fn_b()
                    fn_c()
                last_def = list(q_norm)
                del q_norm[:]
                release(pmq)
                ou = [next_unit("o") for _ in range(3)]
                oslots = [u_[0] for u_ in ou]
                ocols = (384, 384, 256)

                def o_mm(ci, js=range(8)):
                    c = chunks[ci]
                    a, b, n = c["a"], c["b"], c["n"]
                    mr = mixraw[min(ci, 1)]
                    for j in js:
                        ui, jj = (j // 3, j % 3) if j < 6 else (2, j - 6)
                        ws, ncol = oslots[ui], ocols[ui]
                        pb = mm_bank()
                        for kt in range(KT):
                            mm((pb, 0, n), (ws, kt * ncol + jj * 128, kt * ncol + jj * 128 + 128), (cat[kt], a, b),
                               kt == 0, kt == KT - 1)
                        act((mr[j], 0, n), (pb, 0, n), AF.Copy, scale=vcol(l, V_GPOST + j))
                        act((sqO[j], 0, n), (pb, 0, n), AF.Square)

                def o_post(ci):
                    c = chunks[ci]
                    a, b, n = c["a"], c["b"], c["n"]
                    mr = mixraw[min(ci, 1)]
                    rt = stats_rstd([(sqO[j], 0) for j in range(8)], n, ones1024, 8)
                    post_norm_residual(l, V_GPOST, [(mr[j], 0) for j in range(8)], a, b, rt=rt)

                nch = len(chunks)
                if nch > 2:
                    for (fn_b, fn_c) in last_def:
                        fn_b()
                        fn_c()
                    o_mm(0)
                else:
                    o_mm(0, range(0, 4))
                    for (fn_b, fn_c) in last_def:
                        fn_b()
                        fn_c()
                    o_mm(0, range(4, 8))
                o_post(0)
                for k in range(1, nch):
                    o_mm(k, range(0, 5))
                    pre_a(chunks[k - 1]["a"], chunks[k - 1]["b"])
                    o_mm(k, range(5, 8))
                    if k == nch - 1:
                        for u_ in ou:
                            release(u_[1])
                    pre_b(l, V_GPREF, chunks[k - 1]["a"], chunks[k - 1]["b"])
                    o_post(k)
                NLEAD = 3

                def u_halo(i):
                    for t_ in range(2):
                        jt = i + NPAIR * t_
                        us = upraw[(2 * i + t_) % 6]
                        for si, s in enumerate(segs):
                            hs = s["hs"]
                            if s["kind"] == "p" and s["first"]:
                                memset(POOL, (us, hs + H - 2, hs + H), 0.0)
                            else:
                                srcb = b_cf if s["kind"] == "p" else b_stf
                                o2 = (l * 44 + jt) * 2
                                sa = srcb.t[:, o2:o2 + 2]
                                da = us.ap(hs + H - 2, hs + H)
                                P.op(POOL, lambda h, da=da, sa=sa: h.tensor_copy(da, sa),
                                     reads=[(srcb, o2, o2 + 2)], writes=[us.r(hs + H - 2, hs + H)])

                def u_mm(i, ws, ci, small_bank=None):
                    c = chunks[ci]
                    a, b, n = c["a"], c["b"], c["n"]
                    pbs = []
                    for t_ in range(2):
                        if small_bank is not None:
                            pb, c0_ = small_bank, 64 * t_
                        else:
                            pb, c0_ = mm_bank(), 0
                        for kt in range(KT):
                            mm((pb, c0_, c0_ + n), (ws, kt * 256 + t_ * 128, kt * 256 + t_ * 128 + 128), (hv[kt], a, b),
                               kt == 0, kt == KT - 1)
                        pbs.append((pb, c0_))
                    return pbs

                def u_ew(i, ci, pbs, defer_b=None):
                    c = chunks[ci]
                    a, b, n, ha, hb = c["a"], c["b"], c["n"], c["ha"], c["hb"]
                    accs = []
                    for t_ in range(2):
                        jt = i + NPAIR * t_
                        us = upraw[(2 * i + t_) % 6]
                        pb, c0_ = pbs[t_]
                        if n <= 64:
                            fa = facc_s[fctr[1] % 4]
                            fctr[1] += 1
                        else:
                            fa = facc[fctr[0] % 4]
                            fctr[0] += 1
                        act((us, ha, hb), (pb, c0_, c0_ + n), AF.Copy)
                        act((fa, 0, n), (pb, c0_, c0_ + n), AF.Copy, scale=vcol(l, V_FW + 44 * 2 + jt))
                        stt(DVE, (fa, 0, n), (us, ha - 1, hb - 1), vcol(l, V_FW + 44 * 1 + jt), (fa, 0, n),
                            ALU.mult, ALU.add)
                        stt(DVE, (fa, 0, n), (us, ha - 2, hb - 2), vcol(l, V_FW + 44 * 0 + jt), (fa, 0, n),
                            ALU.mult, ALU.add)
                        accs.append(fa)

                    def part_b(i=i, a=a, b=b, n=n, accs=accs):
                        act((accs[0], 0, n), (accs[0], 0, n), AF.Silu)
                        tt(DVE, (actv[i], a, b), (accs[0], 0, n), (accs[1], 0, n), ALU.mult)

                    if defer_b is None:
                        part_b()
                    else:
                        defer_b.append(part_b)

                def u_step(i, ws, ci):
                    u_ew(i, ci, u_mm(i, ws, ci))

                def u_step2(i, ws):
                    c0, c1 = chunks[0], chunks[1]
                    a, b = c0["a"], c1["b"]
                    ha, hb = c0["ha"], c1["hb"]
                    assert c0["b"] == c1["a"] and c0["hb"] == c1["ha"] and b - a == 1024
                    accs = []
                    for t_ in range(2):
                        jt = i + NPAIR * t_
                        us = upraw[(2 * i + t_) % 6]
                        k2 = bp_ctr[0] % 3
                        bp_ctr[0] += 1
                        for ci, c in enumerate((c0, c1)):
                            pb = banks[2 * k2 + ci]
                            for kt in range(KT):
                                mm((pb, 0, 512), (ws, kt * 256 + t_ * 128, kt * 256 + t_ * 128 + 128),
                                   (hv[kt], c["a"], c["b"]), kt == 0, kt == KT - 1)
                        pp = bankpair[k2]
                        fa = facc[fctr[0] % 4]
                        fctr[0] += 1
                        act((us, ha, hb), (pp, 0, 1024), AF.Copy)
                        act((fa, 0, 1024), (pp, 0, 1024), AF.Copy, scale=vcol(l, V_FW + 44 * 2 + jt))
                        stt(DVE, (fa, 0, 1024), (us, ha - 1, hb - 1), vcol(l, V_FW + 44 * 1 + jt), (fa, 0, 1024),
                            ALU.mult, ALU.add)
                        stt(DVE, (fa, 0, 1024), (us, ha - 2, hb - 2), vcol(l, V_FW + 44 * 0 + jt), (fa, 0, 1024),
                            ALU.mult, ALU.add)
                        accs.append(fa)

                    def part_b(i=i, a=a, b=b, accs=accs):
                        act((accs[0], 0, 1024), (accs[0], 0, 1024), AF.Silu)
                        tt(DVE, (actv[i], a, b), (accs[0], 0, 1024), (accs[1], 0, 1024), ALU.mult)

                    return part_b

                def u_tails(i):
                    for t_ in range(2):
                        jt = i + NPAIR * t_
                        us = upraw[(2 * i + t_) % 6]
                        for si, s in enumerate(segs):
                            he = s["hs"] + H + s["n"]
                            if s["last"]:
                                slot = 0 if s["kind"] == "p" else 1
                                o2 = ((l * 2 + slot) * 44 + jt) * 2
                                dstb = b_of
                            else:
                                o2 = (l * 44 + jt) * 2
                                dstb = b_cf
                            da = dstb.t[:, o2:o2 + 2]
                            sa = us.ap(he - 2, he)
                            P.op(POOL, lambda h, da=da, sa=sa: h.tensor_copy(da, sa),
                                 reads=[us.r(he - 2, he)], writes=[(dstb, o2, o2 + 2)])

                fctr = [0, 0]
                lead = []
                pend = []
                for i in range(NLEAD):
                    ws, wq = next_unit("u")
                    lead.append((ws, wq))
                    u_halo(i)
                    for ci in range(nch - 1):
                        nleft = (NLEAD - 1 - i) * (nch - 1) + (nch - 2 - ci)
                        if nleft >= 2:
                            u_step(i, ws, ci)
                        else:
                            pend.append((i, ci, u_mm(i, ws, ci)))
                    if i == 1:
                        pre_a(chunks[nch - 1]["a"], chunks[nch - 1]["b"])
                pre_b(l, V_GPREF, chunks[nch - 1]["a"], chunks[nch - 1]["b"])
                for (i, ci, pbs) in pend:
                    u_ew(i, ci, pbs)
                for i in range(NLEAD):
                    ws, wq = lead[i]
                    u_step(i, ws, nch - 1)
                    release(wq)
                    u_tails(i)
                u_halo(NLEAD)
                small_b = []
                for i in range(NLEAD, NPAIR):
                    ws, wq = next_unit("u")
                    if i + 1 < NPAIR:
                        u_halo(i + 1)
                    pb2 = u_step2(i, ws)
                    prev_small = list(small_b)
                    del small_b[:]
                    for ci in range(2, nch):
                        sb_ = banks[6 + (bp_ctr[1] % 2)]
                        bp_ctr[1] += 1
                        u_ew(i, ci, u_mm(i, ws, ci, small_bank=sb_), defer_b=small_b)
                    pb2()
                    for fn in prev_small:
                        fn()
                    release(wq)
                    u_tails(i)
                for fn in small_b:
                    fn()
                del small_b[:]
                def d_step(j, ws, ci):
                    c = chunks[ci]
                    a, b, n = c["a"], c["b"], c["n"]
                    pb = mm_bank(5)
                    for kt in range(NPAIR):
                        mm((pb, 0, n), (ws, kt * 128, kt * 128 + 128), (actv[kt], a, b), kt == 0, kt == NPAIR - 1)
                    act((fraw[j], a, b), (pb, 0, n), AF.Copy, scale=vcol(l, V_GPOSTF + j))
                    fq = fsq[fctr[2] % 3]
                    fctr[2] += 1
                    act((fq, 0, n), (pb, 0, n), AF.Square)
                    while len(deferred) > (nch - 2):
                        deferred.pop(0)()
                    deferred.append(lambda ci=ci, n=n, fq=fq, j=j: mm((banks[5 + ci], 0, n), (ones1024, 0, 128),
                                                                     (fq, 0, n), j == 0, j == 7))

                def post_fin(ci):
                    c = chunks[ci]
                    a, b, n = c["a"], c["b"], c["n"]
                    rt = next_rt()
                    rstd_from(banks[5 + ci], rt, n)
                    post_norm_residual(l, V_GPOSTF, [(fraw[j], a) for j in range(8)], a, b, rt=rt, all_dve=True)

                fctr.append(0)
                KL = 3
                for j in range(8 - KL):
                    ws, wq = next_unit("d")
                    for ci in range(nch):
                        d_step(j, ws, ci)
                    release(wq)
                if KL:
                    leadd = []
                    for j in range(8 - KL, 8):
                        ws, wq = next_unit("d")
                        leadd.append((j, ws, wq))
                        d_step(j, ws, 0)
                    bsteps = [(j, ws, ci) for (j, ws, wq) in leadd for ci in range(1, nch)]
                    d_step(*bsteps[0])
                    flush_deferred()
                    post_fin(0)
                    for st in bsteps[1:]:
                        d_step(*st)
                    for (j, ws, wq) in leadd:
                        release(wq)
                    flush_deferred()
                    if l == L - 1:
                        final_norm(0, fstage, bank=banks[5])
                        for ci in range(1, nch):
                            post_fin(ci)
                        continue
                    pre_a(chunks[0]["a"], chunks[0]["b"])
                    pre_b(l + 1, V_GPRE, chunks[0]["a"], chunks[0]["b"], bank=banks[5])
                    for ci in range(1, nch):
                        post_fin(ci)
                    halo_init("v", l + 1)
                    vunits = [next_unit("z"), next_unit("z")]
                    v_tiles(l + 1, [0], vunits)
                    for ci in range(1, nch):
                        pre_a(chunks[ci]["a"], chunks[ci]["b"])
                        pre_b(l + 1, V_GPRE, chunks[ci]["a"], chunks[ci]["b"])
                    v_tiles(l + 1, range(1, nch), vunits)
                    release(vunits[0][1])
                    release(vunits[1][1])
                else:
                    flush_deferred()
                    for ci in range(nch):
                        post_fin(ci)
            for ci in range(len(chunks)):
                if ci not in final_done:
                    final_norm(ci, mixraw[ci % 2])
        dma(SP, opool_d, b_op.t[:, :], reads=[(b_op, 0, b_op.width)])
        dma(SP, oconv_d, b_oc.t[:, :], reads=[(b_oc, 0, b_oc.width)])
        dma(SP, offn_d, b_of.t[:, :], reads=[(b_of, 0, b_of.width)])
        assert wstate["used"] == len(wseq), (wstate, len(wseq))

        run = P.emit(sems)
        with nc.Block() as block:
            @block.sync
            def _(h):
                run(SP, h)

            @block.scalar
            def _(h):
                run(ACT, h)

            @block.vector
            def _(h):
                run(DVE, h)

            @block.gpsimd
            def _(h):
                run(POOL, h)

            @block.tensor
            def _(h):
                run(PE, h)
    return nc


def kernel(x_prompt, x_sample, state_pool, state_conv, state_ffn_conv, w_in, pool_mix, pool_scale, conv_w,
           g_pool_out, g_conv_out, w_out, g_pre_mix, g_post_mix, g_pre_ffn, g_post_ffn, w_up, ffn_conv_w,
           w_down, g_final):
    f = lambda a: np.asarray(a, dtype=np.float32)
    x_prompt, x_sample, state_pool, state_conv, state_ffn_conv = map(f, (x_prompt, x_sample, state_pool,
                                                                         state_conv, state_ffn_conv))
    wts = pack_weights(f(w_in), f(pool_mix), f(w_out), f(w_up), f(w_down))
    vecs = pack_vecs(f(pool_scale), f(conv_w), f(g_pool_out), f(g_conv_out), f(g_pre_mix), f(g_post_mix),
                     f(g_pre_ffn), f(g_post_ffn), f(ffn_conv_w), f(g_final))
    in_maps = []
    for b in range(NCORES):
        xp = np.ascontiguousarray(x_prompt[b].T).reshape(KT, 128, SEQ)
        xs = np.ascontiguousarray(x_sample[b].T).reshape(KT, 128, DSEQ)
        stp = np.ascontiguousarray(state_pool[:, b].reshape(L, 15, 4, 128).transpose(3, 0, 2, 1)).reshape(128, -1)
        stc = np.ascontiguousarray(state_conv[:, b].reshape(L, 2, 4, 128).transpose(3, 0, 2, 1)).reshape(128, -1)
        stf = np.ascontiguousarray(state_ffn_conv[:, b].reshape(L, 2, 44, 128).transpose(3, 0, 2, 1)).reshape(128, -1)
        in_maps.append({"xp": xp, "xs": xs, "stp": stp, "stc": stc, "stf": stf, "vecs": vecs, "wts": wts})
    nc = build_nc()
    res = run_bass_kernel_spmd(nc, in_maps, core_ids=list(range(NCORES)))
    B = NCORES
    y_p = np.empty((B, SEQ, D), np.float32)
    y_s = np.empty((B, DSEQ, D), np.float32)
    npool = [np.empty((L, B, 15, 512), np.float32) for _ in range(2)]
    nconv = [np.empty((L, B, 2, 512), np.float32) for _ in range(2)]
    nffn = [np.empty((L, B, 2, 2 * DFF), np.float32) for _ in range(2)]
    for b in range(B):
        r = res.results[b]
        y_p[b] = np.asarray(r["yp"]).reshape(D, SEQ).T
        y_s[b] = np.asarray(r["ys"]).reshape(D, DSEQ).T
        op = np.asarray(r["opool"]).reshape(128, L, 2, 4, 15)
        oc = np.asarray(r["oconv"]).reshape(128, L, 2, 4, 2)
        of = np.asarray(r["offn"]).reshape(128, L, 2, 44, 2)
        for s in range(2):
            npool[s][:, b] = op[:, :, s].transpose(1, 3, 2, 0).reshape(L, 15, 512)
            nconv[s][:, b] = oc[:, :, s].transpose(1, 3, 2, 0).reshape(L, 2, 512)
            nffn[s][:, b] = of[:, :, s].transpose(1, 3, 2, 0).reshape(L, 2, 2 * DFF)
    return (y_p, y_s, npool[0], nconv[0], nffn[0], npool[1], nconv[1], nffn[1])
```

```python
import numpy as np
from contextlib import ExitStack
import concourse.bass as bass
import concourse.mybir as mybir
from concourse.bass_utils import run_bass_kernel_spmd

F32 = mybir.dt.float32
BF16 = mybir.dt.bfloat16
ALU = mybir.AluOpType
AF = mybir.ActivationFunctionType

PE, ACT, DVE, POOL, SP = "pe", "act", "dve", "pool", "sp"
COMPUTE = (PE, ACT, DVE, POOL)

NCORES = 8
L = 4
D = 1024
KT = 8
DFF = 2816
NPAIR = 22
SEQ = 2048
DSEQ = 64
EPS = 1e-6
H = 16
NSLOT = 5
SLOT_E = 3072
PREFETCH = 2
NV_L = 188
NV = L * NV_L + 8
V_GPRE, V_GPOST, V_GPREF, V_GPOSTF, V_PSCALE, V_GPOOL, V_GCONV, V_CW, V_FW = 0, 8, 16, 24, 32, 36, 40, 44, 56


class Buf:
    def __init__(self, name, t, width):
        self.name = name
        self.t = t
        self.width = width
        self.recs = []


class Op:
    __slots__ = ("eng", "fn", "deps", "signal", "count", "is_dma", "dma_sem", "dma_val", "dma_prev")

    def __init__(self, eng, fn, is_dma):
        self.eng = eng
        self.fn = fn
        self.deps = []
        self.signal = False
        self.count = 0
        self.is_dma = is_dma
        self.dma_sem = None
        self.dma_val = 0
        self.dma_prev = None


class Prog:
    def __init__(self, dma_ring=8):
        self.streams = {e: [] for e in (PE, ACT, DVE, POOL, SP)}
        self.dma_ring = dma_ring
        self.dma_count = {e: 0 for e in (ACT, POOL, SP)}
        self.dma_hist = {e: [] for e in (ACT, POOL, SP)}

    def op(self, eng, fn, reads=(), writes=(), dma=False):
        o = Op(eng, fn, dma)
        deps = {}
        for (b, lo, hi) in reads:
            assert 0 <= lo < hi <= b.width, (b.name, lo, hi, b.width)
            for (l2, h2, k2, o2) in b.recs:
                if k2 == "w" and l2 < hi and lo < h2:
                    deps[id(o2)] = (o2, True)
        for (b, lo, hi) in writes:
            assert 0 <= lo < hi <= b.width, (b.name, lo, hi, b.width)
            for (l2, h2, k2, o2) in b.recs:
                if l2 < hi and lo < h2 and id(o2) not in deps:
                    deps[id(o2)] = (o2, False)
        for (b, lo, hi) in writes:
            b.recs = [r for r in b.recs if not (lo <= r[0] and r[1] <= hi)]
            b.recs.append((lo, hi, "w", o))
        for (b, lo, hi) in reads:
            if not dma:
                b.recs = [r for r in b.recs
                          if not (r[2] == "r" and r[3].eng == eng and not r[3].is_dma
                                  and lo <= r[0] and r[1] <= hi)]
            b.recs.append((lo, hi, "r", o))
        for (o2, israw) in deps.values():
            if o2 is o:
                continue
            if (not o2.is_dma) and (not dma) and o2.eng == eng:
                if eng == PE:
                    continue
            o.deps.append(o2)
            if not o2.is_dma:
                o2.signal = True
        if dma:
            k = self.dma_count[eng]
            self.dma_count[eng] += 1
            o.dma_val = 16 * (k // self.dma_ring + 1)
            o.dma_sem = k % self.dma_ring
            hist = self.dma_hist[eng]
            if k >= self.dma_ring:
                o.dma_prev = hist[k - self.dma_ring]
            hist.append(o)
        self.streams[eng].append(o)
        return o

    def emit(self, sems):
        tl = sems["tl"]
        dsem = sems["dma"]
        for e in COMPUTE:
            c = 0
            for o in self.streams[e]:
                if not o.is_dma and o.signal:
                    c += 1
                    o.count = c

        def run_stream(e, h):
            waited = {}
            for o in self.streams[e]:
                waits = {}
                deps = list(o.deps)
                if o.is_dma and o.dma_prev is not None:
                    deps.append(o.dma_prev)
                for d in deps:
                    if d.is_dma:
                        key = ("dma", d.eng, d.dma_sem)
                        val = d.dma_val
                        sem = dsem[d.eng][d.dma_sem]
                    else:
                        key = ("tl", d.eng)
                        val = d.count
                        sem = tl[d.eng]
                        assert val > 0
                    if waited.get(key, 0) >= val:
                        continue
                    if key not in waits or waits[key][1] < val:
                        waits[key] = (sem, val)
                for key, (sem, val) in waits.items():
                    h.wait_ge(sem, val)
                    waited[key] = val
                ins = o.fn(h)
                if o.is_dma:
                    ins.then_inc(dsem[o.eng][o.dma_sem], 16)
                elif o.signal:
                    ins.then_inc(tl[o.eng], 1)
            if e in self.dma_hist:
                last = {}
                for d in self.dma_hist[e]:
                    last[d.dma_sem] = d
                for d in last.values():
                    h.wait_ge(dsem[e][d.dma_sem], d.dma_val)

        return run_stream


class V:
    def __init__(self, b, off32, n, dt):
        self.b = b
        self.off = off32
        self.n = n
        self.dt = dt
        if dt == F32:
            self.base = b.t[:, off32:off32 + n]
        else:
            assert n % 2 == 0
            self.base = b.t[:, off32:off32 + n // 2].bitcast(BF16)

    def ap(self, lo, hi):
        assert 0 <= lo < hi <= self.n, (self.b.name, lo, hi, self.n)
        return self.base[:, lo:hi]

    def r(self, lo, hi):
        assert 0 <= lo < hi <= self.n, (self.b.name, lo, hi, self.n)
        if self.dt == F32:
            return (self.b, self.off + lo, self.off + hi)
        return (self.b, self.off + lo // 2, self.off + (hi + 1) // 2)

    def sub(self, lo, n):
        if self.dt == F32:
            return V(self.b, self.off + lo, n, F32)
        assert lo % 2 == 0
        return V(self.b, self.off + lo // 2, n, BF16)


def _fm(vec):
    return np.ascontiguousarray(vec.reshape(-1, 128).T)


def _unit(w, cols):
    k = w.shape[0] // 128
    u = w[:, cols].reshape(k, 128, len(cols)).transpose(1, 0, 2)
    return u.reshape(128, k * len(cols))


def _zin_cols():
    units = [np.arange(0, 256), np.arange(256, 512)]
    for j in range(4):
        units.append(np.concatenate([np.arange(1024 + 128 * j, 1024 + 128 * j + 128),
                                     np.arange(1536 + 128 * j, 1536 + 128 * j + 128),
                                     np.arange(512 + 128 * j, 512 + 128 * j + 128)]))
    return units


def unit_table():
    tab = []
    zc = _zin_cols()
    for i in (0, 1):
        tab.append(("z", i, 8 * len(zc[i])))
    tab.append(("pm", 0, 512))
    for i in (2, 3, 4, 5):
        tab.append(("z", i, 8 * len(zc[i])))
    for i, nc_ in enumerate((384, 384, 256)):
        tab.append(("o", i, 8 * nc_))
    for i in range(NPAIR):
        tab.append(("u", i, 8 * 256))
    for j in range(8):
        tab.append(("d", j, NPAIR * 128))
    offs = []
    o = 0
    for t in tab:
        offs.append(o)
        o += t[2]
    return tab, offs, o


def pack_weights(w_in, pool_mix, w_out, w_up, w_down):
    tab, offs, tot = unit_table()
    out = np.empty((L, 128, tot), np.float32)
    zc = _zin_cols()
    oc = [np.arange(0, 384), np.arange(384, 768), np.arange(768, 1024)]
    for l in range(L):
        for (kind, i, n), o in zip(tab, offs):
            if kind == "z":
                a = _unit(w_in[l], zc[i])
            elif kind == "pm":
                a = pool_mix[l].transpose(1, 0, 2).reshape(128, 512)
            elif kind == "o":
                a = _unit(w_out[l], oc[i])
            elif kind == "u":
                a = _unit(w_up[l], np.concatenate([np.arange(128 * i, 128 * i + 128),
                                                   np.arange(DFF + 128 * i, DFF + 128 * i + 128)]))
            else:
                a = _unit(w_down[l], np.arange(128 * i, 128 * i + 128))
            out[l, :, o:o + n] = a
    return out


def pack_vecs(pool_scale, conv_w, g_pool_out, g_conv_out, g_pre_mix, g_post_mix, g_pre_ffn, g_post_ffn,
              ffn_conv_w, g_final):
    v = np.zeros((128, NV), np.float32)
    for l in range(L):
        b = l * NV_L
        v[:, b + V_GPRE:b + V_GPRE + 8] = _fm(g_pre_mix[l])
        v[:, b + V_GPOST:b + V_GPOST + 8] = _fm(g_post_mix[l])
        v[:, b + V_GPREF:b + V_GPREF + 8] = _fm(g_pre_ffn[l])
        v[:, b + V_GPOSTF:b + V_GPOSTF + 8] = _fm(g_post_ffn[l])
        v[:, b + V_PSCALE:b + V_PSCALE + 4] = _fm(pool_scale[l])
        v[:, b + V_GPOOL:b + V_GPOOL + 4] = _fm(g_pool_out[l])
        v[:, b + V_GCONV:b + V_GCONV + 4] = _fm(g_conv_out[l])
        for k in range(3):
            v[:, b + V_CW + 4 * k:b + V_CW + 4 * k + 4] = _fm(conv_w[l, k])
            v[:, b + V_FW + 44 * k:b + V_FW + 44 * k + 44] = _fm(ffn_conv_w[l, k])
    v[:, L * NV_L:L * NV_L + 8] = _fm(g_final)
    return v


def build_nc():
    nc = bass.Bass("TRN2", target_bir_lowering=False)
    tab, offs, WTOT = unit_table()
    NU = len(tab)
    xp_d = nc.dram_tensor("xp", [KT, 128, SEQ], F32, kind="ExternalInput").ap()
    xs_d = nc.dram_tensor("xs", [KT, 128, DSEQ], F32, kind="ExternalInput").ap()
    stp_d = nc.dram_tensor("stp", [128, L * 4 * 15], F32, kind="ExternalInput").ap()
    stc_d = nc.dram_tensor("stc", [128, L * 4 * 2], F32, kind="ExternalInput").ap()
    stf_d = nc.dram_tensor("stf", [128, L * 44 * 2], F32, kind="ExternalInput").ap()
    vec_d = nc.dram_tensor("vecs", [128, NV], F32, kind="ExternalInput").ap()
    wts_d = nc.dram_tensor("wts", [L, 128, WTOT], F32, kind="ExternalInput").ap()
    yp_d = nc.dram_tensor("yp", [KT, 128, SEQ], F32, kind="ExternalOutput").ap()
    ys_d = nc.dram_tensor("ys", [KT, 128, DSEQ], F32, kind="ExternalOutput").ap()
    opool_d = nc.dram_tensor("opool", [128, L * 2 * 4 * 15], F32, kind="ExternalOutput").ap()
    oconv_d = nc.dram_tensor("oconv", [128, L * 2 * 4 * 2], F32, kind="ExternalOutput").ap()
    offn_d = nc.dram_tensor("offn", [128, L * 2 * 44 * 2], F32, kind="ExternalOutput").ap()

    WMAX = 1088
    WH = WMAX + 2 * H
    es = ExitStack()
    with es:
        def sbuf(name, n32):
            t = es.enter_context(nc.sbuf_tensor("sb_" + name, [128, n32], F32))
            return Buf(name, t, n32)

        b_x = sbuf("x", KT * WMAX)
        b_h = sbuf("h", KT * WMAX // 2)
        b_sq = sbuf("sq", KT * 512 // 2)
        b_rt = sbuf("rt", 3 * 512)
        b_w = sbuf("w", NSLOT * SLOT_E // 2)
        b_vec = sbuf("vec", NV)
        b_stp = sbuf("stp", L * 4 * 15)
        b_stc = sbuf("stc", L * 4 * 2)
        b_stf = sbuf("stf", L * 44 * 2)
        b_cv = sbuf("cv", L * 4 * 15)
        b_cc = sbuf("cc", L * 4 * 2)
        b_cf = sbuf("cf", L * 44 * 2)
        b_op = sbuf("op", L * 2 * 4 * 15)
        b_oc = sbuf("oc", L * 2 * 4 * 2)
        b_of = sbuf("of", L * 2 * 44 * 2)
        b_cst = sbuf("cst", 3 * 64 + 1 + 64 + 3)
        b_pm = sbuf("pm", 256)
        PH_N = 25200
        PH_N = 25008
        b_ph = sbuf("ph", PH_N)
        pst = es.enter_context(nc.psum_tensor("psum_all", [128, 8 * 512], F32))
        b_ps = Buf("ps", pst, 8 * 512)

        sems = {"tl": {e: es.enter_context(nc.semaphore("tl_" + e)) for e in COMPUTE},
                "dma": {e: [es.enter_context(nc.semaphore("d_%s_%d" % (e, i))) for i in range(8)]
                        for e in (ACT, POOL, SP)}}
        P = Prog()

        xv = [V(b_x, kt * WMAX, WMAX, F32) for kt in range(KT)]
        hv = [V(b_h, kt * WMAX // 2, WMAX, BF16) for kt in range(KT)]
        sqv = [V(b_sq, kt * 256, 512, BF16) for kt in range(KT)]
        rtv = [V(b_rt, i * 512, 512, F32) for i in range(3)]
        wslot = [V(b_w, s * SLOT_E // 2, SLOT_E, BF16) for s in range(NSLOT)]
        vec = V(b_vec, 0, NV, F32)
        pmbuf = V(b_pm, 0, 512, BF16)
        ones1024 = V(b_cst, 0, 128, BF16)
        ones128 = V(b_cst, 64, 128, BF16)
        onesblk = V(b_cst, 128, 128, BF16)
        epsv = V(b_cst, 192, 1, F32)
        invc = [V(b_cst, 193 + 16 * g, 16, F32) for g in range(4)]
        fixt = V(b_cst, 193 + 64, 3, F32)
        banks = [V(b_ps, k * 512, 512, F32) for k in range(8)]
        bankpair = [V(b_ps, k * 1024, 1024, F32) for k in range(4)]
        bp_ctr = [0, 0]
        o = 0
        vbuf = [V(b_ph, o + g * WH, WH, F32) for g in range(4)]; o += 4 * WH
        pooled = [V(b_ph, o + g * (WMAX // 2), WMAX, BF16) for g in range(4)]; o += 4 * WMAX // 2
        gcs = [V(b_ph, o + i * 512, 512, F32) for i in range(1)]; o += 512
        gbs = [V(b_ph, o + i * 512, 512, F32) for i in range(2)]; o += 1024
        mixraw = [[V(b_ph, par * 4480 + j * 512, 512, F32) for j in range(8)] for par in range(2)]
        cacc = [V(b_ph, o + i * 512, 512, F32) for i in range(1)]; o += 512
        assert o >= 8576
        cubuf = [V(b_ph, o + j * WH, WH, F32) for j in range(4)]
        sqO = [V(b_ph, o + j * 256, 512, BF16) for j in range(8)]
        o += 4 * WH
        Tk = [V(b_ph, o + k * WH, WH, F32) for k in range(3)]; o += 3 * WH
        Tfin = [V(b_ph, o + i * 512, 512, F32) for i in range(2)]; o += 1024
        yrot = [V(b_ph, o + i * 512, 512, F32) for i in range(4)]; o += 2048
        sqh = [V(b_ph, o + i * 256, 512, BF16) for i in range(4)]; o += 1024
        cat = [V(b_ph, o + kt * (WMAX // 2), WMAX, BF16) for kt in range(KT)]; o += KT * WMAX // 2
        assert o <= PH_N, o
        o = 0
        upraw = [V(b_ph, o + s * WH, WH, F32) for s in range(6)]; o += 6 * WH
        facc = [V(b_ph, o + i * 1024, 1024, F32) for i in range(4)]; o += 4096
        facc_s = [V(b_ph, o + i * 64, 64, F32) for i in range(4)]; o += 256
        fraw = [V(b_ph, j * WH + H, WMAX, F32) for j in range(8)]
        o = max(o, 8 * WMAX)
        actv = [V(b_ph, o + i * (WMAX // 2), WMAX, BF16) for i in range(NPAIR)]; o += NPAIR * WMAX // 2
        fsq = [V(b_ph, o + i * 256, 512, BF16) for i in range(3)]; o += 768
        fstage = [V(b_ph, 8960 + i * 512, 512, F32) for i in range(3)]
        assert 8960 + 3 * 512 <= 6 * WH + 4096 and 8960 >= 7 * WH + H + WMAX
        fstage += [V(b_ph, o + i * 512, 512, F32) for i in range(2)]; o += 1024
        assert o <= PH_N, o

        def vcol(l, off):
            c = l * NV_L + off
            return vec.ap(c, c + 1), vec.r(c, c + 1)

        def mm(out, lhsT, rhs, start, stop):
            (ov, olo, ohi), (lv, llo, lhi), (rv, rlo, rhi) = out, lhsT, rhs
            oa, la, ra = ov.ap(olo, ohi), lv.ap(llo, lhi), rv.ap(rlo, rhi)
            P.op(PE, lambda h: h.matmul(oa, la, ra, start=start, stop=stop),
                 reads=[lv.r(llo, lhi), rv.r(rlo, rhi)], writes=[ov.r(olo, ohi)])

        def act(out, in_, func, scale=None, bias=None):
            (ov, olo, ohi), (iv, ilo, ihi) = out, in_
            oa, ia = ov.ap(olo, ohi), iv.ap(ilo, ihi)
            reads = [iv.r(ilo, ihi)]
            kw = {}
            if scale is not None:
                if isinstance(scale, tuple):
                    kw["scale"] = scale[0]; reads.append(scale[1])
                else:
                    kw["scale"] = scale
            if bias is not None:
                kw["bias"] = bias[0]; reads.append(bias[1])
            P.op(ACT, lambda h: h.activation(out=oa, in_=ia, func=func, **kw), reads=reads, writes=[ov.r(olo, ohi)])

        def tt(eng, out, in0, in1, op):
            (ov, olo, ohi), (av, alo, ahi), (bv, blo, bhi) = out, in0, in1
            oa, aa, ba = ov.ap(olo, ohi), av.ap(alo, ahi), bv.ap(blo, bhi)
            P.op(eng, lambda h: h.tensor_tensor(oa, aa, ba, op),
                 reads=[av.r(alo, ahi), bv.r(blo, bhi)], writes=[ov.r(olo, ohi)])

        def stt(eng, out, in0, scalar, in1, op0, op1):
            (ov, olo, ohi), (av, alo, ahi), (bv, blo, bhi) = out, in0, in1
            oa, aa, ba = ov.ap(olo, ohi), av.ap(alo, ahi), bv.ap(blo, bhi)
            reads = [av.r(alo, ahi), bv.r(blo, bhi)]
            if isinstance(scalar, tuple):
                sc = scalar[0]; reads.append(scalar[1])
            else:
                sc = scalar
            P.op(eng, lambda h: h.scalar_tensor_tensor(oa, aa, sc, ba, op0, op1), reads=reads, writes=[ov.r(olo, ohi)])

        def recip(out, in_):
            (ov, olo, ohi), (iv, ilo, ihi) = out, in_
            oa, ia = ov.ap(olo, ohi), iv.ap(ilo, ihi)
            P.op(DVE, lambda h: h.reciprocal(oa, ia), reads=[iv.r(ilo, ihi)], writes=[ov.r(olo, ohi)])

        def copy(eng, out, in_):
            (ov, olo, ohi), (iv, ilo, ihi) = out, in_
            oa, ia = ov.ap(olo, ohi), iv.ap(ilo, ihi)
            P.op(eng, lambda h: h.tensor_copy(oa, ia), reads=[iv.r(ilo, ihi)], writes=[ov.r(olo, ohi)])

        def memset(eng, out, val):
            (ov, olo, ohi) = out
            oa = ov.ap(olo, ohi)
            P.op(eng, lambda h: h.memset(oa, val), writes=[ov.r(olo, ohi)])

        def dma(eng, out_ap, in_ap, reads=(), writes=()):
            P.op(eng, lambda h: h.dma_start(out=out_ap, in_=in_ap), reads=list(reads), writes=list(writes), dma=True)

        bank_ctr = [0, 0]

        def mm_bank(limit=6):
            k = bank_ctr[0] % limit
            bank_ctr[0] += 1
            return banks[k]

        def st_bank():
            k = 6 + bank_ctr[1] % 2
            bank_ctr[1] += 1
            return banks[k]

        rt_ctr = [0]

        def next_rt():
            r_ = rtv[rt_ctr[0] % 3]
            rt_ctr[0] += 1
            return r_

        memset(POOL, (ones1024, 0, 128), 1.0 / 1024)
        memset(POOL, (ones128, 0, 128), 1.0 / 128)
        memset(POOL, (onesblk, 0, 128), 0.0)
        oa = onesblk.base[0:64, 0:64]
        P.op(POOL, lambda h: h.memset(oa, 1.0 / 64), writes=[onesblk.r(0, 128)])
        ob = onesblk.base[64:128, 64:128]
        P.op(POOL, lambda h: h.memset(ob, 1.0 / 64), writes=[onesblk.r(0, 128)])
        memset(POOL, (epsv, 0, 1), EPS)
        for g in range(4):
            w = 2 << g
            memset(POOL, (invc[g], 0, 16), 1.0)
            for t in range(min(w - 1, 16)):
                memset(POOL, (invc[g], t, t + 1), float(w) / (t + 1))
        dma(SP, vec.ap(0, NV), vec_d, writes=[vec.r(0, NV)])
        dma(SP, b_stp.t[:, :], stp_d, writes=[(b_stp, 0, b_stp.width)])
        dma(SP, b_stc.t[:, :], stc_d, writes=[(b_stc, 0, b_stc.width)])
        dma(SP, b_stf.t[:, :], stf_d, writes=[(b_stf, 0, b_stf.width)])

        wseq = []
        for blk in range(2):
            for l in range(L):
                for ui in range(NU):
                    wseq.append((blk, l, ui))
        wstate = {"issued": 0, "used": 0, "live": set(), "next_ring": 0}
        ring_idx = []
        rc = 0
        for (_, l_, ui_) in wseq:
            if tab[ui_][0] == "pm":
                ring_idx.append(None)
            else:
                ring_idx.append(rc)
                rc += 1

        def issue_weights():
            while wstate["issued"] < len(wseq):
                q = wstate["issued"]
                (_, l, ui) = wseq[q]
                n = tab[ui][2]
                if ring_idx[q] is None:
                    if q > wstate["used"] + 4:
                        break
                    sv = pmbuf
                else:
                    r = ring_idx[q]
                    done_upto = min(wstate["live"]) if wstate["live"] else wstate["next_ring"]
                    if r - NSLOT >= done_upto:
                        break
                    sv = wslot[r % NSLOT]
                dma(POOL, sv.ap(0, n), wts_d[l, :, offs[ui]:offs[ui] + n], writes=[sv.r(0, n)])
                wstate["issued"] += 1

        def next_unit(kind):
            q = wstate["used"]
            wstate["used"] += 1
            (_, l, ui) = wseq[q]
            assert tab[ui][0] == kind, (tab[ui], kind)
            if ring_idx[q] is None:
                issue_weights()
                assert wstate["issued"] > q
                return pmbuf, None
            wstate["live"].add(ring_idx[q])
            wstate["next_ring"] = ring_idx[q] + 1
            issue_weights()
            assert wstate["issued"] > q
            return wslot[ring_idx[q] % NSLOT], ring_idx[q]

        def release(q):
            wstate["live"].discard(q)
            issue_weights()

        def stats_rstd(src, n, ones_v, nsrc, bank=None):
            pb = bank if bank is not None else st_bank()
            for i, (sv, lo) in enumerate(src):
                mm((pb, 0, n), (ones_v, 0, 128), (sv, lo, lo + n), i == 0, i == nsrc - 1)
            rt = next_rt()
            rstd_from(pb, rt, n)
            return rt

        def rstd_from(pb, rt, n):
            act((rt, 0, n), (pb, 0, n), AF.Ln, bias=(epsv.ap(0, 1), epsv.r(0, 1)))
            act((rt, 0, n), (rt, 0, n), AF.Exp, scale=-0.5)

        def pre_a(a, b):
            n = b - a
            for kt in range(KT):
                act((sqv[kt], 0, n), (xv[kt], a, b), AF.Square)

        def pre_b(l, goff, a, b, bank=None):
            n = b - a
            rt = stats_rstd([(sqv[kt], 0) for kt in range(KT)], n, ones1024, KT, bank=bank)
            for kt in range(KT):
                stt(DVE, (hv[kt], a, b), (xv[kt], a, b), vcol(l, goff + kt), (rt, 0, n), ALU.mult, ALU.mult)

        def pre_norm(l, goff, a, b):
            pre_a(a, b)
            pre_b(l, goff, a, b)

        def post_norm_residual(l, goff, raws, a, b, rt=None, all_dve=False):
            n = b - a
            assert rt is not None
            for j in range(8):
                rv_, lo = raws[j]
                tt(DVE, (rv_, lo, lo + n), (rv_, lo, lo + n), (rt, 0, n), ALU.mult)
                tt(POOL if (j % 2 == 0 and not all_dve) else DVE, (xv[j], a, b), (xv[j], a, b), (rv_, lo, lo + n), ALU.add)

        blocks = [
            [dict(kind="p", t0=0, n=1024, pl=0, hs=0, first=True, last=False)],
            [dict(kind="p", t0=1024, n=1024, pl=0, hs=0, first=False, last=True),
             dict(kind="s", t0=0, n=DSEQ, pl=1024, hs=H + 1024, first=False, last=True)],
        ]

        def ap3(v, ncol_each, stride, cnt, lo, hi):
            full = v.b.t[:, v.off:v.off + stride * cnt].rearrange("p (g w) -> p g w", g=cnt)
            return full[:, :, lo:hi]

        for bi, segs in enumerate(blocks):
            W = sum(s["n"] for s in segs)
            chunks = []
            for si, s in enumerate(segs):
                c0 = 0
                while c0 < s["n"]:
                    n = min(512, s["n"] - c0)
                    chunks.append(dict(a=s["pl"] + c0, b=s["pl"] + c0 + n, n=n, seg=si,
                                       ha=s["hs"] + H + c0, hb=s["hs"] + H + c0 + n,
                                       first=(c0 == 0), last=(c0 + n == s["n"])))
                    c0 += n
            for c in chunks:
                s = segs[c["seg"]]
                t0 = s["t0"] + (c["a"] - s["pl"])
                for kt in range(KT):
                    src = (xp_d[kt, :, t0:t0 + c["n"]] if s["kind"] == "p" else xs_d[kt, :, t0:t0 + c["n"]])
                    dma(SP, xv[kt].ap(c["a"], c["b"]), src, writes=[xv[kt].r(c["a"], c["b"])])

            final_done = set()

            def final_norm(ci, stage, bank=None):
                c = chunks[ci]
                a, b, n = c["a"], c["b"], c["n"]
                s = segs[c["seg"]]
                pre_a(a, b)
                rt = stats_rstd([(sqv[kt], 0) for kt in range(KT)], n, ones1024, KT, bank=bank)
                for kt in range(KT):
                    st_ = stage[kt % len(stage)]
                    gcol = (vec.ap(L * NV_L + kt, L * NV_L + kt + 1), vec.r(L * NV_L + kt, L * NV_L + kt + 1))
                    stt(DVE, (st_, 0, n), (xv[kt], a, b), gcol, (rt, 0, n), ALU.mult, ALU.mult)
                    t0 = s["t0"] + (a - s["pl"])
                    dst = yp_d[kt, :, t0:t0 + n] if s["kind"] == "p" else ys_d[kt, :, t0:t0 + n]
                    dma(SP, dst, st_.ap(0, n), reads=[st_.r(0, n)])
                final_done.add(ci)

            for l in range(L):
                def halo_init(which, l=l):
                    for si, s in enumerate(segs):
                        hs = s["hs"]
                        if s["kind"] == "p" and s["first"]:
                            for g in range(4):
                                if which == "v":
                                    memset(POOL, (vbuf[g], hs, hs + H), 0.0)
                                else:
                                    memset(POOL, (cubuf[g], hs, hs + H), 0.0)
                        else:
                            srcb, srcc = (b_cv, b_cc) if s["kind"] == "p" else (b_stp, b_stc)
                            for g in range(4):
                                if which == "v":
                                    o15 = (l * 4 + g) * 15
                                    sa = srcb.t[:, o15:o15 + 15]
                                    da = vbuf[g].ap(hs + 1, hs + H)
                                    P.op(POOL, lambda h, da=da, sa=sa: h.tensor_copy(da, sa),
                                         reads=[(srcb, o15, o15 + 15)], writes=[vbuf[g].r(hs + 1, hs + H)])
                                else:
                                    o2 = (l * 4 + g) * 2
                                    sa2 = srcc.t[:, o2:o2 + 2]
                                    da2 = cubuf[g].ap(hs + H - 2, hs + H)
                                    P.op(POOL, lambda h, da2=da2, sa2=sa2: h.tensor_copy(da2, sa2),
                                         reads=[(srcc, o2, o2 + 2)], writes=[cubuf[g].r(hs + H - 2, hs + H)])

                def v_tiles(lv, cis, vunits):
                    for g in range(4):
                        ws = vunits[g // 2][0]
                        jj = g % 2
                        for ci in cis:
                            c = chunks[ci]
                            a, b, n, ha, hb = c["a"], c["b"], c["n"], c["ha"], c["hb"]
                            pb = mm_bank()
                            for kt in range(KT):
                                mm((pb, 0, n), (ws, kt * 256 + jj * 128, kt * 256 + jj * 128 + 128), (hv[kt], a, b),
                                   kt == 0, kt == KT - 1)
                            act((vbuf[g], ha, hb), (pb, 0, n), AF.Copy)

                nch = len(chunks)
                if l == 0:
                    halo_init("v")
                    for c in chunks:
                        pre_norm(l, V_GPRE, c["a"], c["b"])
                    vunits = [next_unit("z"), next_unit("z")]
                    v_tiles(l, range(nch), vunits)
                    release(vunits[0][1])
                    release(vunits[1][1])
                halo_init("cu")
                pm_slot, pmq = next_unit("pm")
                def pool_chain(g, c, par):
                    a, b, n, ha, hb = c["a"], c["b"], c["n"], c["ha"], c["hb"]
                    s = segs[c["seg"]]
                    hs = s["hs"]
                    cur = vbuf[g]
                    tf = Tfin[par]
                    for k in range(g + 1):
                        sh = 1 << k
                        if k == g:
                            tt(POOL, (tf, 0, n), (cur, ha, hb), (cur, ha - sh, hb - sh), ALU.add)
                        else:
                            dst = Tk[k]
                            lo = (hs + 2 * sh) if c["first"] else ha
                            tt(POOL, (dst, lo, hb), (cur, lo, hb), (cur, lo - sh, hb - sh), ALU.add)
                            cur = dst
                    if s["kind"] == "p" and s["first"] and c["first"]:
                        tt(POOL, (tf, 0, 16), (tf, 0, 16), (invc[g], 0, 16), ALU.mult)
                    if c is chunks[-1]:
                        for si, s in enumerate(segs):
                            he = s["hs"] + H + s["n"]
                            if s["last"]:
                                slot = 0 if s["kind"] == "p" else 1
                                o15 = ((l * 2 + slot) * 4 + g) * 15
                                dstb = b_op
                            else:
                                o15 = (l * 4 + g) * 15
                                dstb = b_cv
                            da = dstb.t[:, o15:o15 + 15]
                            sa = vbuf[g].ap(he - 15, he)
                            P.op(POOL, lambda h, da=da, sa=sa: h.tensor_copy(da, sa),
                                 reads=[vbuf[g].r(he - 15, he)], writes=[(dstb, o15, o15 + 15)])

                steps = [(j, ci) for j in range(4) for ci in range(len(chunks))]
                pool_chain(steps[0][0], chunks[steps[0][1]], 0)
                deferred = []

                def flush_deferred():
                    for fn in deferred:
                        fn()
                    del deferred[:]

                step = 0
                q_norm = []
                q_ppm = []

                def run_ppm():
                    for fn in q_ppm:
                        fn()
                    del q_ppm[:]

                for j in range(4):
                    ws, wq = next_unit("z")
                    for ci, c in enumerate(chunks):
                        a, b, n, ha, hb = c["a"], c["b"], c["n"], c["ha"], c["hb"]
                        pgc, pu, pgb = mm_bank(), mm_bank(), mm_bank()
                        for ti, pb in enumerate((pgc, pu, pgb)):
                            for kt in range(KT):
                                mm((pb, 0, n), (ws, kt * 384 + ti * 128, kt * 384 + ti * 128 + 128), (hv[kt], a, b),
                                   kt == 0, kt == KT - 1)
                        if step + 1 < len(steps):
                            pool_chain(steps[step + 1][0], chunks[steps[step + 1][1]], (step + 1) % 2)
                        stt(DVE, (pooled[j], a, b), (Tfin[step % 2], 0, n), 1.0 / (2 << j), (vbuf[j], ha, hb),
                            ALU.mult, ALU.subtract)
                        gc_ = gcs[0]
                        gb_ = gbs[step % 2]
                        ca = cacc[0]
                        yc = yrot[(2 * step) % 4]
                        sqc = sqh[(2 * step) % 4]
                        act((gc_, 0, n), (pgc, 0, n), AF.Copy)
                        act((gb_, 0, n), (pgb, 0, n), AF.Copy)
                        tt(DVE, (cubuf[j], ha, hb), (pu, 0, n), (gc_, 0, n), ALU.mult)
                        act((ca, 0, n), (cubuf[j], ha, hb), AF.Copy, scale=vcol(l, V_CW + 4 * 2 + j))
                        prev = list(q_norm)
                        del q_norm[:]
                        run_ppm()
                        for (fn_b, fn_c) in prev:
                            fn_b()
                        stt(DVE, (ca, 0, n), (cubuf[j], ha - 1, hb - 1), vcol(l, V_CW + 4 * 1 + j), (ca, 0, n),
                            ALU.mult, ALU.add)
                        stt(DVE, (ca, 0, n), (cubuf[j], ha - 2, hb - 2), vcol(l, V_CW + 4 * 0 + j), (ca, 0, n),
                            ALU.mult, ALU.add)
                        tt(DVE, (yc, 0, n), (gb_, 0, n), (ca, 0, n), ALU.mult)
                        for (fn_b, fn_c) in prev:
                            fn_c()
                        act((sqc, 0, n), (yc, 0, n), AF.Square)

                        boxc = {}

                        def stage_b_c(n=n, sqc=sqc, box=boxc):
                            box["rt"] = stats_rstd([(sqc, 0)], n, onesblk, 1)

                        def stage_c_c(j=j, a=a, b=b, n=n, yc=yc, box=boxc):
                            stt(DVE, (cat[4 + j], a, b), (yc, 0, n), vcol(l, V_GCONV + j), (box["rt"], 0, n),
                                ALU.mult, ALU.mult)

                        q_norm.append((stage_b_c, stage_c_c))

                        def ppm_stage(j=j, a=a, b=b, n=n, step=step):
                            ppm = mm_bank()
                            mm((ppm, 0, n), (pm_slot, j * 128, j * 128 + 128), (pooled[j], a, b), True, True)
                            yp_ = yrot[(2 * step + 3) % 4]
                            sqp = sqh[(2 * step + 3) % 4]
                            act((yp_, 0, n), (ppm, 0, n), AF.Copy, scale=vcol(l, V_PSCALE + j))
                            act((sqp, 0, n), (ppm, 0, n), AF.Square, scale=vcol(l, V_PSCALE + j))
                            boxp = {}

                            def stage_b_p(n=n, sqp=sqp, box=boxp):
                                box["rt"] = stats_rstd([(sqp, 0)], n, ones128, 1)

                            def stage_c_p(j=j, a=a, b=b, n=n, yp_=yp_, box=boxp):
                                stt(DVE, (cat[j], a, b), (yp_, 0, n), vcol(l, V_GPOOL + j), (box["rt"], 0, n),
                                    ALU.mult, ALU.mult)

                            q_norm.append((stage_b_p, stage_c_p))

                        q_ppm.append(ppm_stage)
                        step += 1
                    release(wq)
                    for si, s in enumerate(segs):
                        he = s["hs"] + H + s["n"]
                        if s["last"]:
                            slot = 0 if s["kind"] == "p" else 1
                            o2 = ((l * 2 + slot) * 4 + j) * 2
                            dstb = b_oc
                        else:
                            o2 = (l * 4 + j) * 2
                            dstb = b_cc
                        da = dstb.t[:, o2:o2 + 2]
                        sa = cubuf[j].ap(he - 2, he)
                        P.op(POOL, lambda h, da=da, sa=sa: h.tensor_copy(da, sa),
                             reads=[cubuf[j].r(he - 2, he)], writes=[(dstb, o2, o2 + 2)])
                prev = list(q_norm)
                del q_norm[:]
                run_ppm()
                for (fn_b, fn_c) in prev:
                    fn_b()
                    fn_c()
                last_def = list(q_norm)
                del q_norm[:]
                release(pmq)
                ou = [next_unit("o") for _ in range(3)]
                oslots = [u_[0] for u_ in ou]
                ocols = (384, 384, 256)

                def o_mm(ci, js=range(8)):
                    c = chunks[ci]
                    a, b, n = c["a"], c["b"], c["n"]
                    mr = mixraw[min(ci, 1)]
                    for j in js:
                        ui, jj = (j // 3, j % 3) if j < 6 else (2, j - 6)
                        ws, ncol = oslots[ui], ocols[ui]
                        pb = mm_bank()
                        for kt in range(KT):
                            mm((pb, 0, n), (ws, kt * ncol + jj * 128, kt * ncol + jj * 128 + 128), (cat[kt], a, b),
                               kt == 0, kt == KT - 1)
                        act((mr[j], 0, n), (pb, 0, n), AF.Copy, scale=vcol(l, V_GPOST + j))
                        act((sqO[j], 0, n), (pb, 0, n), AF.Square)

                def o_post(ci):
                    c = chunks[ci]
                    a, b, n = c["a"], c["b"], c["n"]
                    mr = mixraw[min(ci, 1)]
                    rt = stats_rstd([(sqO[j], 0) for j in range(8)], n, ones1024, 8)
                    post_norm_residual(l, V_GPOST, [(mr[j], 0) for j in range(8)], a, b, rt=rt)

                nch = len(chunks)
                if nch > 2:
                    for (fn_b, fn_c) in last_def:
                        fn_b()
                        fn_c()
                    o_mm(0)
                else:
                    o_mm(0, range(0, 4))
                    for (fn_b, fn_c) in last_def:
                        fn_b()
                        fn_c()
                    o_mm(0, range(4, 8))
                o_post(0)
                for k in range(1, nch):
                    o_mm(k, range(0, 5))
                    pre_a(chunks[k - 1]["a"], chunks[k - 1]["b"])
                    o_mm(k, range(5, 8))
                    if k == nch - 1:
                        for u_ in ou:
                            release(u_[1])
                    pre_b(l, V_GPREF, chunks[k - 1]["a"], chunks[k - 1]["b"])
                    o_post(k)
                NLEAD = 3

                def u_halo(i):
                    for t_ in range(2):
                        jt = i + NPAIR * t_
                        us = upraw[(2 * i + t_) % 6]
                        for si, s in enumerate(segs):
                            hs = s["hs"]
                            if s["kind"] == "p" and s["first"]:
                                memset(POOL, (us, hs + H - 2, hs + H), 0.0)
                            else:
                                srcb = b_cf if s["kind"] == "p" else b_stf
                                o2 = (l * 44 + jt) * 2
                                sa = srcb.t[:, o2:o2 + 2]
                                da = us.ap(hs + H - 2, hs + H)
                                P.op(POOL, lambda h, da=da, sa=sa: h.tensor_copy(da, sa),
                                     reads=[(srcb, o2, o2 + 2)], writes=[us.r(hs + H - 2, hs + H)])

                def u_mm(i, ws, ci, small_bank=None):
                    c = chunks[ci]
                    a, b, n = c["a"], c["b"], c["n"]
                    pbs = []
                    for t_ in range(2):
                        if small_bank is not None:
                            pb, c0_ = banks[6 + t_], 0
                        else:
                            pb, c0_ = mm_bank(), 0
                        for kt in range(KT):
                            mm((pb, c0_, c0_ + n), (ws, kt * 256 + t_ * 128, kt * 256 + t_ * 128 + 128), (hv[kt], a, b),
                               kt == 0, kt == KT - 1)
                        pbs.append((pb, c0_))
                    return pbs

                def u_ew(i, ci, pbs, defer_b=None):
                    c = chunks[ci]
                    a, b, n, ha, hb = c["a"], c["b"], c["n"], c["ha"], c["hb"]
                    accs = []
                    for t_ in range(2):
                        jt = i + NPAIR * t_
                        us = upraw[(2 * i + t_) % 6]
                        pb, c0_ = pbs[t_]
                        if n <= 64:
                            fa = facc_s[fctr[1] % 4]
                            fctr[1] += 1
                        else:
                            fa = facc[fctr[0] % 4]
                            fctr[0] += 1
                        act((us, ha, hb), (pb, c0_, c0_ + n), AF.Copy)
                        act((fa, 0, n), (pb, c0_, c0_ + n), AF.Copy, scale=vcol(l, V_FW + 44 * 2 + jt))
                        stt(DVE, (fa, 0, n), (us, ha - 1, hb - 1), vcol(l, V_FW + 44 * 1 + jt), (fa, 0, n),
                            ALU.mult, ALU.add)
                        stt(DVE, (fa, 0, n), (us, ha - 2, hb - 2), vcol(l, V_FW + 44 * 0 + jt), (fa, 0, n),
                            ALU.mult, ALU.add)
                        accs.append(fa)

                    def part_b(i=i, a=a, b=b, n=n, accs=accs):
                        act((accs[0], 0, n), (accs[0], 0, n), AF.Silu)
                        tt(DVE, (actv[i], a, b), (accs[0], 0, n), (accs[1], 0, n), ALU.mult)

                    if defer_b is None:
                        part_b()
                    else:
                        defer_b.append(part_b)

                def u_step(i, ws, ci):
                    u_ew(i, ci, u_mm(i, ws, ci))

                def u_step2(i, ws):
                    c0, c1 = chunks[0], chunks[1]
                    a, b = c0["a"], c1["b"]
                    ha, hb = c0["ha"], c1["hb"]
                    assert c0["b"] == c1["a"] and c0["hb"] == c1["ha"] and b - a == 1024
                    accs = []
                    for t_ in range(2):
                        jt = i + NPAIR * t_
                        us = upraw[(2 * i + t_) % 6]
                        k2 = bp_ctr[0] % 3
                        bp_ctr[0] += 1
                        for ci, c in enumerate((c0, c1)):
                            pb = banks[2 * k2 + ci]
                            for kt in range(KT):
                                mm((pb, 0, 512), (ws, kt * 256 + t_ * 128, kt * 256 + t_ * 128 + 128),
                                   (hv[kt], c["a"], c["b"]), kt == 0, kt == KT - 1)
                        pp = bankpair[k2]
                        fa = facc[fctr[0] % 4]
                        fctr[0] += 1
                        act((us, ha, hb), (pp, 0, 1024), AF.Copy)
                        act((fa, 0, 1024), (pp, 0, 1024), AF.Copy, scale=vcol(l, V_FW + 44 * 2 + jt))
                        stt(DVE, (fa, 0, 1024), (us, ha - 1, hb - 1), vcol(l, V_FW + 44 * 1 + jt), (fa, 0, 1024),
                            ALU.mult, ALU.add)
                        stt(DVE, (fa, 0, 1024), (us, ha - 2, hb - 2), vcol(l, V_FW + 44 * 0 + jt), (fa, 0, 1024),
                            ALU.mult, ALU.add)
                        accs.append(fa)

                    def part_b(i=i, a=a, b=b, accs=accs):
                        act((accs[0], 0, 1024), (accs[0], 0, 1024), AF.Silu)
                        tt(DVE, (actv[i], a, b), (accs[0], 0, 1024), (accs[1], 0, 1024), ALU.mult)

                    return part_b

                def u_tails(i):
                    for t_ in range(2):
                        jt = i + NPAIR * t_
                        us = upraw[(2 * i + t_) % 6]
                        for si, s in enumerate(segs):
                            he = s["hs"] + H + s["n"]
                            if s["last"]:
                                slot = 0 if s["kind"] == "p" else 1
                                o2 = ((l * 2 + slot) * 44 + jt) * 2
                                dstb = b_of
                            else:
                                o2 = (l * 44 + jt) * 2
                                dstb = b_cf
                            da = dstb.t[:, o2:o2 + 2]
                            sa = us.ap(he - 2, he)
                            P.op(POOL, lambda h, da=da, sa=sa: h.tensor_copy(da, sa),
                                 reads=[us.r(he - 2, he)], writes=[(dstb, o2, o2 + 2)])

                fctr = [0, 0]
                lead = []
                pend = []
                for i in range(NLEAD):
                    ws, wq = next_unit("u")
                    lead.append((ws, wq))
                    u_halo(i)
                    for ci in range(nch - 1):
                        nleft = (NLEAD - 1 - i) * (nch - 1) + (nch - 2 - ci)
                        if nleft >= 2:
                            u_step(i, ws, ci)
                        else:
                            pend.append((i, ci, u_mm(i, ws, ci)))
                    if i == 1:
                        pre_a(chunks[nch - 1]["a"], chunks[nch - 1]["b"])
                pre_b(l, V_GPREF, chunks[nch - 1]["a"], chunks[nch - 1]["b"])
                for (i, ci, pbs) in pend:
                    u_ew(i, ci, pbs)
                for i in range(NLEAD):
                    ws, wq = lead[i]
                    u_step(i, ws, nch - 1)
                    release(wq)
                    u_tails(i)
                u_halo(NLEAD)
                small_b = []
                for i in range(NLEAD, NPAIR):
                    ws, wq = next_unit("u")
                    if i + 1 < NPAIR:
                        u_halo(i + 1)
                    pb2 = u_step2(i, ws)
                    prev_small = list(small_b)
                    del small_b[:]
                    for ci in range(2, nch):
                        sb_ = banks[6 + (bp_ctr[1] % 2)]
                        bp_ctr[1] += 1
                        u_ew(i, ci, u_mm(i, ws, ci, small_bank=sb_), defer_b=small_b)
                    pb2()
                    for fn in prev_small:
                        fn()
                    release(wq)
                    u_tails(i)
                for fn in small_b:
                    fn()
                del small_b[:]
                def d_step(j, ws, ci):
                    c = chunks[ci]
                    a, b, n = c["a"], c["b"], c["n"]
                    pb = mm_bank(5)
                    for kt in range(NPAIR):
                        mm((pb, 0, n), (ws, kt * 128, kt * 128 + 128), (actv[kt], a, b), kt == 0, kt == NPAIR - 1)
                    act((fraw[j], a, b), (pb, 0, n), AF.Copy, scale=vcol(l, V_GPOSTF + j))
                    fq = fsq[fctr[2] % 3]
                    fctr[2] += 1
                    act((fq, 0, n), (pb, 0, n), AF.Square)
                    while len(deferred) > (nch - 2):
                        deferred.pop(0)()
                    deferred.append(lambda ci=ci, n=n, fq=fq, j=j: mm((banks[5 + ci], 0, n), (ones1024, 0, 128),
                                                                     (fq, 0, n), j == 0, j == 7))

                def post_fin(ci):
                    c = chunks[ci]
                    a, b, n = c["a"], c["b"], c["n"]
                    rt = next_rt()
                    rstd_from(banks[5 + ci], rt, n)
                    post_norm_residual(l, V_GPOSTF, [(fraw[j], a) for j in range(8)], a, b, rt=rt, all_dve=True)

                fctr.append(0)
                KL = 3
                for j in range(8 - KL):
                    ws, wq = next_unit("d")
                    for ci in range(nch):
                        d_step(j, ws, ci)
                    release(wq)
                if KL:
                    leadd = []
                    for j in range(8 - KL, 8):
                        ws, wq = next_unit("d")
                        leadd.append((j, ws, wq))
                        d_step(j, ws, 0)
                    bsteps = [(j, ws, ci) for (j, ws, wq) in leadd for ci in range(1, nch)]
                    d_step(*bsteps[0])
                    flush_deferred()
                    post_fin(0)
                    for st in bsteps[1:]:
                        d_step(*st)
                    for (j, ws, wq) in leadd:
                        release(wq)
                    flush_deferred()
                    if l == L - 1:
                        final_norm(0, fstage, bank=banks[5])
                        for ci in range(1, nch):
                            post_fin(ci)
                        continue
                    pre_a(chunks[0]["a"], chunks[0]["b"])
                    pre_b(l + 1, V_GPRE, chunks[0]["a"], chunks[0]["b"], bank=banks[5])
                    for ci in range(1, nch):
                        post_fin(ci)
                    halo_init("v", l + 1)
                    vunits = [next_unit("z"), next_unit("z")]
                    v_tiles(l + 1, [0], vunits)
                    for ci in range(1, nch):
                        pre_a(chunks[ci]["a"], chunks[ci]["b"])
                        pre_b(l + 1, V_GPRE, chunks[ci]["a"], chunks[ci]["b"])
                    v_tiles(l + 1, range(1, nch), vunits)
                    release(vunits[0][1])
                    release(vunits[1][1])
                else:
                    flush_deferred()
                    for ci in range(nch):
                        post_fin(ci)
            for ci in range(len(chunks)):
                if ci not in final_done:
                    final_norm(ci, mixraw[ci % 2])
        dma(SP, opool_d, b_op.t[:, :], reads=[(b_op, 0, b_op.width)])
        dma(SP, oconv_d, b_oc.t[:, :], reads=[(b_oc, 0, b_oc.width)])
        dma(SP, offn_d, b_of.t[:, :], reads=[(b_of, 0, b_of.width)])
        assert wstate["used"] == len(wseq), (wstate, len(wseq))

        run = P.emit(sems)
        with nc.Block() as block:
            @block.sync
            def _(h):
                run(SP, h)

            @block.scalar
            def _(h):
                run(ACT, h)

            @block.vector
            def _(h):
                run(DVE, h)

            @block.gpsimd
            def _(h):
                run(POOL, h)

            @block.tensor
            def _(h):
                run(PE, h)
    return nc


def kernel(x_prompt, x_sample, state_pool, state_conv, state_ffn_conv, w_in, pool_mix, pool_scale, conv_w,
           g_pool_out, g_conv_out, w_out, g_pre_mix, g_post_mix, g_pre_ffn, g_post_ffn, w_up, ffn_conv_w,
           w_down, g_final):
    f = lambda a: np.asarray(a, dtype=np.float32)
    x_prompt, x_sample, state_pool, state_conv, state_ffn_conv = map(f, (x_prompt, x_sample, state_pool,
                                                                         state_conv, state_ffn_conv))
    wts = pack_weights(f(w_in), f(pool_mix), f(w_out), f(w_up), f(w_down))
    vecs = pack_vecs(f(pool_scale), f(conv_w), f(g_pool_out), f(g_conv_out), f(g_pre_mix), f(g_post_mix),
                     f(g_pre_ffn), f(g_post_ffn), f(ffn_conv_w), f(g_final))
    in_maps = []
    for b in range(NCORES):
        xp = np.ascontiguousarray(x_prompt[b].T).reshape(KT, 128, SEQ)
        xs = np.ascontiguousarray(x_sample[b].T).reshape(KT, 128, DSEQ)
        stp = np.ascontiguousarray(state_pool[:, b].reshape(L, 15, 4, 128).transpose(3, 0, 2, 1)).reshape(128, -1)
        stc = np.ascontiguousarray(state_conv[:, b].reshape(L, 2, 4, 128).transpose(3, 0, 2, 1)).reshape(128, -1)
        stf = np.ascontiguousarray(state_ffn_conv[:, b].reshape(L, 2, 44, 128).transpose(3, 0, 2, 1)).reshape(128, -1)
        in_maps.append({"xp": xp, "xs": xs, "stp": stp, "stc": stc, "stf": stf, "vecs": vecs, "wts": wts})
    nc = build_nc()
    res = run_bass_kernel_spmd(nc, in_maps, core_ids=list(range(NCORES)))
    B = NCORES
    y_p = np.empty((B, SEQ, D), np.float32)
    y_s = np.empty((B, DSEQ, D), np.float32)
    npool = [np.empty((L, B, 15, 512), np.float32) for _ in range(2)]
    nconv = [np.empty((L, B, 2, 512), np.float32) for _ in range(2)]
    nffn = [np.empty((L, B, 2, 2 * DFF), np.float32) for _ in range(2)]
    for b in range(B):
        r = res.results[b]
        y_p[b] = np.asarray(r["yp"]).reshape(D, SEQ).T
        y_s[b] = np.asarray(r["ys"]).reshape(D, DSEQ).T
        op = np.asarray(r["opool"]).reshape(128, L, 2, 4, 15)
        oc = np.asarray(r["oconv"]).reshape(128, L, 2, 4, 2)
        of = np.asarray(r["offn"]).reshape(128, L, 2, 44, 2)
        for s in range(2):
            npool[s][:, b] = op[:, :, s].transpose(1, 3, 2, 0).reshape(L, 15, 512)
            nconv[s][:, b] = oc[:, :, s].transpose(1, 3, 2, 0).reshape(L, 2, 512)
            nffn[s][:, b] = of[:, :, s].transpose(1, 3, 2, 0).reshape(L, 2, 2 * DFF)
    return (y_p, y_s, npool[0], nconv[0], nffn[0], npool[1], nconv[1], nffn[1])
```
